# Optimizing a Trainium2 kernel written in Bass

```python
import math
import jax
import jax.numpy as jnp
from jax import lax
import numpy as np

D_MODEL = 1024
BATCH = 8
SEQ = 2048
DEPTH = 2

CTX_LEN = 256
GRID_W = 64
MIX_WIDTH = D_MODEL
LRU_WIDTH = MIX_WIDTH // 2
LRU_BLOCK = 64
LRU_HEADS = LRU_WIDTH // LRU_BLOCK
LRU_C = 8.0
SSD_INNER = MIX_WIDTH - LRU_WIDTH
SSD_HEAD_DIM = 64
SSD_HEADS = SSD_INNER // SSD_HEAD_DIM
SSD_GROUPS = 2
SSD_HPG = SSD_HEADS // SSD_GROUPS
SSD_STATE = 128
SSD_CHUNK = 128
SSD_XBC = SSD_INNER + 2 * SSD_GROUPS * SSD_STATE
SSD_DT = 2 * SSD_HEADS
CONV_K = 4
CONV_LEFT = 1
D_FF = 4 * D_MODEL
N_SCAN_COLS = LRU_WIDTH + SSD_XBC + SSD_DT
N_IN_COLS = N_SCAN_COLS + LRU_WIDTH + SSD_INNER
N_MOD = 6
EPS = 1e-6

kernel_name = 'hybrid_rglru_ssd_prefix_dit_block'


def _rmsnorm(x, g):
    xf = x.astype(jnp.float32)
    y = xf * lax.rsqrt(jnp.mean(xf * xf, axis=-1, keepdims=True) + EPS)
    return (y * g.astype(jnp.float32)).astype(x.dtype)


def _modulate(h, shift, scale):
    return h * (1.0 + scale) + shift


def _sq_relu_mlp(h, w1, w2):
    return jnp.square(jax.nn.relu(h @ w1)) @ w2


def _dwconv(u, w, b):
    y = lax.conv_general_dilated(
        u, w[:, None, :].astype(u.dtype), window_strides=(1,),
        padding=[(CONV_LEFT, CONV_K - 1 - CONV_LEFT)],
        dimension_numbers=('NWC', 'WIO', 'NWC'), feature_group_count=u.shape[-1])
    return y + b.astype(u.dtype)


def _to_col_major(t, rows):
    b, l, ch = t.shape
    return t.reshape(b, rows, GRID_W, ch).swapaxes(1, 2).reshape(b, l, ch)


def _from_col_major(t, rows):
    b, l, ch = t.shape
    return t.reshape(b, GRID_W, rows, ch).swapaxes(1, 2).reshape(b, l, ch)


def _flip(t, d):
    return jnp.flip(t, axis=1) if d else t


def _compose(left, right):
    return left[0] * right[0], right[0] * left[1] + right[1]


def _linear_scan(a, b, h0):
    a_cum, h = lax.associative_scan(_compose, (a, b), axis=1)
    if h0 is None:
        return h
    return h + a_cum * h0[:, None]


def _lru_coeffs(u, wa, ba, wx, bx, lam):
    bsz, ln, _ = u.shape
    uf = u.astype(jnp.float32)
    ub = uf.reshape(bsz, ln, LRU_HEADS, LRU_BLOCK)
    r = jax.nn.sigmoid(jnp.einsum('blhi,hij->blhj', ub, wa.astype(jnp.float32)).reshape(bsz, ln, LRU_WIDTH) + ba.astype(jnp.float32))
    i = jax.nn.sigmoid(jnp.einsum('blhi,hij->blhj', ub, wx.astype(jnp.float32)).reshape(bsz, ln, LRU_WIDTH) + bx.astype(jnp.float32))
    log_a = -LRU_C * r * jax.nn.softplus(-lam.astype(jnp.float32))
    return jnp.exp(log_a), jnp.sqrt(-jnp.expm1(2.0 * log_a)) * (i * uf)


def _rglru_bidir(u_ctx, u_lat, wa, ba, wx, bx, lam, need_ctx_out):
    ys_ctx, ys_lat = [], []
    for d in range(2):
        a, b = _lru_coeffs(_flip(u_ctx, d), wa[d], ba[d], wx[d], bx[d], lam[d])
        h_ctx = _linear_scan(a, b, None)
        a, b = _lru_coeffs(_flip(u_lat, d), wa[d], ba[d], wx[d], bx[d], lam[d])
        h_lat = _linear_scan(a, b, h_ctx[:, -1])
        ys_lat.append(_flip(h_lat, d))
        if need_ctx_out:
            ys_ctx.append(_flip(h_ctx, d))
    y_ctx = ys_ctx[0] + ys_ctx[1] if need_ctx_out else None
    return y_ctx, ys_lat[0] + ys_lat[1]


def _ssd_chunked(x, log_a, bm, cm, h0, want_y, want_final):
    bsz, ln = x.shape[0], x.shape[1]
    nc = ln // SSD_CHUNK
    X = x.reshape(bsz, nc, SSD_CHUNK, SSD_GROUPS, SSD_HPG, SSD_HEAD_DIM)
    A = log_a.reshape(bsz, nc, SSD_CHUNK, SSD_GROUPS, SSD_HPG)
    Bc = bm.reshape(bsz, nc, SSD_CHUNK, SSD_GROUPS, SSD_STATE)
    Cc = cm.reshape(bsz, nc, SSD_CHUNK, SSD_GROUPS, SSD_STATE)
    a_cs = jnp.cumsum(A, axis=2)
    a_last = a_cs[:, :, -1]
    states = jnp.einsum('bclgn,bclge,bclgep->bcgepn', Bc, jnp.exp(a_last[:, :, None] - a_cs), X)
    chunk_cum = jnp.cumsum(jnp.pad(a_last, ((0, 0), (1, 0), (0, 0), (0, 0))), axis=1)
    row_idx = np.arange(0 if want_y else nc, nc + 1 if want_final else nc)
    if h0 is None:
        col_idx = np.arange(1, nc + 1)
        states_all = states
    else:
        col_idx = np.arange(0, nc + 1)
        h0g = h0.reshape(bsz, SSD_GROUPS, SSD_HPG, SSD_HEAD_DIM, SSD_STATE)
        states_all = jnp.concatenate([h0g[:, None], states], axis=1)
    seg = chunk_cum[:, row_idx][:, :, None] - chunk_cum[:, col_idx][:, None, :]
    cmask = (row_idx[:, None] >= col_idx[None, :])[None, :, :, None, None]
    decay_chunk = jnp.exp(jnp.where(cmask, seg, -jnp.inf))
    new_states = jnp.einsum('bzkge,bkgepn->bzgepn', decay_chunk, states_all)
    final = new_states[:, -1].reshape(bsz, SSD_HEADS, SSD_HEAD_DIM, SSD_STATE) if want_final else None
    y = None
    if want_y:
        prev = new_states[:, :nc]
        seg_in = a_cs[:, :, :, None] - a_cs[:, :, None, :]
        tri = np.tril(np.ones((SSD_CHUNK, SSD_CHUNK), dtype=bool))[None, None, :, :, None, None]
        l_mat = jnp.exp(jnp.where(tri, seg_in, -jnp.inf))
        cb = jnp.einsum('bclgn,bcsgn->bclsg', Cc, Bc)
        y_diag = jnp.einsum('bclsg,bclsge,bcsgep->bclgep', cb, l_mat, X)
        y_off = jnp.einsum('bclgn,bcgepn,bclge->bclgep', Cc, prev, jnp.exp(a_cs))
        y = (y_diag + y_off).reshape(bsz, ln, SSD_HEADS, SSD_HEAD_DIM)
    return y, final


def _ssd_unpack(xbc, dtr):
    bsz, ln, _ = xbc.shape
    xs, bm, cm = jnp.split(xbc.astype(jnp.float32), [SSD_INNER, SSD_INNER + SSD_GROUPS * SSD_STATE], axis=-1)
    return (xs.reshape(bsz, ln, SSD_HEADS, SSD_HEAD_DIM),
            bm.reshape(bsz, ln, SSD_GROUPS, SSD_STATE),
            cm.reshape(bsz, ln, SSD_GROUPS, SSD_STATE),
            dtr.astype(jnp.float32).reshape(bsz, ln, 2, SSD_HEADS))


def _ssd_bidir(xbc_c, dt_c, xbc_l, dt_l, dt_bias, a_log, d_skip, need_ctx_out):
    xs_c, b_c, cm_c, dtr_c = _ssd_unpack(xbc_c, dt_c)
    xs_l, b_l, cm_l, dtr_l = _ssd_unpack(xbc_l, dt_l)

    def drive(xs, dtr, d):
        dt = jax.nn.softplus(dtr[:, :, d] + dt_bias[d].astype(jnp.float32))
        return xs * dt[..., None], dt * (-jnp.exp(a_log[d].astype(jnp.float32)))

    ys_ctx, ys_lat = [], []
    for d in range(2):
        xc, ac = drive(xs_c, dtr_c, d)
        yc, hc = _ssd_chunked(_flip(xc, d), _flip(ac, d), _flip(b_c, d), _flip(cm_c, d), None, need_ctx_out, True)
        xl, al = drive(xs_l, dtr_l, d)
        yl, _ = _ssd_chunked(_flip(xl, d), _flip(al, d), _flip(b_l, d), _flip(cm_l, d), hc, True, False)
        ys_lat.append(_flip(yl, d))
        if need_ctx_out:
            ys_ctx.append(_flip(yc, d))
    skip = d_skip.astype(jnp.float32)[:, None]
    y_lat = ys_lat[0] + ys_lat[1] + skip * xs_l
    y_ctx = ys_ctx[0] + ys_ctx[1] + skip * xs_c if need_ctx_out else None
    return y_ctx, y_lat


def _gated_group_rmsnorm(y, z, g):
    yf = y * jax.nn.silu(z.astype(jnp.float32))
    b, l, w = yf.shape
    yg = yf.reshape(b, l, SSD_GROUPS, w // SSD_GROUPS)
    yg = yg * lax.rsqrt(jnp.mean(yg * yg, axis=-1, keepdims=True) + EPS)
    return yg.reshape(b, l, w) * g.astype(jnp.float32)


def _mixer(h_lat, h_ctx, w_in, lru_conv_w, lru_conv_b, lru_wa, lru_ba, lru_wx, lru_bx, lru_lambda,
           ssd_conv_w, ssd_conv_b, ssd_dt_bias, ssd_a_log, ssd_d, ssd_norm_g, w_out, need_ctx_out):
    bsz, n_lat, _ = h_lat.shape
    rows = n_lat // GRID_W
    splits = [LRU_WIDTH, LRU_WIDTH + SSD_XBC, N_SCAN_COLS, N_SCAN_COLS + LRU_WIDTH]
    lx_l, xbc_l, dt_l, lg_l, z_l = jnp.split(h_lat @ w_in, splits, axis=-1)
    if need_ctx_out:
        lx_c, xbc_c, dt_c, lg_c, z_c = jnp.split(h_ctx @ w_in, splits, axis=-1)
    else:
        lx_c, xbc_c, dt_c = jnp.split(h_ctx @ w_in[:, :N_SCAN_COLS], splits[:2], axis=-1)

    lru_c, lru_l = _rglru_bidir(_dwconv(lx_c, lru_conv_w, lru_conv_b), _dwconv(lx_l, lru_conv_w, lru_conv_b),
                                lru_wa, lru_ba, lru_wx, lru_bx, lru_lambda, need_ctx_out)

    xbc_l = jax.nn.silu(_dwconv(_to_col_major(xbc_l, rows), ssd_conv_w, ssd_conv_b))
    xbc_c = jax.nn.silu(_dwconv(xbc_c, ssd_conv_w, ssd_conv_b))
    ssd_c, ssd_l = _ssd_bidir(xbc_c, dt_c, xbc_l, _to_col_major(dt_l, rows),
                              ssd_dt_bias, ssd_a_log, ssd_d, need_ctx_out)
    ssd_l = _from_col_major(ssd_l.reshape(bsz, n_lat, SSD_INNER), rows)

    cat_l = jnp.concatenate([lru_l * jax.nn.gelu(lg_l.astype(jnp.float32)),
                             _gated_group_rmsnorm(ssd_l, z_l, ssd_norm_g)], axis=-1)
    out_l = cat_l.astype(h_lat.dtype) @ w_out
    out_c = None
    if need_ctx_out:
        ssd_c = ssd_c.reshape(bsz, h_ctx.shape[1], SSD_INNER)
        cat_c = jnp.concatenate([lru_c * jax.nn.gelu(lg_c.astype(jnp.float32)),
                                 _gated_group_rmsnorm(ssd_c, z_c, ssd_norm_g)], axis=-1)
        out_c = cat_c.astype(h_ctx.dtype) @ w_out
    return out_l, out_c


def setup_inputs(seed: int = 0) -> dict:
    key = jax.random.key(seed)
    ks = jax.random.split(key, 32)

    def nrm(k, shape, scale):
        return jax.random.normal(k, shape, jnp.float32) * scale

    x = nrm(ks[0], (BATCH, SEQ, D_MODEL), 1.0)
    c = nrm(ks[1], (BATCH, D_MODEL), 1.0)
    ctx = nrm(ks[2], (BATCH, CTX_LEN, D_MODEL), 1.0)
    c_ctx = nrm(ks[3], (D_MODEL,), 1.0)
    ada_w = nrm(ks[4], (DEPTH, D_MODEL, N_MOD * D_MODEL), 0.5 * D_MODEL ** -0.5)
    ada_b = nrm(ks[5], (DEPTH, N_MOD * D_MODEL), 0.01)
    norm1_g = 1.0 + nrm(ks[6], (DEPTH, D_MODEL), 0.1)
    norm2_g = 1.0 + nrm(ks[7], (DEPTH, D_MODEL), 0.1)
    w_in = nrm(ks[8], (DEPTH, D_MODEL, N_IN_COLS), D_MODEL ** -0.5)
    lru_conv_w = nrm(ks[9], (DEPTH, CONV_K, LRU_WIDTH), CONV_K ** -0.5)
    lru_conv_b = nrm(ks[10], (DEPTH, LRU_WIDTH), 0.01)
    lru_wa = nrm(ks[11], (DEPTH, 2, LRU_HEADS, LRU_BLOCK, LRU_BLOCK), LRU_BLOCK ** -0.5)
    lru_ba = nrm(ks[12], (DEPTH, 2, LRU_WIDTH), 0.01)
    lru_wx = nrm(ks[13], (DEPTH, 2, LRU_HEADS, LRU_BLOCK, LRU_BLOCK), LRU_BLOCK ** -0.5)
    lru_bx = nrm(ks[14], (DEPTH, 2, LRU_WIDTH), 0.01)
    a_pow_c = jax.random.uniform(ks[15], (DEPTH, 2, LRU_WIDTH), jnp.float32, 0.9, 0.999)
    log_a = jnp.log(a_pow_c) / LRU_C
    lru_lambda = log_a - jnp.log(-jnp.expm1(log_a))
    ssd_conv_w = nrm(ks[16], (DEPTH, CONV_K, SSD_XBC), CONV_K ** -0.5)
    ssd_conv_b = nrm(ks[17], (DEPTH, SSD_XBC), 0.01)
    dt0 = jnp.exp(jax.random.uniform(ks[18], (DEPTH, 2, SSD_HEADS), jnp.float32, math.log(1e-3), math.log(1e-1)))
    ssd_dt_bias = dt0 + jnp.log(-jnp.expm1(-dt0))
    ssd_a_log = jnp.log(jax.random.uniform(ks[19], (DEPTH, 2, SSD_HEADS), jnp.float32, 1.0, 16.0))
    ssd_d = 1.0 + nrm(ks[20], (DEPTH, SSD_HEADS), 0.1)
    ssd_norm_g = 1.0 + nrm(ks[21], (DEPTH, SSD_INNER), 0.1)
    w_out = nrm(ks[22], (DEPTH, MIX_WIDTH, D_MODEL), MIX_WIDTH ** -0.5)
    mlp_w1 = nrm(ks[23], (DEPTH, D_MODEL, D_FF), D_MODEL ** -0.5)
    mlp_w2 = nrm(ks[24], (DEPTH, D_FF, D_MODEL), D_FF ** -0.5)
    final_g = 1.0 + nrm(ks[25], (D_MODEL,), 0.1)
    return {'x': x, 'c': c, 'ctx': ctx, 'c_ctx': c_ctx, 'ada_w': ada_w, 'ada_b': ada_b,
            'norm1_g': norm1_g, 'norm2_g': norm2_g, 'w_in': w_in,
            'lru_conv_w': lru_conv_w, 'lru_conv_b': lru_conv_b, 'lru_wa': lru_wa, 'lru_ba': lru_ba,
            'lru_wx': lru_wx, 'lru_bx': lru_bx, 'lru_lambda': lru_lambda,
            'ssd_conv_w': ssd_conv_w, 'ssd_conv_b': ssd_conv_b, 'ssd_dt_bias': ssd_dt_bias,
            'ssd_a_log': ssd_a_log, 'ssd_d': ssd_d, 'ssd_norm_g': ssd_norm_g,
            'w_out': w_out, 'mlp_w1': mlp_w1, 'mlp_w2': mlp_w2, 'final_g': final_g}


def reference(x, c, ctx, c_ctx, ada_w, ada_b, norm1_g, norm2_g, w_in,
              lru_conv_w, lru_conv_b, lru_wa, lru_ba, lru_wx, lru_bx, lru_lambda,
              ssd_conv_w, ssd_conv_b, ssd_dt_bias, ssd_a_log, ssd_d, ssd_norm_g,
              w_out, mlp_w1, mlp_w2, final_g):
    silu_c = jax.nn.silu(c)
    silu_cc = jax.nn.silu(c_ctx)
    for l in range(DEPTH):
        need_ctx_out = l < DEPTH - 1
        mod = silu_c @ ada_w[l] + ada_b[l]
        sh1, sc1, g1, sh2, sc2, g2 = jnp.split(mod[:, None, :], N_MOD, axis=-1)
        csh1, csc1, cg1, csh2, csc2, cg2 = jnp.split(silu_cc @ ada_w[l] + ada_b[l], N_MOD, axis=-1)
        h_lat = _modulate(_rmsnorm(x, norm1_g[l]), sh1, sc1)
        h_ctx = _modulate(_rmsnorm(ctx, norm1_g[l]), csh1, csc1)
        y_lat, y_ctx = _mixer(h_lat, h_ctx, w_in[l], lru_conv_w[l], lru_conv_b[l], lru_wa[l], lru_ba[l],
                              lru_wx[l], lru_bx[l], lru_lambda[l], ssd_conv_w[l], ssd_conv_b[l],
                              ssd_dt_bias[l], ssd_a_log[l], ssd_d[l], ssd_norm_g[l], w_out[l], need_ctx_out)
        x = x + g1 * y_lat
        x = x + g2 * _sq_relu_mlp(_modulate(_rmsnorm(x, norm2_g[l]), sh2, sc2), mlp_w1[l], mlp_w2[l])
        if need_ctx_out:
            ctx = ctx + cg1 * y_ctx
            ctx = ctx + cg2 * _sq_relu_mlp(_modulate(_rmsnorm(ctx, norm2_g[l]), csh2, csc2), mlp_w1[l], mlp_w2[l])
    return _rmsnorm(x, final_g)
```

```python
import numpy as np
import concourse.bass as bass
import concourse.mybir as mybir
from concourse.bass_utils import run_bass_kernel_spmd

F32 = mybir.dt.float32
BF16 = mybir.dt.bfloat16
AF = mybir.ActivationFunctionType
ALU = mybir.AluOpType
AX = mybir.AxisListType

D = 1024
SEQ = 2048
CTX = 256
DEPTH = 2
DFF = 4096
KT = D // 128
NIN = 2576
NSCAN = 1552
EPS = 1e-6
GRAN = 64


class Acc:
    __slots__ = ("ap", "space", "ranges")

    def __init__(self, ap, space, ranges):
        self.ap = ap
        self.space = space
        self.ranges = ranges

    def w(self, ap):
        return Acc(ap, self.space, self.ranges)


class View:
    def __init__(self, base_ap, space, byte_off, shape, dtype, base_off=0):
        self.space = space
        self.off = byte_off
        self.shape = tuple(shape)
        self.dtype = dtype
        self.sz = 2 if dtype == BF16 else 4
        n = int(np.prod(shape))
        e0 = (byte_off - base_off) // self.sz
        ap = base_ap[:, e0:e0 + n]
        if len(shape) == 2:
            ap = ap.rearrange("p (a b) -> p a b", b=shape[1])
        elif len(shape) == 3:
            ap = ap.rearrange("p (a b c) -> p a b c", b=shape[1], c=shape[2])
        self.ap = ap
        self.nbytes = n * self.sz

    def __call__(self, *idx, p=None):
        idx = list(idx) + [slice(None)] * (len(self.shape) - len(idx))
        key = [slice(None) if p is None else p]
        los, his = [], []
        for i, d in zip(idx, self.shape):
            if isinstance(i, int):
                los.append(i)
                his.append(i + 1)
                key.append(i)
            else:
                lo = 0 if i.start is None else i.start
                hi = d if i.stop is None else i.stop
                assert 0 <= lo < hi <= d, (i, d, self.shape)
                los.append(lo)
                his.append(hi)
                key.append(slice(lo, hi))
        ap = self.ap[tuple(key)]
        strides = []
        s = self.sz
        for d in reversed(self.shape):
            strides.append(s)
            s *= d
        strides = strides[::-1]
        nd = len(self.shape)
        outer = 1
        for k in range(nd - 1):
            outer *= his[k] - los[k]
        if outer > 48:
            lo_b = self.off + sum(l * st for l, st in zip(los, strides))
            hi_b = self.off + sum((h - 1) * st for h, st in zip(his, strides)) + self.sz
            ranges = [(lo_b, hi_b)]
        else:
            ranges = []
            import itertools
            for combo in itertools.product(*[range(los[k], his[k]) for k in range(nd - 1)]):
                b = self.off + sum(c * st for c, st in zip(combo, strides[:-1]))
                ranges.append((b + los[-1] * strides[-1], b + his[-1] * strides[-1]))
        return Acc(ap, self.space, ranges)


class Eng:
    def __init__(self, nc, eng, name, is_pe=False):
        self.eng = eng
        self.name = name
        self.is_pe = is_pe
        self.sem = nc.alloc_semaphore("s_" + name)
        self.count = 0
        self.seen = {}


class DSem:
    def __init__(self, nc, name):
        self.sem = nc.alloc_semaphore(name)
        self.count = 0
        self.name = name


class Tracker:
    def __init__(self, nc):
        self.nc = nc
        self.lastw = {"sb": {}, "ps": {}, "dr": {}}
        self.readers = {"sb": {}, "ps": {}, "dr": {}}
        self.n_wait = 0
        self.n_inst = 0

    @staticmethod
    def _grans(ranges, space="sb"):
        gran = 2048 if space == "ps" else GRAN
        for lo, hi in ranges:
            for g in range(lo // gran, (hi - 1) // gran + 1):
                yield g

    def _collect(self, reads, writes, E=None):
        need = {}
        for a in reads:
            lw = self.lastw[a.space]
            for g in self._grans(a.ranges, a.space):
                w = lw.get(g)
                if w is not None and need.get(w[0], 0) < w[1]:
                    need[w[0]] = w[1]
            if a.space == "ps":
                rd = self.readers["ps"]
                for g in self._grans(a.ranges, a.space):
                    r = rd.get(g)
                    if r:
                        for k, v in r.items():
                            if k is not E and need.get(k, 0) < v:
                                need[k] = v
        for a in writes:
            lw = self.lastw[a.space]
            rd = self.readers[a.space]
            for g in self._grans(a.ranges, a.space):
                w = lw.get(g)
                if w is not None and need.get(w[0], 0) < w[1]:
                    need[w[0]] = w[1]
                r = rd.get(g)
                if r:
                    for k, v in r.items():
                        if need.get(k, 0) < v:
                            need[k] = v
        return need

    def _emit_waits(self, E, need):
        for semobj, val in need.items():
            if semobj is E and E.is_pe:
                continue
            if E.seen.get(semobj, 0) >= val:
                continue
            h = semobj.sem
            E.eng.wait_ge(h, val)
            E.seen[semobj] = val
            self.n_wait += 1

    def _record(self, tag, reads, writes):
        for a in writes:
            lw = self.lastw[a.space]
            rd = self.readers[a.space]
            for g in self._grans(a.ranges, a.space):
                lw[g] = tag
                if g in rd:
                    del rd[g]
        for a in reads:
            rd = self.readers[a.space]
            for g in self._grans(a.ranges, a.space):
                r = rd.get(g)
                if r is None:
                    rd[g] = {tag[0]: tag[1]}
                elif r.get(tag[0], 0) < tag[1]:
                    r[tag[0]] = tag[1]

    def op(self, E, fn, reads=(), writes=(), inc=True):
        need = self._collect(reads, writes, E)
        self._emit_waits(E, need)
        inst = fn(E.eng)
        self.n_inst += 1
        if inc:
            E.count += 1
            inst.then_inc(E.sem, 1)
            tag = (E, E.count)
        else:
            tag = (E, E.count + 1)
        self._record(tag, reads, writes)
        return inst

    def dma(self, Q, dsem, out_ap, in_ap, reads=(), writes=(), **kw):
        need = self._collect(reads, writes)
        self._emit_waits(Q, need)
        inst = Q.eng.dma_start(out=out_ap, in_=in_ap, **kw)
        dsem.count += 16
        inst.then_inc(dsem.sem, 16)
        self.n_inst += 1
        tag = (dsem, dsem.count)
        self._record(tag, reads, writes)
        return inst


class Bump:
    def __init__(self, t32, tbf, space, limit):
        self.t32 = t32
        self.tbf = tbf
        self.space = space
        self.ptr = 0
        self.limit = limit
        self.peak = 0

    def alloc(self, shape, dtype, align=64):
        sz = 2 if dtype == BF16 else 4
        n = int(np.prod(shape)) * sz
        self.ptr = (self.ptr + align - 1) // align * align
        v = View(self.t32 if dtype == F32 else self.tbf, self.space, self.ptr, shape, dtype)
        self.ptr += n
        self.peak = max(self.peak, self.ptr)
        assert self.ptr <= self.limit, ("SBUF arena overflow", self.ptr, self.limit)
        return v

    def mark(self):
        return self.ptr

    def reset(self, m):
        self.ptr = m


class Ctx:
    pass


def build_nc(cfg=None):
    cfg = dict(cfg or {})
    n_layers = cfg.get("n_layers", DEPTH)
    K = Ctx()
    K.cfg = cfg

    nc = bass.Bass("TRN2", target_bir_lowering=False)
    K.nc = nc

    def din(name, shape):
        return nc.dram_tensor(name, list(shape), F32, kind="ExternalInput").ap()

    K.x_d = din("x", [SEQ, D])
    K.ctx_d = din("ctx", [CTX, D])
    K.ccol_d = din("ccol", [128, 2 * KT])
    K.ada_w_d = din("ada_w", [DEPTH, D, 6 * D])
    K.ada_b_d = din("ada_b", [DEPTH, 6 * D])
    K.ngcol_d = din("ngcol", [128, DEPTH * 2 * KT])
    K.w_in_d = din("w_in", [DEPTH, D, NIN])
    K.w_out_d = din("w_out", [DEPTH, D, D])
    K.w1_d = din("mlp_w1", [DEPTH, D, DFF])
    K.w2_d = din("mlp_w2", [DEPTH, DFF, D])
    K.fg_d = din("final_g_bc", [128, D])
    K.ident_d = din("ident", [128, 128])
    if cfg.get("mixer_decl"):
        cfg["mixer_decl"](K, din)
    K.out_d = nc.dram_tensor("out", [SEQ, D], F32, kind="ExternalOutput").ap()
    K.dbg_n = [0]
    if cfg.get("dump"):
        K.dbg_d = nc.dram_tensor("dbg", [24, 128, 2560], F32, kind="ExternalOutput").ap()

    def dump(acc, n, name=""):
        if not cfg.get("dump") or K.dbg_n[0] >= 24:
            return
        j = K.dbg_n[0]
        K.dbg_n[0] += 1
        print("dump slot", j, name, n)
        if acc.ap.dtype != F32:
            return
        T.dma(K.SP, K.dsem("d_dbg"), K.dbg_d[j, :, 0:n], acc.ap, reads=[acc])

    K.dump = dump
    K.modrows_d = nc.dram_tensor("modrows", [2, 6 * D], F32, kind="Internal").ap()

    ARENA = 212800 // 64 * 64
    arena = nc.alloc_sbuf_tensor("arena", [128, ARENA // 4], F32)
    sb = Bump(arena, arena.bitcast(BF16), "sb", ARENA)
    K.sb = sb
    psum_t = [nc.alloc_psum_tensor("ps%d" % i, [128, 1024], F32) for i in range(4)]
    PSP = []
    for i, t in enumerate(psum_t):
        PSP.append((View(t, "ps", i * 4096, [1024], F32, i * 4096), View(t.bitcast(BF16), "ps", i * 4096, [2048], BF16, i * 4096)))
    PSB = []
    PSBb = []
    for i, t in enumerate(psum_t):
        for h in range(2):
            PSB.append(View(t, "ps", i * 4096 + h * 2048, [512], F32, i * 4096))
            PSBb.append(View(t.bitcast(BF16), "ps", i * 4096 + h * 2048, [1024], BF16, i * 4096))

    T = Tracker(nc)
    K.T = T
    PE = Eng(nc, nc.tensor, "pe", is_pe=True)
    ACT = Eng(nc, nc.scalar, "act")
    DVE = Eng(nc, nc.vector, "dve")
    POOL = Eng(nc, nc.gpsimd, "pool")
    SP = Eng(nc, nc.sync, "sp")
    K.PE, K.ACT, K.DVE, K.POOL, K.SP = PE, ACT, DVE, POOL, SP

    ps_rr = [0]
    ps_lim = [8]

    def ps_pair():
        b = (ps_rr[0] + 1) // 2 * 2
        if b + 2 > ps_lim[0]:
            b = 0
        ps_rr[0] = (b + 2) % ps_lim[0]
        return PSP[b // 2]

    def ps_bank(bf=False):
        b = ps_rr[0]
        ps_rr[0] = (b + 1) % ps_lim[0]
        return PSBb[b] if bf else PSB[b]

    K.ps_pair, K.ps_bank, K.ps_lim, K.PSB, K.PSP, K.PSBb, K.ps_rr = ps_pair, ps_bank, ps_lim, PSB, PSP, PSBb, ps_rr

    dsems = {}

    def dsem(name):
        if name not in dsems:
            dsems[name] = DSem(nc, name)
        return dsems[name]

    K.dsem = dsem

    xs = sb.alloc([16, D], F32)
    cx = sb.alloc([2, D], F32)
    ident = sb.alloc([128], BF16)
    gates = sb.alloc([2, D], F32)
    ccol = sb.alloc([2 * KT], F32)
    ngcol = sb.alloc([DEPTH * 2 * KT], F32)
    Sst = sb.alloc([KT, 64], BF16)
    modcol = sb.alloc([2, 48], F32)
    Gsh = sb.alloc([2, 4, KT], F32)
    stat = sb.alloc([64], F32)
    ones_f = sb.alloc([128], F32)
    K.xs, K.cx, K.ident, K.gates, K.stat, K.ones_f, K.Gsh, K.modcol = xs, cx, ident, gates, stat, ones_f, Gsh, modcol

    dsem_ld = dsem("d_ld")
    dsem_misc = dsem("d_misc")
    dsem_out = dsem("d_out")

    def seal(ds, accs):
        for a in accs:
            T._record((ds, ds.count), [], [a])

    K.seal = seal

    for t in range(16):
        a = xs(t)
        T.dma(SP, dsem_ld, a.ap, K.x_d[t * 128:(t + 1) * 128, :], writes=[a])
    for t in range(2):
        a = cx(t)
        T.dma(SP, dsem_ld, a.ap, K.ctx_d[t * 128:(t + 1) * 128, :], writes=[a])
    seal(dsem_ld, [xs(t) for t in range(16)] + [cx(t) for t in range(2)])

    a = ccol()
    T.dma(SP, dsem("d_cc"), a.ap, K.ccol_d, writes=[a])
    a = ngcol()
    T.dma(SP, dsem("d_cc"), a.ap, K.ngcol_d, writes=[a])
    seal(dsem("d_cc"), [ccol(), ngcol()])
    a = ident()
    T.dma(POOL, dsem("d_ident"), a.ap, K.ident_d, writes=[a])

    a = ones_f()
    T.op(DVE, lambda e: e.memset(a.ap, 1.0), writes=[a])
    a = Sst()
    T.op(DVE, lambda e: e.memset(a.ap, 0.0), writes=[a])
    silu_tmp = stat(slice(32, 48))
    T.op(ACT, lambda e: e.activation(out=silu_tmp.ap, in_=ccol().ap, func=AF.Silu), reads=[ccol()], writes=[silu_tmp])
    for j in range(2):
        dst = Sst(slice(0, KT), slice(32 * j, 32 * j + 1))
        src = stat(slice(32 + KT * j, 32 + KT * (j + 1)))
        T.op(DVE, lambda e: e.tensor_copy(out=dst.ap, in_=src.ap.rearrange("p (k o) -> p k o", o=1)),
             reads=[src], writes=[dst])

    def dr_acc(r, c0, c1):
        return Acc(K.modrows_d[r:r + 1, c0:c1], "dr", [((r * 6 * D + c0) * 4, (r * 6 * D + c1) * 4)])

    def load_gates(mi, l):
        for r in range(2):
            if r == 1 and l == DEPTH - 1:
                continue
            src = dr_acc(r, mi * D, (mi + 1) * D)
            dst = gates(r)
            T.dma(SP, dsem("d_gate%d" % r), dst.ap, src.ap.partition_broadcast(128), reads=[src], writes=[dst])

    K.load_gates = load_gates

    def mod_begin(l):
        st = {"l": l}
        st["wbuf"] = [sb.alloc([KT, 512], BF16) for _ in range(2)]
        st["adab"] = [sb.alloc([512], F32) for _ in range(2)]
        st["rowbs"] = [sb.alloc([2, 512], F32) for _ in range(2)]
        st["colps"] = PSP[3][0]
        st["next"] = 0
        ps_lim[0] = 6
        if ps_rr[0] >= 6:
            ps_rr[0] = 0
        return st

    def mod_block(st):
        j = st["next"]
        if j >= 12:
            return
        st["next"] = j + 1
        l = st["l"]
        colps = st["colps"]
        wb = st["wbuf"][j % 2]
        rowb = st["rowbs"][j % 2]
        ab = st["adab"][j % 2]
        a = wb()
        T.dma(POOL, dsem("d_ada%d" % (j % 2)), a.ap,
              K.ada_w_d[l, :, j * 512:(j + 1) * 512].rearrange("(kt p) n -> p kt n", p=128), writes=[a])
        for prt in (0, 32):
            a = ab(p=slice(prt, prt + 1))
            T.dma(SP, dsem("d_adab%d_%d" % (j % 2, prt)), a.ap, K.ada_b_d[l:l + 1, j * 512:(j + 1) * 512], writes=[a])
        ps = ps_bank()
        o = ps(slice(0, 512), p=slice(0, 33))
        for k in range(KT):
            lh = Sst(k, slice(0, 33))
            rh = wb(k)
            T.op(PE, lambda e: e.matmul(o.ap, lhsT=lh.ap, rhs=rh.ap, start=(k == 0), stop=(k == KT - 1)),
                 reads=[lh, rh], writes=[o], inc=(k == KT - 1))
        for r, prt in enumerate((0, 32)):
            dst = rowb(r, p=slice(prt, prt + 1))
            i0_ = ps(slice(0, 512), p=slice(prt, prt + 1))
            i1_ = ab(p=slice(prt, prt + 1))
            T.op(DVE, lambda e: e.tensor_tensor(out=dst.ap, in0=i0_.ap, in1=i1_.ap, op=ALU.add),
                 reads=[i0_, i1_], writes=[dst])
            dd = dr_acc(r, j * 512, (j + 1) * 512)
            T.dma(SP, dsem("d_mrow%d_%d" % (r, j % 2)), dd.ap, dst.ap, reads=[dst], writes=[dd])
            for q in range(4):
                ft = j * 4 + q
                oc = colps(slice(64 * r + ft, 64 * r + ft + 1))
                lh = rowb(r, slice(q * 128, (q + 1) * 128), p=slice(prt, prt + 1))
                rh = ones_f(slice(0, 1), p=slice(prt, prt + 1))
                T.op(PE, lambda e: e.matmul(oc.ap, lhsT=lh.ap, rhs=rh.ap, start=True, stop=True),
                     reads=[lh, rh], writes=[oc])

    def mod_end(st):
        while st["next"] < 12:
            mod_block(st)
        ps_lim[0] = 8
        l = st["l"]
        colps = st["colps"]
        for r in range(2):
            for n_i, (mi_sh, mi_sc) in enumerate(((0, 1), (3, 4))):
                g = ngcol(slice((l * 2 + n_i) * KT, (l * 2 + n_i + 1) * KT))
                sc = colps(slice(64 * r + mi_sc * KT, 64 * r + (mi_sc + 1) * KT))
                sh = colps(slice(64 * r + mi_sh * KT, 64 * r + (mi_sh + 1) * KT))
                Gd = Gsh(r, 2 * n_i)
                shd = Gsh(r, 2 * n_i + 1)
                T.op(DVE, lambda e: e.scalar_tensor_tensor(out=Gd.ap, in0=sc.ap, scalar=1.0, in1=g.ap,
                                                           op0=ALU.add, op1=ALU.mult),
                     reads=[sc, g], writes=[Gd])
                T.op(DVE, lambda e: e.tensor_copy(out=shd.ap, in_=sh.ap), reads=[sh], writes=[shd])

    def compute_mod(l):
        m = sb.mark()
        st = mod_begin(l)
        mod_end(st)
        sb.reset(m)

    K.mod_begin, K.mod_block, K.mod_end = mod_begin, mod_block, mod_end

    def norm_tile(src, hn, junk, stat_col):
        ss = stat(slice(stat_col, stat_col + 1))
        rs = stat(slice(stat_col + 1, stat_col + 2))
        T.op(ACT, lambda e: e.activation(out=junk.ap, in_=src.ap, func=AF.Square, accum_out=ss.ap),
             reads=[src], writes=[junk, ss])
        T.op(ACT, lambda e: e.activation(out=rs.ap, in_=ss.ap, func=AF.Sqrt, scale=1.0 / D, bias=EPS),
             reads=[ss], writes=[rs])
        T.op(DVE, lambda e: e.reciprocal(out=rs.ap, in_=rs.ap), reads=[rs], writes=[rs])
        T.op(DVE, lambda e: e.tensor_scalar(out=hn.ap, in0=src.ap, scalar1=rs.ap, scalar2=None, op0=ALU.mult),
             reads=[src, rs], writes=[hn])
        return rs

    def tile_to_hT(hn, hT, col0, r, n_i, modulate=True):
        banks = [ps_bank(bf=True), ps_bank(bf=True)]
        for k in range(KT):
            o = banks[k // 4](slice((k % 4) * 128, (k % 4 + 1) * 128))
            i = hn(slice(k * 128, (k + 1) * 128))
            T.op(PE, lambda e: e.transpose(out=o.ap, in_=i.ap, identity=ident().ap),
                 reads=[i, ident()], writes=[o], inc=(k % 4 == 3))
        for kk in range(4):
            for half, eng in ((0, ACT), (1, DVE)):
                k = half * 4 + kk
                o = hT(k, slice(col0, col0 + 128))
                i = banks[half](slice(kk * 128, (kk + 1) * 128))
                G = Gsh(r, 2 * n_i, slice(k, k + 1))
                sh = Gsh(r, 2 * n_i + 1, slice(k, k + 1))
                if not modulate:
                    if eng is ACT:
                        T.op(ACT, lambda e: e.copy(out=o.ap, in_=i.ap), reads=[i], writes=[o])
                    else:
                        T.op(DVE, lambda e: e.tensor_copy(out=o.ap, in_=i.ap), reads=[i], writes=[o])
                elif eng is ACT:
                    T.op(ACT, lambda e: e.activation(out=o.ap, in_=i.ap, func=AF.Identity, scale=G.ap, bias=sh.ap),
                         reads=[i, G, sh], writes=[o])
                else:
                    T.op(DVE, lambda e: e.tensor_scalar(out=o.ap, in0=i.ap, scalar1=G.ap, scalar2=sh.ap,
                                                        op0=ALU.mult, op1=ALU.add),
                         reads=[i, G, sh], writes=[o])

    def apply_mod(hT, n_i, with_ctx):
        for k in range(KT):
            for r, (c0, c1) in enumerate(((CTX, CTX + SEQ), (0, CTX))):
                if r == 1 and not with_ctx:
                    continue
                a_ = hT(k, slice(c0, c1))
                G = Gsh(r, 2 * n_i, slice(k, k + 1))
                sh = Gsh(r, 2 * n_i + 1, slice(k, k + 1))
                T.op(DVE, lambda e: e.tensor_scalar(out=a_.ap, in0=a_.ap, scalar1=G.ap, scalar2=sh.ap,
                                                    op0=ALU.mult, op1=ALU.add), reads=[a_, G, sh], writes=[a_])

    K.apply_mod = apply_mod

    def norm_all_to_hT(l, n_i, hT, with_ctx, hn, junk, modulate=True, hook=None):
        tiles = []
        if with_ctx:
            tiles += [(1, cx(t), t * 128) for t in range(2)]
        tiles += [(0, xs(t), 256 + t * 128) for t in range(16)]
        n = len(tiles)
        norm_tile(tiles[0][1], hn[0](), junk(), 0)
        for t in range(n):
            if t + 1 < n:
                norm_tile(tiles[t + 1][1], hn[(t + 1) % 2](), junk(), 2 * ((t + 1) % 8))
            r, src, col0 = tiles[t]
            tile_to_hT(hn[t % 2], hT, col0, r, n_i, modulate)
            if hook is not None:
                hook(t)

    K.norm_all_to_hT = norm_all_to_hT

    def mlp_phase(l):
        m = sb.mark()
        with_ctx = l < DEPTH - 1
        load_gates(5, l)
        hT = sb.alloc([KT, CTX + SEQ], BF16)
        hn = [sb.alloc([D], BF16) for _ in range(2)]
        junk = sb.alloc([D], BF16)
        w1b = [sb.alloc([KT, 512], BF16) for _ in range(2)]
        w2b = [sb.alloc([4, D], BF16) for _ in range(2)]

        def load_mlp_w(fg):
            a1 = w1b[fg % 2]()
            T.dma(POOL, dsem("d_w1_%d" % (fg % 2)), a1.ap,
                  K.w1_d[l, :, fg * 512:(fg + 1) * 512].rearrange("(kt p) n -> p kt n", p=128), writes=[a1])
            a2 = w2b[fg % 2]()
            T.dma(POOL, dsem("d_w2_%d" % (fg % 2)), a2.ap,
                  K.w2_d[l, fg * 512:(fg + 1) * 512, :].rearrange("(ft p) n -> p ft n", p=128), writes=[a2])

        load_mlp_w(0)
        norm_all_to_hT(l, 1, hT, with_ctx, hn, junk)
        h1T = [sb.alloc([4, 512], BF16) for _ in range(2)]
        tmp = [sb.alloc([D], F32) for _ in range(2)]
        blocks = []
        if with_ctx:
            blocks.append((1, 0, [cx(0), cx(1)]))
        for b in range(4):
            blocks.append((0, 256 + b * 512, [xs(t) for t in range(4 * b, 4 * b + 4)]))
        cnt = 0
        ev = 0
        for fg in range(8):
            wb1 = w1b[fg % 2]
            wb2 = w2b[fg % 2]
            if fg + 1 < 8:
                load_mlp_w(fg + 1)
            for r, c0, tiles in blocks:
                nt = len(tiles)
                ntok = nt * 128
                h1 = h1T[cnt % 2]
                cnt += 1
                for fi in range(4):
                    ps = ps_bank()
                    o = ps(slice(0, ntok))
                    for k in range(KT):
                        lh = wb1(k, slice(fi * 128, (fi + 1) * 128))
                        rh = hT(k, slice(c0, c0 + ntok))
                        T.op(PE, lambda e: e.matmul(o.ap, lhsT=lh.ap, rhs=rh.ap, start=(k == 0), stop=(k == KT - 1)),
                             reads=[lh, rh], writes=[o], inc=(k == KT - 1))
                    dst = h1(fi, slice(0, ntok))
                    T.op(ACT, lambda e: e.activation(out=dst.ap, in_=o.ap, func=AF.Relu), reads=[o], writes=[dst])
                    T.op(ACT, lambda e: e.activation(out=dst.ap, in_=dst.ap, func=AF.Square),
                         reads=[dst], writes=[dst])
                if cfg.get("mlp_upto", 3) < 3:
                    continue
                for ti in range(nt):
                    pp = ps_pair()[0]
                    for hh in range(2):
                        o = pp(slice(hh * 512, (hh + 1) * 512))
                        for fi in range(4):
                            lh = h1(fi, slice(ti * 128, (ti + 1) * 128))
                            rh = wb2(fi, slice(hh * 512, (hh + 1) * 512))
                            T.op(PE, lambda e: e.matmul(o.ap, lhsT=lh.ap, rhs=rh.ap, start=(fi == 0), stop=(fi == 3)),
                                 reads=[lh, rh], writes=[o], inc=(fi == 3))
                    gt = gates(r)
                    tp = tmp[ev % 2]()
                    ev += 1
                    o = pp()
                    dst = tiles[ti]
                    T.op(DVE, lambda e: e.tensor_tensor(out=tp.ap, in0=o.ap, in1=gt.ap, op=ALU.mult),
                         reads=[o, gt], writes=[tp])
                    T.op(POOL if ev % 2 == 0 else DVE,
                         lambda e: e.tensor_tensor(out=dst.ap, in0=dst.ap, in1=tp.ap, op=ALU.add),
                         reads=[dst, tp], writes=[dst])
        sb.reset(m)

    for l in range(n_layers):
        stages = cfg.get("stages", ("mod", "mixer", "mlp"))
        if "mod" in stages and not ("mixer" in stages and cfg.get("mixer_fn") and cfg.get("mod_in_mixer", True)):
            compute_mod(l)
        if "mixer" in stages and cfg.get("mixer_fn"):
            cfg["mixer_fn"](K, l)
        if "mlp" in stages:
            mlp_phase(l)

    m = sb.mark()
    fg = sb.alloc([D], F32)
    ob = [sb.alloc([D], F32) for _ in range(2)]
    junk = sb.alloc([D], BF16)
    a = fg()
    T.dma(SP, dsem("d_fg"), a.ap, K.fg_d, writes=[a])
    for t in range(16):
        src = xs(t)
        ss = stat(slice(2 * (t % 8), 2 * (t % 8) + 1))
        rs = stat(slice(2 * (t % 8) + 1, 2 * (t % 8) + 2))
        jk = junk()
        T.op(ACT, lambda e: e.activation(out=jk.ap, in_=src.ap, func=AF.Square, accum_out=ss.ap),
             reads=[src], writes=[jk, ss])
        T.op(ACT, lambda e: e.activation(out=rs.ap, in_=ss.ap, func=AF.Sqrt, scale=1.0 / D, bias=EPS),
             reads=[ss], writes=[rs])
        T.op(DVE, lambda e: e.reciprocal(out=rs.ap, in_=rs.ap), reads=[rs], writes=[rs])
        o = ob[t % 2]()
        T.op(DVE, lambda e: e.scalar_tensor_tensor(out=o.ap, in0=src.ap, scalar=rs.ap, in1=fg().ap,
                                                   op0=ALU.mult, op1=ALU.mult),
             reads=[src, rs, fg()], writes=[o])
        T.dma(SP, dsem("d_out%d" % (t % 2)), K.out_d[t * 128:(t + 1) * 128, :], o.ap, reads=[o])
    for t in range(2):
        SP.eng.wait_ge(dsem("d_out%d" % t).sem, dsem("d_out%d" % t).count)
    if cfg.get("dump") and "d_dbg" in dsems:
        SP.eng.wait_ge(dsems["d_dbg"].sem, dsems["d_dbg"].count)
    sb.reset(m)
    print("build: inst=%d waits=%d sbuf_peak=%d" % (T.n_inst, T.n_wait, sb.peak))
    return nc


WR = 2310
NLC = 11


def mixer_decl(K, din):
    K.lrucol_d = din("lrucol", [128, DEPTH * 4 * NLC])
    K.wab_d = din("wab", [DEPTH * 16, 128, 128])
    if K.cfg.get("ssd_decl"):
        K.cfg["ssd_decl"](K, din)


def mixer_phase(K, l):
    nc, sb, T = K.nc, K.sb, K.T
    PE, ACT, DVE, POOL, SP = K.PE, K.ACT, K.DVE, K.POOL, K.SP
    xs, cx, gates, ident = K.xs, K.cx, K.gates, K.ident
    ps_bank, ps_pair, dsem = K.ps_bank, K.ps_pair, K.dsem
    cfg = K.cfg
    with_ctx_out = l < DEPTH - 1

    m0 = sb.mark()
    NT = CTX + SEQ
    hT = sb.alloc([KT, NT], BF16)
    catT = sb.alloc([4, NT], BF16)
    m1 = sb.mark()
    hn = [sb.alloc([D], BF16) for _ in range(2)]
    junk = sb.alloc([D], BF16)
    if cfg.get("mod_in_mixer", True):
        mst = K.mod_begin(l)

        def hook(t):
            if t % 3 != 2:
                K.mod_block(mst)

        K.norm_all_to_hT(l, 0, hT, True, hn, junk, modulate=False, hook=hook)
        K.mod_end(mst)
        K.apply_mod(hT, 0, True)
    else:
        K.norm_all_to_hT(l, 0, hT, True, hn, junk)
    sb.reset(m1)
    K.load_gates(2, l)

    wt_bufs = [sb.alloc([KT, 128], BF16) for _ in range(2)]
    wt_cnt = [0]

    def load_wcols(col0, ncols=128):
        i = wt_cnt[0] % 2
        wt_cnt[0] += 1
        wb = wt_bufs[i]
        a = wb(slice(0, KT), slice(0, ncols))
        T.dma(POOL, dsem("d_win%d" % i), a.ap,
              K.w_in_d[l, :, col0:col0 + ncols].rearrange("(kt p) n -> p kt n", p=128), writes=[a])
        return wb

    blocks = [(0, CTX, 2)] + [(CTX + 512 * b, 512, 260 + 512 * b) for b in range(4)]

    def outproj_half(which):
        m = sb.mark()
        wo = sb.alloc([4, D], BF16)
        wos = [sb.alloc([4, D], BF16) for _ in range(2 if with_ctx_out else 1)]
        a = wo()
        T.dma(POOL, dsem("d_wo"), a.ap,
              K.w_out_d[l, which * 512:(which + 1) * 512, :].rearrange("(ft p) n -> p ft n", p=128), writes=[a])
        for r in range(len(wos)):
            for fi in range(4):
                src = wo(fi)
                dst = wos[r](fi)
                gt = gates(r)
                T.op(DVE if (fi + r) % 2 == 0 else POOL,
                     lambda e: e.tensor_tensor(out=dst.ap, in0=src.ap, in1=gt.ap, op=ALU.mult),
                     reads=[src, gt], writes=[dst])
        tl = []
        if with_ctx_out:
            tl += [(1, cx(t), t * 128) for t in range(2)]
        tl += [(0, xs(t), CTX + t * 128) for t in range(16)]
        for ev, (r, dst, c0) in enumerate(tl):
            pp = ps_pair()[0]
            wr = wos[r]
            for hh in range(2):
                o = pp(slice(hh * 512, (hh + 1) * 512))
                for fi in range(4):
                    lh = catT(fi, slice(c0, c0 + 128))
                    rh = wr(fi, slice(hh * 512, (hh + 1) * 512))
                    T.op(PE, lambda e: e.matmul(o.ap, lhsT=lh.ap, rhs=rh.ap, start=(fi == 0), stop=(fi == 3)),
                         reads=[lh, rh], writes=[o], inc=(fi == 3))
            o = pp()
            T.op(DVE, lambda e: e.tensor_tensor(out=dst.ap, in0=dst.ap, in1=o.ap, op=ALU.add),
                 reads=[dst, o], writes=[dst])
        sb.reset(m)

    def lru_part():
        m = sb.mark()
        LX = sb.alloc([WR], F32)
        U = sb.alloc([WR], F32)
        Ub = sb.alloc([WR], BF16)
        H0 = sb.alloc([WR], F32)
        H1 = LX
        NTT = CTX + SEQ
        Af = sb.alloc([NTT], F32)
        Sf = sb.alloc([NTT], F32)
        Bf = sb.alloc([NTT], F32)
        Gt = [View(sb.t32, "sb", Af.off + 2048 * i_, [512], F32) for i_ in range(2)]
        Bb = [View(sb.t32, "sb", Af.off + 4096 + 2048 * i_, [512], F32) for i_ in range(2)]
        lcol = sb.alloc([4, NLC], F32)
        lhalf = sb.alloc([4, NLC], F32)
        lsp = sb.alloc([4, NLC], F32)
        lcs = sb.alloc([4, NLC], F32)
        lhcs = sb.alloc([4, NLC], F32)
        wab = sb.alloc([16, 128], BF16)
        a = lcol()
        T.dma(SP, dsem("d_lcol"), a.ap, K.lrucol_d[:, l * 4 * NLC:(l + 1) * 4 * NLC], writes=[a])
        a = wab()
        T.dma(POOL, dsem("d_wab"), a.ap, K.wab_d[l * 16:(l + 1) * 16, :, :].rearrange("m p n -> p m n"), writes=[a])
        T.op(DVE, lambda e: e.tensor_scalar(out=lhalf().ap, in0=lcol().ap, scalar1=0.5, scalar2=None, op0=ALU.mult),
             reads=[lcol()], writes=[lhalf()])
        T.op(ACT, lambda e: e.activation(out=lsp().ap, in_=lcol().ap, func=AF.Exp, scale=-1.0),
             reads=[lcol()], writes=[lsp()])
        T.op(ACT, lambda e: e.activation(out=lsp().ap, in_=lsp().ap, func=AF.Ln, bias=1.0),
             reads=[lsp()], writes=[lsp()])
        T.op(DVE, lambda e: e.tensor_scalar(out=lcs().ap, in0=lsp().ap, scalar1=-8.0, scalar2=None, op0=ALU.mult),
             reads=[lsp()], writes=[lcs()])
        T.op(DVE, lambda e: e.tensor_scalar(out=lhcs().ap, in0=lsp().ap, scalar1=-4.0, scalar2=None, op0=ALU.mult),
             reads=[lsp()], writes=[lhcs()])
        for (c0, c1) in ((0, 2), (258, 260), (2308, 2310)):
            a = LX(slice(c0, c1))
            T.op(DVE, lambda e: e.memset(a.ap, 0.0), writes=[a])

        bcnt = 0
        wb_next = load_wcols(0)
        for i in range(4):
            wb = wb_next
            for (hc0, n, dc) in blocks:
                ps = ps_bank()
                o = ps(slice(0, n))
                for k in range(KT):
                    lh = wb(k)
                    rh = hT(k, slice(hc0, hc0 + n))
                    T.op(PE, lambda e: e.matmul(o.ap, lhsT=lh.ap, rhs=rh.ap, start=(k == 0), stop=(k == KT - 1)),
                         reads=[lh, rh], writes=[o], inc=(k == KT - 1))
                dst = LX(slice(dc, dc + n))
                T.op(ACT, lambda e: e.copy(out=dst.ap, in_=o.ap), reads=[o], writes=[dst])
            wb_lg = load_wcols(NSCAN + 128 * i)
            if i + 1 < 4:
                wb_next = load_wcols(128 * (i + 1))
            n = 2306
            uo = U(slice(2, 2 + n))
            i0 = LX(slice(1, 1 + n))
            w0 = lcol(i, slice(0, 1))
            cb = lcol(i, slice(4, 5))
            T.op(DVE, lambda e: e.tensor_scalar(out=uo.ap, in0=i0.ap, scalar1=w0.ap, scalar2=cb.ap,
                                                op0=ALU.mult, op1=ALU.add),
                 reads=[i0, w0, cb], writes=[uo])
            for k in range(1, 4):
                ik = LX(slice(1 + k, 1 + k + n))
                wk = lcol(i, slice(k, k + 1))
                T.op(DVE, lambda e: e.scalar_tensor_tensor(out=uo.ap, in0=ik.ap, scalar=wk.ap, in1=uo.ap,
                                                           op0=ALU.mult, op1=ALU.add),
                     reads=[ik, wk, uo], writes=[uo])
            if i == 0:
                K.dump(LX(), WR, "LX")
                K.dump(U(slice(2, 2 + n)), n, "U")
            ubo = Ub(slice(2, 2 + n))
            T.op(ACT, lambda e: e.copy(out=ubo.ap, in_=uo.ap), reads=[uo], writes=[ubo])
            for d in range(2):
                Hd = H0 if d == 0 else H1
                hba = lhalf(i, slice(5 + 3 * d, 6 + 3 * d))
                hbx = lhalf(i, slice(6 + 3 * d, 7 + 3 * d))
                cs = lcs(i, slice(7 + 3 * d, 8 + 3 * d))
                hcs = lhcs(i, slice(7 + 3 * d, 8 + 3 * d))
                for (hc0, n, dc) in blocks:
                    A_ = Af(slice(hc0, hc0 + n))
                    S_ = Sf(slice(hc0, hc0 + n))
                    B_ = Bf(slice(hc0, hc0 + n))
                    ub = Ub(slice(dc, dc + n))
                    uf = U(slice(dc, dc + n))
                    psa = ps_bank()(slice(0, n))
                    wa = wab((d * 2 + 0) * 4 + i)
                    T.op(PE, lambda e: e.matmul(psa.ap, lhsT=wa.ap, rhs=ub.ap, start=True, stop=True),
                         reads=[wa, ub], writes=[psa])
                    psx = ps_bank()(slice(0, n))
                    wx = wab((d * 2 + 1) * 4 + i)
                    T.op(PE, lambda e: e.matmul(psx.ap, lhsT=wx.ap, rhs=ub.ap, start=True, stop=True),
                         reads=[wx, ub], writes=[psx])
                    T.op(ACT, lambda e: e.activation(out=A_.ap, in_=psa.ap, func=AF.Tanh, scale=0.5, bias=hba.ap),
                         reads=[psa, hba], writes=[A_])
                    T.op(ACT, lambda e: e.activation(out=B_.ap, in_=psx.ap, func=AF.Tanh, scale=0.5, bias=hbx.ap),
                         reads=[psx, hbx], writes=[B_])
                    T.op(ACT, lambda e: e.activation(out=A_.ap, in_=A_.ap, func=AF.Exp, scale=hcs.ap, bias=hcs.ap),
                         reads=[A_, hcs], writes=[A_])
                    T.op(DVE, lambda e: e.tensor_tensor(out=S_.ap, in0=A_.ap, in1=A_.ap, op=ALU.mult),
                         reads=[A_], writes=[S_])
                    T.op(DVE, lambda e: e.scalar_tensor_tensor(out=B_.ap, in0=B_.ap, scalar=1.0, in1=uf.ap,
                                                               op0=ALU.add, op1=ALU.mult),
                         reads=[B_, uf], writes=[B_])
                T.op(ACT, lambda e: e.activation(out=Sf().ap, in_=Sf().ap, func=AF.Sqrt, scale=-1.0, bias=1.0),
                     reads=[Sf()], writes=[Sf()])
                order = blocks if d == 0 else [blocks[0]] + blocks[:0:-1]
                prev = None
                for (hc0, n, dc) in order:
                    A_ = Af(slice(hc0, hc0 + n))
                    S_ = Sf(slice(hc0, hc0 + n))
                    B_ = Bf(slice(hc0, hc0 + n))
                    T.op(DVE, lambda e: e.scalar_tensor_tensor(out=B_.ap, in0=B_.ap, scalar=0.5, in1=S_.ap,
                                                               op0=ALU.mult, op1=ALU.mult),
                         reads=[B_, S_], writes=[B_])
                    ho = Hd(slice(dc, dc + n))
                    rd = [A_, B_] + ([prev] if prev is not None else [])
                    init = prev.ap if prev is not None else 0.0
                    if d == 0:
                        T.op(DVE, lambda e: e.tensor_tensor_scan(out=ho.ap, data0=A_.ap, data1=B_.ap, initial=init,
                                                                 op0=ALU.mult, op1=ALU.add),
                             reads=rd, writes=[ho])
                        prev = Hd(slice(dc + n - 1, dc + n))
                    else:
                        T.op(DVE, lambda e: e.tensor_tensor_scan(out=ho.ap[:, ::-1], data0=A_.ap[:, ::-1],
                                                                 data1=B_.ap[:, ::-1], initial=init,
                                                                 op0=ALU.mult, op1=ALU.add),
                             reads=rd, writes=[ho])
                        prev = Hd(slice(dc, dc + 1))
            if i == 0:
                K.dump(H0(slice(260, 2308)), 2048, "H0")
                K.dump(H1(slice(260, 2308)), 2048, "H1")
            wb = wb_lg
            gblocks = blocks if with_ctx_out else blocks[1:]
            for gi, (hc0, n, dc) in enumerate(gblocks):
                ps = ps_bank()
                o = ps(slice(0, n))
                for k in range(KT):
                    lh = wb(k)
                    rh = hT(k, slice(hc0, hc0 + n))
                    T.op(PE, lambda e: e.matmul(o.ap, lhsT=lh.ap, rhs=rh.ap, start=(k == 0), stop=(k == KT - 1)),
                         reads=[lh, rh], writes=[o], inc=(k == KT - 1))
                t1 = Gt[gi % 2](slice(0, n))
                t2 = Bb[gi % 2](slice(0, n))
                T.op(ACT, lambda e: e.activation(out=t1.ap, in_=o.ap, func=AF.Square, scale=0.21145921592600305),
                     reads=[o], writes=[t1])
                T.op(DVE, lambda e: e.scalar_tensor_tensor(out=t1.ap, in0=t1.ap, scalar=1.0, in1=o.ap,
                                                           op0=ALU.add, op1=ALU.mult), reads=[t1, o], writes=[t1])
                T.op(ACT, lambda e: e.activation(out=t1.ap, in_=t1.ap, func=AF.Tanh, scale=0.7978845608028654),
                     reads=[t1], writes=[t1])
                T.op(DVE, lambda e: e.scalar_tensor_tensor(out=t1.ap, in0=t1.ap, scalar=1.0, in1=o.ap,
                                                           op0=ALU.add, op1=ALU.mult), reads=[t1, o], writes=[t1])
                h0 = H0(slice(dc, dc + n))
                h1 = H1(slice(dc, dc + n))
                T.op(POOL, lambda e: e.tensor_tensor(out=t2.ap, in0=h0.ap, in1=h1.ap, op=ALU.add),
                     reads=[h0, h1], writes=[t2])
                if i == 0 and hc0 == CTX:
                    K.dump(t1, n, "gelu2")
                    K.dump(t2, n, "hsum")
                co = catT(i, slice(hc0, hc0 + n))
                T.op(DVE, lambda e: e.scalar_tensor_tensor(out=co.ap, in0=t2.ap, scalar=0.5, in1=t1.ap,
                                                           op0=ALU.mult, op1=ALU.mult), reads=[t2, t1], writes=[co])
        sb.reset(m)

    stages = cfg.get("mix_stages", ("lru", "ssd"))
    if "lru" in stages:
        lru_part()
        outproj_half(0)
    if "ssd" in stages and cfg.get("ssd_fn"):
        cfg["ssd_fn"](K, l, locals())
        K.load_gates(2, l)
        outproj_half(1)
    sb.reset(m0)


NBC = 16 + 16 + 8 + 512


def ssd_decl(K, din):
    K.ssdcol_d = din("ssdcol", [128, DEPTH * 8 * 5])
    K.ssdbc_d = din("ssdbc", [DEPTH, 128, NBC])
    K.masks_d = din("masks", [128, 5 * 128])


def ssd_part(K, l, env):
    nc, sb, T = K.nc, K.sb, K.T
    PE, ACT, DVE, POOL, SP = K.PE, K.ACT, K.DVE, K.POOL, K.SP
    ident = K.ident
    ps_bank, dsem = K.ps_bank, K.dsem
    hT, catT, load_wcols, blocks = env["hT"], env["catT"], env["load_wcols"], env["blocks"]
    with_ctx_out = l < DEPTH - 1
    stat = K.stat
    NCH = 18

    m = sb.mark()
    masks = sb.alloc([5, 128], F32)
    LE, GE, GT, LT, ONES = (masks(j) for j in range(5))
    bc = sb.alloc([NBC], F32)
    scol = sb.alloc([8, 5], F32)
    a = masks()
    T.dma(SP, dsem("d_masks"), a.ap, K.masks_d.rearrange("p (j n) -> p j n", n=128), writes=[a])
    a = bc()
    T.dma(SP, dsem("d_sbc"), a.ap, K.ssdbc_d[l], writes=[a])
    a = scol()
    T.dma(SP, dsem("d_scol"), a.ap, K.ssdcol_d[:, l * 40:(l + 1) * 40].rearrange("p (t c) -> p t c", c=5), writes=[a])
    dtbias = bc(slice(0, 16))
    alog = bc(slice(16, 32))
    dsk = bc(slice(32, 40))
    nexpA = sb.alloc([16], F32)
    T.op(ACT, lambda e: e.activation(out=nexpA().ap, in_=alog.ap, func=AF.Exp), reads=[alog], writes=[nexpA()])
    T.op(DVE, lambda e: e.tensor_scalar(out=nexpA().ap, in0=nexpA().ap, scalar1=-1.0, scalar2=None, op0=ALU.mult),
         reads=[nexpA()], writes=[nexpA()])

    mp = sb.mark()
    tmpPs = [sb.alloc([SEQ], BF16) for _ in range(2)]
    for k in range(KT):
        src = hT(k, slice(CTX, CTX + SEQ))
        tp_ = tmpPs[k % 2]()
        T.op(DVE, lambda e: e.tensor_copy(out=tp_.ap.rearrange("p (w r) -> p w r", r=32),
                                          in_=src.ap.rearrange("p (r w) -> p w r", w=64)),
             reads=[src], writes=[tp_])
        T.op(ACT, lambda e: e.copy(out=src.ap, in_=tp_.ap), reads=[tp_], writes=[src])
    sb.reset(mp)

    def chunk_cols(view, row, c):
        if c < 2:
            return view(row, slice(128 * c, 128 * c + 128))
        j = c - 2
        full = view(row, slice(CTX, CTX + SEQ))
        return full.w(full.ap.rearrange("p (r w) -> p w r", w=64)[:, 4 * j:4 * j + 4, :])

    DT = sb.alloc([NCH, 16], F32)
    AA = sb.alloc([NCH, 16], F32)
    LNDT = sb.alloc([NCH, 16], F32)
    CSX = sb.alloc([NCH, 16], F32)
    EX = sb.alloc([NCH, 16], F32)
    WX = sb.alloc([NCH, 16], F32)
    ETOT = sb.alloc([NCH, 16], F32)
    wb = load_wcols(1536, 16)
    dps = ps_bank()
    for c in range(NCH):
        o = dps(slice(16 * c, 16 * c + 16))
        for k in range(KT):
            lh = hT(k, slice(128 * c, 128 * c + 128))
            rh = wb(k, slice(0, 16))
            T.op(PE, lambda e: e.matmul(o.ap, lhsT=lh.ap, rhs=rh.ap, start=(k == 0), stop=(k == KT - 1)),
                 reads=[lh, rh], writes=[o], inc=(k == KT - 1))
    dall = dps(slice(0, 16 * NCH))
    bcb = dtbias.ap.rearrange("p (o n) -> p o n", o=1).to_broadcast([128, NCH, 16])
    nab = nexpA().ap.rearrange("p (o n) -> p o n", o=1).to_broadcast([128, NCH, 16])
    T.op(DVE, lambda e: e.tensor_tensor(out=DT().ap, in0=dall.ap.rearrange("p (c n) -> p c n", n=16), in1=bcb, op=ALU.add),
         reads=[dall, dtbias], writes=[DT()])
    T.op(ACT, lambda e: e.activation(out=DT().ap, in_=DT().ap, func=AF.Exp), reads=[DT()], writes=[DT()])
    T.op(ACT, lambda e: e.activation(out=DT().ap, in_=DT().ap, func=AF.Ln, bias=1.0), reads=[DT()], writes=[DT()])
    T.op(ACT, lambda e: e.activation(out=LNDT().ap, in_=DT().ap, func=AF.Ln), reads=[DT()], writes=[LNDT()])
    T.op(DVE, lambda e: e.tensor_tensor(out=AA().ap, in0=DT().ap, in1=nab, op=ALU.mult),
         reads=[DT(), nexpA()], writes=[AA()])
    aflat = AA().w(AA().ap.rearrange("p c n -> p (c n)"))
    cps = [ps_bank() for _ in range(3)]
    for pj, msk in enumerate((LE, GE, ONES)):
        o = cps[pj](slice(0, 16 * NCH))
        T.op(PE, lambda e: e.matmul(o.ap, lhsT=msk.ap, rhs=aflat.ap, start=True, stop=True),
             reads=[msk, aflat], writes=[o])
    for d in range(2):
        src = cps[d](slice(0, 16 * NCH))
        dst = CSX(slice(0, NCH), slice(8 * d, 8 * d + 8))
        T.op(DVE, lambda e: e.tensor_copy(out=dst.ap, in_=src.ap.rearrange("p (c n) -> p c n", n=16)[:, :, 8 * d:8 * d + 8]),
             reads=[src], writes=[dst])
    tot = cps[2](slice(0, 16 * NCH))
    tot3 = tot.ap.rearrange("p (c n) -> p c n", n=16)
    T.op(ACT, lambda e: e.activation(out=ETOT().ap, in_=tot3, func=AF.Exp), reads=[tot], writes=[ETOT()])
    T.op(ACT, lambda e: e.activation(out=EX().ap, in_=CSX().ap, func=AF.Exp), reads=[CSX()], writes=[EX()])
    T.op(DVE, lambda e: e.tensor_tensor(out=WX().ap, in0=tot3, in1=CSX().ap, op=ALU.subtract),
         reads=[tot, CSX()], writes=[WX()])
    T.op(ACT, lambda e: e.activation(out=WX().ap, in_=WX().ap, func=AF.Exp), reads=[WX()], writes=[WX()])
    T.op(DVE, lambda e: e.tensor_tensor(out=WX().ap, in0=WX().ap, in1=DT().ap, op=ALU.mult),
         reads=[WX(), DT()], writes=[WX()])

    def bc4(view, c, d, g):
        a_ = view(c, slice(8 * d + 4 * g, 8 * d + 4 * g + 4))
        return a_, a_.ap.rearrange("p (h o) -> p h o", o=1).to_broadcast([128, 4, 64])

    small = sb.mark()
    for g in range(2):
        sb.reset(small)
        FT = sb.alloc([4, WR], BF16)
        ovl = sb.mark()
        CXb = sb.alloc([WR], F32)
        CU = sb.alloc([WR], F32)
        for (c0, c1) in ((0, 2), (258, 260), (2308, 2310)):
            a = CXb(slice(c0, c1))
            T.op(DVE, lambda e: e.memset(a.ap, 0.0), writes=[a])
        tile_cols = [512 + 256 * g, 512 + 256 * g + 128, 1024 + 128 * g, 1280 + 128 * g]
        wb_n = load_wcols(tile_cols[0])
        for ti, col0 in enumerate(tile_cols):
            t8 = (col0 - 512) // 128
            wb = wb_n
            if ti + 1 < 4:
                wb_n = load_wcols(tile_cols[ti + 1])
            for bi, (hc0, n, dc) in enumerate(blocks):
                ps = ps_bank()
                o = ps(slice(0, n))
                for k in range(KT):
                    lh = wb(k)
                    rh = hT(k, slice(hc0, hc0 + n))
                    T.op(PE, lambda e: e.matmul(o.ap, lhsT=lh.ap, rhs=rh.ap, start=(k == 0), stop=(k == KT - 1)),
                         reads=[lh, rh], writes=[o], inc=(k == KT - 1))
                dst = CXb(slice(dc, dc + n))
                T.op(ACT, lambda e: e.copy(out=dst.ap, in_=o.ap), reads=[o], writes=[dst])
            n = 2306
            uo = CU(slice(2, 2 + n))
            i0 = CXb(slice(1, 1 + n))
            w0 = scol(t8, slice(0, 1))
            cb = scol(t8, slice(4, 5))
            T.op(DVE, lambda e: e.tensor_scalar(out=uo.ap, in0=i0.ap, scalar1=w0.ap, scalar2=cb.ap,
                                                op0=ALU.mult, op1=ALU.add), reads=[i0, w0, cb], writes=[uo])
            for k in range(1, 4):
                ik = CXb(slice(1 + k, 1 + k + n))
                wk = scol(t8, slice(k, k + 1))
                T.op(DVE, lambda e: e.scalar_tensor_tensor(out=uo.ap, in0=ik.ap, scalar=wk.ap, in1=uo.ap,
                                                           op0=ALU.mult, op1=ALU.add), reads=[ik, wk, uo], writes=[uo])
            fo = FT(ti, slice(2, 2 + n))
            T.op(ACT, lambda e: e.activation(out=fo.ap, in_=uo.ap, func=AF.Silu), reads=[uo], writes=[fo])
        sb.reset(ovl)
        Y = sb.alloc([NCH, 256], F32)
        S = sb.alloc([256], F32)
        Sbf = [sb.alloc([256], BF16) for _ in range(2)]
        XBc = [sb.alloc([384], BF16) for _ in range(2)]
        Xs = [sb.alloc([256], BF16) for _ in range(2)]
        Xd = [sb.alloc([256], BF16) for _ in range(2)]
        CBm = sb.alloc([128], F32)
        Yt = [sb.alloc([256], F32) for _ in range(2)]
        SZ = sb.alloc([256], F32)
        YZ = sb.alloc([256], F32)
        ON = sb.alloc([256], BF16)
        jk = sb.alloc([256], BF16)
        gbase = K.gates.off
        RA = [View(sb.t32, "sb", gbase + 2048 * i, [4, 128], F32) for i in range(2)]
        Lm = View(sb.t32, "sb", gbase + 4096, [4, 128], F32)
        MT = [View(sb.tbf, "sb", gbase + 6144 + 1024 * i, [4, 128], BF16) for i in range(2)]
        wtb = env["wt_bufs"]
        wz = View(sb.tbf, "sb", wtb[0].off, [KT, 256], BF16)
        assert wtb[1].off == wtb[0].off + 2048
        a = wz()
        T.dma(POOL, dsem("d_wz"), a.ap,
              K.w_in_d[l, :, 2064 + 256 * g:2064 + 256 * g + 256].rearrange("(kt p) n -> p kt n", p=128), writes=[a])
        normg = bc(slice(40 + 256 * g, 40 + 256 * g + 256))

        def pad_cols(c):
            return (2 + 128 * c) if c < 2 else (260 + 128 * (c - 2))

        DSKM = sb.alloc([8, 128], BF16)
        idf = Lm(0)
        dtmp = Lm(1)
        T.op(DVE, lambda e: e.tensor_tensor(out=idf.ap, in0=LE.ap, in1=GE.ap, op=ALU.mult), reads=[LE, GE], writes=[idf])
        for h in range(4):
            dcol = bc(slice(32 + 4 * g + h, 32 + 4 * g + h + 1))
            hi = DSKM(2 * h)
            lo = DSKM(2 * h + 1)
            T.op(ACT, lambda e: e.activation(out=hi.ap, in_=idf.ap, func=AF.Copy, scale=dcol.ap),
                 reads=[idf, dcol], writes=[hi])
            T.op(DVE, lambda e: e.scalar_tensor_tensor(out=dtmp.ap, in0=idf.ap, scalar=dcol.ap, in1=hi.ap,
                                                       op0=ALU.mult, op1=ALU.subtract),
                 reads=[idf, dcol, hi], writes=[dtmp])
            T.op(DVE, lambda e: e.tensor_copy(out=lo.ap, in_=dtmp.ap), reads=[dtmp], writes=[lo])

        def h4(acc_):
            return acc_.ap.rearrange("p (h q) -> p h q", q=64)

        for d in range(2):
            T.op(DVE, lambda e: e.memset(S().ap, 0.0), writes=[S()])
            T.op(DVE, lambda e: e.memset(Sbf[0]().ap, 0.0), writes=[Sbf[0]()])
            order = list(range(NCH)) if d == 0 else [1, 0] + list(range(NCH - 1, 1, -1))
            st = {}

            def stageA(i):
                c = order[i]
                want_y = with_ctx_out or c >= 2
                pc = pad_cols(c)
                xb = XBc[i % 2]
                tp = ps_bank(bf=True)
                for ti in range(3):
                    o = tp(slice(128 * ti, 128 * ti + 128))
                    i_ = FT(ti, slice(pc, pc + 128))
                    T.op(PE, lambda e: e.transpose(out=o.ap, in_=i_.ap, identity=ident().ap),
                         reads=[i_, ident()], writes=[o], inc=(ti == 2))
                tpa = tp(slice(0, 384))
                T.op(ACT, lambda e: e.copy(out=xb().ap, in_=tpa.ap), reads=[tpa], writes=[xb()])
                xtok = xb(slice(0, 256))
                btok = xb(slice(256, 384))
                r = {"c": c, "want_y": want_y, "pc": pc, "xtok": xtok}
                if want_y:
                    ct = FT(3, slice(pc, pc + 128))
                    bt = FT(2, slice(pc, pc + 128))
                    cbp = K.PSB[4 + 2 * (i % 2)](slice(0, 128))
                    T.op(PE, lambda e: e.matmul(cbp.ap, lhsT=bt.ap, rhs=ct.ap, start=True, stop=True),
                         reads=[bt, ct], writes=[cbp])
                    ra = RA[i % 2]()
                    dp = K.PSB[5 + 2 * (i % 2)](slice(0, 512))
                    smask = GT if d == 0 else LT
                    T.op(PE, lambda e: e.matmul(dp.ap, lhsT=smask.ap, rhs=ra.ap.rearrange("p h n -> p (h n)"),
                                                start=True, stop=True), reads=[smask, ra], writes=[dp])
                    xd = Xd[i % 2]()
                    da, db = bc4(DT, c, d, g)
                    T.op(POOL, lambda e: e.tensor_tensor(out=h4(xd), in0=h4(xtok), in1=db, op=ALU.mult),
                         reads=[xtok, da], writes=[xd])
                    r.update(ct=ct, cbp=cbp, dp=dp, xd=xd)
                xs_ = Xs[i % 2]()
                wa, wbq = bc4(WX, c, d, g)
                T.op(POOL, lambda e: e.tensor_tensor(out=h4(xs_), in0=h4(xtok), in1=wbq, op=ALU.mult),
                     reads=[xtok, wa], writes=[xs_])
                stp = K.PSB[4 + 2 * (i % 2)](slice(128, 384))
                T.op(PE, lambda e: e.matmul(stp.ap, lhsT=btok.ap, rhs=xs_.ap, start=True, stop=True),
                     reads=[btok, xs_], writes=[stp])
                r["stp"] = stp
                st[i] = r

            def emit_RA(i):
                c = order[i]
                if not (with_ctx_out or c >= 2):
                    return
                rmask = LE if d == 0 else GE
                for h in range(4):
                    ac1 = AA(c, slice(8 * d + 4 * g + h, 8 * d + 4 * g + h + 1))
                    rah = RA[i % 2](h)
                    T.op(ACT, lambda e: e.activation(out=rah.ap, in_=rmask.ap, func=AF.Copy, scale=ac1.ap),
                         reads=[rmask, ac1], writes=[rah])

            def stageB(i):
                r = st[i]
                if not r["want_y"]:
                    return
                dp, cbp, xd = r["dp"], r["cbp"], r["xd"]
                T.op(ACT, lambda e: e.activation(out=Lm().ap.rearrange("p h n -> p (h n)"), in_=dp.ap, func=AF.Exp),
                     reads=[dp], writes=[Lm()])
                cmask = LE if d == 0 else GE
                T.op(DVE, lambda e: e.tensor_tensor(out=CBm().ap, in0=cbp.ap, in1=cmask.ap, op=ALU.mult),
                     reads=[cbp, cmask], writes=[CBm()])
                mt = MT[i % 2]
                T.op(DVE, lambda e: e.tensor_tensor(
                    out=mt().ap, in0=Lm().ap,
                    in1=CBm().ap.rearrange("p (o n) -> p o n", o=1).to_broadcast([128, 4, 128]), op=ALU.mult),
                    reads=[Lm(), CBm()], writes=[mt()])
                yd = ps_bank()
                xtok = r["xtok"]
                for h in range(4):
                    o = yd(slice(64 * h, 64 * h + 64))
                    lh = mt(h)
                    rh = Acc(xd.ap[:, 64 * h:64 * h + 64], "sb", xd.ranges)
                    T.op(PE, lambda e: e.matmul(o.ap, lhsT=lh.ap, rhs=rh.ap, start=True, stop=(d == 1)),
                         reads=[lh, rh], writes=[o], inc=(h == 3 and d == 1))
                    if d == 0:
                        xr = Acc(xtok.ap[:, 64 * h:64 * h + 64], "sb", xtok.ranges)
                        for part in range(2):
                            dm = DSKM(2 * h + part)
                            T.op(PE, lambda e: e.matmul(o.ap, lhsT=dm.ap, rhs=xr.ap, start=False, stop=(part == 1)),
                                 reads=[dm, xr], writes=[o], inc=(h == 3 and part == 1))
                r["yd"] = yd(slice(0, 256))

            def stageC(i):
                r = st.pop(i)
                c = r["c"]
                sb_in = Sbf[i % 2]()
                if r["want_y"]:
                    yo = ps_bank()(slice(0, 256))
                    ct = r["ct"]
                    T.op(PE, lambda e: e.matmul(yo.ap, lhsT=ct.ap, rhs=sb_in.ap, start=True, stop=True),
                         reads=[ct, sb_in], writes=[yo])
                stp = r["stp"]
                ta, tb = bc4(ETOT, c, d, g)
                T.op(DVE, lambda e: e.tensor_tensor(out=h4(S()), in0=h4(S()), in1=tb, op=ALU.mult),
                     reads=[S(), ta], writes=[S()])
                T.op(DVE, lambda e: e.tensor_tensor(out=S().ap, in0=S().ap, in1=stp.ap, op=ALU.add),
                     reads=[S(), stp], writes=[S()])
                sb_out = Sbf[(i + 1) % 2]()
                T.op(DVE, lambda e: e.tensor_copy(out=sb_out.ap, in_=S().ap), reads=[S()], writes=[sb_out])
                if r["want_y"]:
                    yt = Yt[i % 2]()
                    ea, eb = bc4(EX, c, d, g)
                    T.op(DVE, lambda e: e.tensor_tensor(out=h4(yt), in0=h4(yo), in1=eb, op=ALU.mult),
                         reads=[yo, ea], writes=[yt])
                    yda = r["yd"]
                    yc = Y(c)
                    if d == 0:
                        T.op(DVE, lambda e: e.tensor_tensor(out=yc.ap, in0=yt.ap, in1=yda.ap, op=ALU.add),
                             reads=[yt, yda], writes=[yc])
                    else:
                        T.op(DVE, lambda e: e.tensor_tensor(out=yt.ap, in0=yt.ap, in1=yda.ap, op=ALU.add),
                             reads=[yt, yda], writes=[yt])
                        T.op(POOL, lambda e: e.tensor_tensor(out=yc.ap, in0=yc.ap, in1=yt.ap, op=ALU.add),
                             reads=[yc, yt], writes=[yc])

            n_it = len(order)
            K.ps_lim[0] = 4
            if K.ps_rr[0] >= 4:
                K.ps_rr[0] = 0
            emit_RA(0)
            stageA(0)
            if n_it > 1:
                emit_RA(1)
            for i in range(n_it):
                if i + 1 < n_it:
                    stageA(i + 1)
                stageB(i)
                stageC(i)
                if i + 2 < n_it:
                    emit_RA(i + 2)
            K.ps_lim[0] = 8

        fin = [c for c in range(NCH) if (with_ctx_out or c >= 2)]
        ssq = sb.alloc([NCH], F32)
        SZ2 = [SZ, YZ]
        zst = {}

        def f1_head(fi_):
            c = fin[fi_]
            zp = ps_bank()(slice(0, 256))
            for k in range(KT):
                lh = hT(k, slice(128 * c, 128 * c + 128))
                rh = wz(k)
                T.op(PE, lambda e: e.matmul(zp.ap, lhsT=lh.ap, rhs=rh.ap, start=(k == 0), stop=(k == KT - 1)),
                     reads=[lh, rh], writes=[zp], inc=(k == KT - 1))
            sz = SZ2[fi_ % 2]()
            T.op(ACT, lambda e: e.activation(out=sz.ap, in_=zp.ap, func=AF.Tanh, scale=0.5), reads=[zp], writes=[sz])
            zst[fi_] = (zp, sz)

        def f1_tail(fi_):
            c = fin[fi_]
            zp, sz = zst.pop(fi_)
            yc = Y(c)
            T.op(DVE, lambda e: e.scalar_tensor_tensor(out=sz.ap, in0=sz.ap, scalar=1.0, in1=zp.ap,
                                                       op0=ALU.add, op1=ALU.mult), reads=[sz, zp], writes=[sz])
            T.op(DVE, lambda e: e.scalar_tensor_tensor(out=yc.ap, in0=yc.ap, scalar=0.5, in1=sz.ap,
                                                       op0=ALU.mult, op1=ALU.mult), reads=[yc, sz], writes=[yc])
            ss = ssq(slice(c, c + 1))
            T.op(ACT, lambda e: e.activation(out=jk().ap, in_=yc.ap, func=AF.Square, accum_out=ss.ap),
                 reads=[yc], writes=[jk(), ss])

        f1_head(0)
        for fi_ in range(len(fin)):
            if fi_ + 1 < len(fin):
                f1_head(fi_ + 1)
            f1_tail(fi_)
        c_lo, c_hi = fin[0], fin[-1] + 1
        rsq = ssq(slice(c_lo, c_hi))
        T.op(ACT, lambda e: e.activation(out=rsq.ap, in_=rsq.ap, func=AF.Sqrt, scale=1.0 / 256, bias=EPS),
             reads=[rsq], writes=[rsq])
        T.op(DVE, lambda e: e.reciprocal(out=rsq.ap, in_=rsq.ap), reads=[rsq], writes=[rsq])
        ON2 = [ON, jk]

        def f3_head(fi_):
            c = fin[fi_]
            yc = Y(c)
            rs = ssq(slice(c, c + 1))
            on = ON2[fi_ % 2]
            T.op(DVE, lambda e: e.scalar_tensor_tensor(out=on().ap, in0=yc.ap, scalar=rs.ap, in1=normg.ap,
                                                       op0=ALU.mult, op1=ALU.mult),
                 reads=[yc, rs, normg], writes=[on()])

        f3_head(0)
        for fi_, c in enumerate(fin):
            on = ON2[fi_ % 2]
            if fi_ + 1 < len(fin):
                f3_head(fi_ + 1)
            tps = [ps_bank(bf=True), ps_bank(bf=True)]
            for ti in range(2):
                o = tps[ti](slice(0, 128))
                i_ = on(slice(128 * ti, 128 * ti + 128))
                T.op(PE, lambda e: e.transpose(out=o.ap, in_=i_.ap, identity=ident().ap),
                     reads=[i_, ident()], writes=[o])
            for ti in range(2):
                src = tps[ti](slice(0, 128))
                dst = chunk_cols(catT, 2 * g + ti, c)
                sap = src.ap if c < 2 else src.ap.rearrange("p (w r) -> p w r", r=32)
                if ti == 0:
                    T.op(ACT, lambda e: e.copy(out=dst.ap, in_=sap), reads=[src], writes=[dst])
                else:
                    T.op(DVE, lambda e: e.tensor_copy(out=dst.ap, in_=sap), reads=[src], writes=[dst])
    sb.reset(m)


def make_in_maps(inputs, cores):
    f = np.float32
    ident = np.eye(128, dtype=f)
    ngcol = np.zeros((128, DEPTH * 2 * KT), f)
    for l in range(DEPTH):
        ngcol[:, (l * 2) * KT:(l * 2 + 1) * KT] = np.asarray(inputs["norm1_g"][l], f).reshape(KT, 128).T
        ngcol[:, (l * 2 + 1) * KT:(l * 2 + 2) * KT] = np.asarray(inputs["norm2_g"][l], f).reshape(KT, 128).T
    fg_bc = np.ascontiguousarray(np.broadcast_to(np.asarray(inputs["final_g"], f)[None, :], (128, D)))
    shared = {
        "ada_w": np.ascontiguousarray(inputs["ada_w"], f), "ada_b": np.ascontiguousarray(inputs["ada_b"], f),
        "ngcol": ngcol, "w_in": np.ascontiguousarray(inputs["w_in"], f),
        "w_out": np.ascontiguousarray(inputs["w_out"], f), "mlp_w1": np.ascontiguousarray(inputs["mlp_w1"], f),
        "mlp_w2": np.ascontiguousarray(inputs["mlp_w2"], f), "final_g_bc": fg_bc, "ident": ident,
    }
    lrucol = np.zeros((128, DEPTH * 4 * NLC), f)
    wab = np.zeros((DEPTH * 16, 128, 128), f)
    for l in range(DEPTH):
        for i in range(4):
            base = (l * 4 + i) * NLC
            ch = slice(i * 128, (i + 1) * 128)
            for k in range(4):
                lrucol[:, base + k] = inputs["lru_conv_w"][l, k, ch]
            lrucol[:, base + 4] = inputs["lru_conv_b"][l, ch]
            for d in range(2):
                lrucol[:, base + 5 + 3 * d] = inputs["lru_ba"][l, d, ch]
                lrucol[:, base + 6 + 3 * d] = inputs["lru_bx"][l, d, ch]
                lrucol[:, base + 7 + 3 * d] = inputs["lru_lambda"][l, d, ch]
                for which, nm in enumerate(("lru_wa", "lru_wx")):
                    mtx = wab[l * 16 + (d * 2 + which) * 4 + i]
                    for hl in range(2):
                        mtx[hl * 64:(hl + 1) * 64, hl * 64:(hl + 1) * 64] = inputs[nm][l, d, 2 * i + hl]
    shared["lrucol"] = lrucol
    shared["wab"] = wab
    ssdcol = np.zeros((128, DEPTH * 8 * 5), f)
    ssdbc = np.zeros((DEPTH, 128, NBC), f)
    for l in range(DEPTH):
        for t8 in range(8):
            ch = slice(t8 * 128, (t8 + 1) * 128)
            for k in range(4):
                ssdcol[:, (l * 8 + t8) * 5 + k] = inputs["ssd_conv_w"][l, k, ch]
            ssdcol[:, (l * 8 + t8) * 5 + 4] = inputs["ssd_conv_b"][l, ch]
        ssdbc[l, :, 0:16] = np.asarray(inputs["ssd_dt_bias"][l], f).reshape(1, 16)
        ssdbc[l, :, 16:32] = np.asarray(inputs["ssd_a_log"][l], f).reshape(1, 16)
        ssdbc[l, :, 32:40] = np.asarray(inputs["ssd_d"][l], f).reshape(1, 8)
        ssdbc[l, :, 40:552] = np.asarray(inputs["ssd_norm_g"][l], f).reshape(1, 512)
    ia = np.arange(128)[:, None]
    ib = np.arange(128)[None, :]
    masks = np.concatenate([(ia <= ib), (ia >= ib), (ia > ib), (ia < ib), np.ones((128, 128), bool)], axis=1).astype(f)
    shared["ssdcol"] = ssdcol
    shared["ssdbc"] = ssdbc
    shared["masks"] = masks
    maps = []
    for b in cores:
        ccol = np.zeros((128, 2 * KT), f)
        ccol[:, 0:KT] = np.asarray(inputs["c"][b], f).reshape(KT, 128).T
        ccol[:, KT:2 * KT] = np.asarray(inputs["c_ctx"], f).reshape(KT, 128).T
        mp = dict(shared)
        mp["x"] = np.ascontiguousarray(inputs["x"][b], f)
        mp["ctx"] = np.ascontiguousarray(inputs["ctx"][b], f)
        mp["ccol"] = ccol
        maps.append(mp)
    return maps


DEFAULT_CFG = {"mixer_decl": mixer_decl, "mixer_fn": mixer_phase, "ssd_decl": ssd_decl, "ssd_fn": ssd_part}


def kernel(**inputs):
    nc = build_nc(DEFAULT_CFG)
    cores = list(range(8))
    maps = make_in_maps(inputs, cores)
    res = run_bass_kernel_spmd(nc, maps, core_ids=cores)
    return np.stack([np.asarray(r["out"], np.float32) for r in res.results], axis=0)
```

```python
import numpy as np
import concourse.bass as bass
import concourse.mybir as mybir
from concourse.bass_utils import run_bass_kernel_spmd

F32 = mybir.dt.float32
BF16 = mybir.dt.bfloat16
AF = mybir.ActivationFunctionType
ALU = mybir.AluOpType
AX = mybir.AxisListType

D = 1024
SEQ = 2048
CTX = 256
DEPTH = 2
DFF = 4096
KT = D // 128
NIN = 2576
NSCAN = 1552
EPS = 1e-6
GRAN = 64


class Acc:
    __slots__ = ("ap", "space", "ranges")

    def __init__(self, ap, space, ranges):
        self.ap = ap
        self.space = space
        self.ranges = ranges

    def w(self, ap):
        return Acc(ap, self.space, self.ranges)


class View:
    def __init__(self, base_ap, space, byte_off, shape, dtype, base_off=0):
        self.space = space
        self.off = byte_off
        self.shape = tuple(shape)
        self.dtype = dtype
        self.sz = 2 if dtype == BF16 else 4
        n = int(np.prod(shape))
        e0 = (byte_off - base_off) // self.sz
        ap = base_ap[:, e0:e0 + n]
        if len(shape) == 2:
            ap = ap.rearrange("p (a b) -> p a b", b=shape[1])
        elif len(shape) == 3:
            ap = ap.rearrange("p (a b c) -> p a b c", b=shape[1], c=shape[2])
        self.ap = ap
        self.nbytes = n * self.sz

    def __call__(self, *idx, p=None):
        idx = list(idx) + [slice(None)] * (len(self.shape) - len(idx))
        key = [slice(None) if p is None else p]
        los, his = [], []
        for i, d in zip(idx, self.shape):
            if isinstance(i, int):
                los.append(i)
                his.append(i + 1)
                key.append(i)
            else:
                lo = 0 if i.start is None else i.start
                hi = d if i.stop is None else i.stop
                assert 0 <= lo < hi <= d, (i, d, self.shape)
                los.append(lo)
                his.append(hi)
                key.append(slice(lo, hi))
        ap = self.ap[tuple(key)]
        strides = []
        s = self.sz
        for d in reversed(self.shape):
            strides.append(s)
            s *= d
        strides = strides[::-1]
        nd = len(self.shape)
        outer = 1
        for k in range(nd - 1):
            outer *= his[k] - los[k]
        if outer > 48:
            lo_b = self.off + sum(l * st for l, st in zip(los, strides))
            hi_b = self.off + sum((h - 1) * st for h, st in zip(his, strides)) + self.sz
            ranges = [(lo_b, hi_b)]
        else:
            ranges = []
            import itertools
            for combo in itertools.product(*[range(los[k], his[k]) for k in range(nd - 1)]):
                b = self.off + sum(c * st for c, st in zip(combo, strides[:-1]))
                ranges.append((b + los[-1] * strides[-1], b + his[-1] * strides[-1]))
        return Acc(ap, self.space, ranges)


class Eng:
    def __init__(self, nc, eng, name, is_pe=False):
        self.eng = eng
        self.name = name
        self.is_pe = is_pe
        self.sem = nc.alloc_semaphore("s_" + name)
        self.count = 0
        self.seen = {}


class DSem:
    def __init__(self, nc, name):
        self.sem = nc.alloc_semaphore(name)
        self.count = 0
        self.name = name


class Tracker:
    def __init__(self, nc):
        self.nc = nc
        self.lastw = {"sb": {}, "ps": {}, "dr": {}}
        self.readers = {"sb": {}, "ps": {}, "dr": {}}
        self.n_wait = 0
        self.n_inst = 0

    @staticmethod
    def _grans(ranges, space="sb"):
        gran = 2048 if space == "ps" else GRAN
        for lo, hi in ranges:
            for g in range(lo // gran, (hi - 1) // gran + 1):
                yield g

    def _collect(self, reads, writes, E=None):
        need = {}
        for a in reads:
            lw = self.lastw[a.space]
            for g in self._grans(a.ranges, a.space):
                w = lw.get(g)
                if w is not None and need.get(w[0], 0) < w[1]:
                    need[w[0]] = w[1]
            if a.space == "ps":
                rd = self.readers["ps"]
                for g in self._grans(a.ranges, a.space):
                    r = rd.get(g)
                    if r:
                        for k, v in r.items():
                            if k is not E and need.get(k, 0) < v:
                                need[k] = v
        for a in writes:
            lw = self.lastw[a.space]
            rd = self.readers[a.space]
            for g in self._grans(a.ranges, a.space):
                w = lw.get(g)
                if w is not None and need.get(w[0], 0) < w[1]:
                    need[w[0]] = w[1]
                r = rd.get(g)
                if r:
                    for k, v in r.items():
                        if need.get(k, 0) < v:
                            need[k] = v
        return need

    def _emit_waits(self, E, need):
        for semobj, val in need.items():
            if semobj is E and E.is_pe:
                continue
            if E.seen.get(semobj, 0) >= val:
                continue
            h = semobj.sem
            E.eng.wait_ge(h, val)
            E.seen[semobj] = val
            self.n_wait += 1

    def _record(self, tag, reads, writes):
        for a in writes:
            lw = self.lastw[a.space]
            rd = self.readers[a.space]
            for g in self._grans(a.ranges, a.space):
                lw[g] = tag
                if g in rd:
                    del rd[g]
        for a in reads:
            rd = self.readers[a.space]
            for g in self._grans(a.ranges, a.space):
                r = rd.get(g)
                if r is None:
                    rd[g] = {tag[0]: tag[1]}
                elif r.get(tag[0], 0) < tag[1]:
                    r[tag[0]] = tag[1]

    def op(self, E, fn, reads=(), writes=(), inc=True):
        need = self._collect(reads, writes, E)
        self._emit_waits(E, need)
        inst = fn(E.eng)
        self.n_inst += 1
        if inc:
            E.count += 1
            inst.then_inc(E.sem, 1)
            tag = (E, E.count)
        else:
            tag = (E, E.count + 1)
        self._record(tag, reads, writes)
        return inst

    def dma(self, Q, dsem, out_ap, in_ap, reads=(), writes=(), **kw):
        need = self._collect(reads, writes)
        self._emit_waits(Q, need)
        inst = Q.eng.dma_start(out=out_ap, in_=in_ap, **kw)
        dsem.count += 16
        inst.then_inc(dsem.sem, 16)
        self.n_inst += 1
        tag = (dsem, dsem.count)
        self._record(tag, reads, writes)
        return inst


class Bump:
    def __init__(self, t32, tbf, space, limit):
        self.t32 = t32
        self.tbf = tbf
        self.space = space
        self.ptr = 0
        self.limit = limit
        self.peak = 0

    def alloc(self, shape, dtype, align=64):
        sz = 2 if dtype == BF16 else 4
        n = int(np.prod(shape)) * sz
        self.ptr = (self.ptr + align - 1) // align * align
        v = View(self.t32 if dtype == F32 else self.tbf, self.space, self.ptr, shape, dtype)
        self.ptr += n
        self.peak = max(self.peak, self.ptr)
        assert self.ptr <= self.limit, ("SBUF arena overflow", self.ptr, self.limit)
        return v

    def mark(self):
        return self.ptr

    def reset(self, m):
        self.ptr = m


class Ctx:
    pass


def build_nc(cfg=None):
    cfg = dict(cfg or {})
    n_layers = cfg.get("n_layers", DEPTH)
    K = Ctx()
    K.cfg = cfg

    nc = bass.Bass("TRN2", target_bir_lowering=False)
    K.nc = nc

    def din(name, shape):
        return nc.dram_tensor(name, list(shape), F32, kind="ExternalInput").ap()

    K.x_d = din("x", [SEQ, D])
    K.ctx_d = din("ctx", [CTX, D])
    K.ccol_d = din("ccol", [128, 2 * KT])
    K.ada_w_d = din("ada_w", [DEPTH, D, 6 * D])
    K.ada_b_d = din("ada_b", [DEPTH, 6 * D])
    K.ngcol_d = din("ngcol", [128, DEPTH * 2 * KT])
    K.w_in_d = din("w_in", [DEPTH, D, NIN])
    K.w_out_d = din("w_out", [DEPTH, D, D])
    K.w1_d = din("mlp_w1", [DEPTH, D, DFF])
    K.w2_d = din("mlp_w2", [DEPTH, DFF, D])
    K.fg_d = din("final_g_bc", [128, D])
    K.ident_d = din("ident", [128, 128])
    if cfg.get("mixer_decl"):
        cfg["mixer_decl"](K, din)
    K.out_d = nc.dram_tensor("out", [SEQ, D], F32, kind="ExternalOutput").ap()
    K.dbg_n = [0]
    if cfg.get("dump"):
        K.dbg_d = nc.dram_tensor("dbg", [24, 128, 2560], F32, kind="ExternalOutput").ap()

    def dump(acc, n, name=""):
        if not cfg.get("dump") or K.dbg_n[0] >= 24:
            return
        j = K.dbg_n[0]
        K.dbg_n[0] += 1
        print("dump slot", j, name, n)
        if acc.ap.dtype != F32:
            return
        T.dma(K.SP, K.dsem("d_dbg"), K.dbg_d[j, :, 0:n], acc.ap, reads=[acc])

    K.dump = dump
    K.modrows_d = nc.dram_tensor("modrows", [2, 6 * D], F32, kind="Internal").ap()

    ARENA = 212800 // 64 * 64
    arena = nc.alloc_sbuf_tensor("arena", [128, ARENA // 4], F32)
    sb = Bump(arena, arena.bitcast(BF16), "sb", ARENA)
    K.sb = sb
    psum_t = [nc.alloc_psum_tensor("ps%d" % i, [128, 1024], F32) for i in range(4)]
    PSP = []
    for i, t in enumerate(psum_t):
        PSP.append((View(t, "ps", i * 4096, [1024], F32, i * 4096), View(t.bitcast(BF16), "ps", i * 4096, [2048], BF16, i * 4096)))
    PSB = []
    PSBb = []
    for i, t in enumerate(psum_t):
        for h in range(2):
            PSB.append(View(t, "ps", i * 4096 + h * 2048, [512], F32, i * 4096))
            PSBb.append(View(t.bitcast(BF16), "ps", i * 4096 + h * 2048, [1024], BF16, i * 4096))

    T = Tracker(nc)
    K.T = T
    PE = Eng(nc, nc.tensor, "pe", is_pe=True)
    ACT = Eng(nc, nc.scalar, "act")
    DVE = Eng(nc, nc.vector, "dve")
    POOL = Eng(nc, nc.gpsimd, "pool")
    SP = Eng(nc, nc.sync, "sp")
    K.PE, K.ACT, K.DVE, K.POOL, K.SP = PE, ACT, DVE, POOL, SP

    ps_rr = [0]
    ps_lim = [8]

    def ps_pair():
        b = (ps_rr[0] + 1) // 2 * 2
        if b + 2 > ps_lim[0]:
            b = 0
        ps_rr[0] = (b + 2) % ps_lim[0]
        return PSP[b // 2]

    def ps_bank(bf=False):
        b = ps_rr[0]
        ps_rr[0] = (b + 1) % ps_lim[0]
        return PSBb[b] if bf else PSB[b]

    K.ps_pair, K.ps_bank, K.ps_lim, K.PSB, K.PSP, K.PSBb, K.ps_rr = ps_pair, ps_bank, ps_lim, PSB, PSP, PSBb, ps_rr

    dsems = {}

    def dsem(name):
        if name not in dsems:
            dsems[name] = DSem(nc, name)
        return dsems[name]

    K.dsem = dsem

    xs = sb.alloc([16, D], F32)
    cx = sb.alloc([2, D], F32)
    ident = sb.alloc([128], BF16)
    gates = sb.alloc([2, D], F32)
    ccol = sb.alloc([2 * KT], F32)
    ngcol = sb.alloc([DEPTH * 2 * KT], F32)
    Sst = sb.alloc([KT, 64], BF16)
    modcol = sb.alloc([2, 48], F32)
    Gsh = sb.alloc([2, 4, KT], F32)
    stat = sb.alloc([64], F32)
    ones_f = sb.alloc([128], F32)
    K.xs, K.cx, K.ident, K.gates, K.stat, K.ones_f, K.Gsh, K.modcol = xs, cx, ident, gates, stat, ones_f, Gsh, modcol

    dsem_ld = dsem("d_ld")
    dsem_misc = dsem("d_misc")
    dsem_out = dsem("d_out")

    def seal(ds, accs):
        for a in accs:
            T._record((ds, ds.count), [], [a])

    K.seal = seal

    for t in range(16):
        a = xs(t)
        T.dma(SP, dsem_ld, a.ap, K.x_d[t * 128:(t + 1) * 128, :], writes=[a])
    for t in range(2):
        a = cx(t)
        T.dma(SP, dsem_ld, a.ap, K.ctx_d[t * 128:(t + 1) * 128, :], writes=[a])
    seal(dsem_ld, [xs(t) for t in range(16)] + [cx(t) for t in range(2)])

    a = ccol()
    T.dma(SP, dsem("d_cc"), a.ap, K.ccol_d, writes=[a])
    a = ngcol()
    T.dma(SP, dsem("d_cc"), a.ap, K.ngcol_d, writes=[a])
    seal(dsem("d_cc"), [ccol(), ngcol()])
    a = ident()
    T.dma(POOL, dsem("d_ident"), a.ap, K.ident_d, writes=[a])

    a = ones_f()
    T.op(DVE, lambda e: e.memset(a.ap, 1.0), writes=[a])
    a = Sst()
    T.op(DVE, lambda e: e.memset(a.ap, 0.0), writes=[a])
    silu_tmp = stat(slice(32, 48))
    T.op(ACT, lambda e: e.activation(out=silu_tmp.ap, in_=ccol().ap, func=AF.Silu), reads=[ccol()], writes=[silu_tmp])
    for j in range(2):
        dst = Sst(slice(0, KT), slice(32 * j, 32 * j + 1))
        src = stat(slice(32 + KT * j, 32 + KT * (j + 1)))
        T.op(DVE, lambda e: e.tensor_copy(out=dst.ap, in_=src.ap.rearrange("p (k o) -> p k o", o=1)),
             reads=[src], writes=[dst])

    def dr_acc(r, c0, c1):
        return Acc(K.modrows_d[r:r + 1, c0:c1], "dr", [((r * 6 * D + c0) * 4, (r * 6 * D + c1) * 4)])

    def load_gates(mi, l):
        for r in range(2):
            if r == 1 and l == DEPTH - 1:
                continue
            src = dr_acc(r, mi * D, (mi + 1) * D)
            dst = gates(r)
            T.dma(SP, dsem("d_gate%d" % r), dst.ap, src.ap.partition_broadcast(128), reads=[src], writes=[dst])

    K.load_gates = load_gates

    def mod_begin(l):
        st = {"l": l}
        st["wbuf"] = [sb.alloc([KT, 512], BF16) for _ in range(2)]
        st["adab"] = [sb.alloc([512], F32) for _ in range(2)]
        st["rowbs"] = [sb.alloc([2, 512], F32) for _ in range(2)]
        st["colps"] = PSP[3][0]
        st["next"] = 0
        ps_lim[0] = 6
        if ps_rr[0] >= 6:
            ps_rr[0] = 0
        return st

    def mod_block(st):
        j = st["next"]
        if j >= 12:
            return
        st["next"] = j + 1
        l = st["l"]
        colps = st["colps"]
        wb = st["wbuf"][j % 2]
        rowb = st["rowbs"][j % 2]
        ab = st["adab"][j % 2]
        a = wb()
        T.dma(POOL, dsem("d_ada%d" % (j % 2)), a.ap,
              K.ada_w_d[l, :, j * 512:(j + 1) * 512].rearrange("(kt p) n -> p kt n", p=128), writes=[a])
        for prt in (0, 32):
            a = ab(p=slice(prt, prt + 1))
            T.dma(SP, dsem("d_adab%d_%d" % (j % 2, prt)), a.ap, K.ada_b_d[l:l + 1, j * 512:(j + 1) * 512], writes=[a])
        ps = ps_bank()
        o = ps(slice(0, 512), p=slice(0, 33))
        for k in range(KT):
            lh = Sst(k, slice(0, 33))
            rh = wb(k)
            T.op(PE, lambda e: e.matmul(o.ap, lhsT=lh.ap, rhs=rh.ap, start=(k == 0), stop=(k == KT - 1)),
                 reads=[lh, rh], writes=[o], inc=(k == KT - 1))
        for r, prt in enumerate((0, 32)):
            dst = rowb(r, p=slice(prt, prt + 1))
            i0_ = ps(slice(0, 512), p=slice(prt, prt + 1))
            i1_ = ab(p=slice(prt, prt + 1))
            T.op(DVE, lambda e: e.tensor_tensor(out=dst.ap, in0=i0_.ap, in1=i1_.ap, op=ALU.add),
                 reads=[i0_, i1_], writes=[dst])
            dd = dr_acc(r, j * 512, (j + 1) * 512)
            T.dma(SP, dsem("d_mrow%d_%d" % (r, j % 2)), dd.ap, dst.ap, reads=[dst], writes=[dd])
            for q in range(4):
                ft = j * 4 + q
                oc = colps(slice(64 * r + ft, 64 * r + ft + 1))
                lh = rowb(r, slice(q * 128, (q + 1) * 128), p=slice(prt, prt + 1))
                rh = ones_f(slice(0, 1), p=slice(prt, prt + 1))
                T.op(PE, lambda e: e.matmul(oc.ap, lhsT=lh.ap, rhs=rh.ap, start=True, stop=True),
                     reads=[lh, rh], writes=[oc])

    def mod_end(st):
        while st["next"] < 12:
            mod_block(st)
        ps_lim[0] = 8
        l = st["l"]
        colps = st["colps"]
        for r in range(2):
            for n_i, (mi_sh, mi_sc) in enumerate(((0, 1), (3, 4))):
                g = ngcol(slice((l * 2 + n_i) * KT, (l * 2 + n_i + 1) * KT))
                sc = colps(slice(64 * r + mi_sc * KT, 64 * r + (mi_sc + 1) * KT))
                sh = colps(slice(64 * r + mi_sh * KT, 64 * r + (mi_sh + 1) * KT))
                Gd = Gsh(r, 2 * n_i)
                shd = Gsh(r, 2 * n_i + 1)
                T.op(DVE, lambda e: e.scalar_tensor_tensor(out=Gd.ap, in0=sc.ap, scalar=1.0, in1=g.ap,
                                                           op0=ALU.add, op1=ALU.mult),
                     reads=[sc, g], writes=[Gd])
                T.op(DVE, lambda e: e.tensor_copy(out=shd.ap, in_=sh.ap), reads=[sh], writes=[shd])

    def compute_mod(l):
        m = sb.mark()
        st = mod_begin(l)
        mod_end(st)
        sb.reset(m)

    K.mod_begin, K.mod_block, K.mod_end = mod_begin, mod_block, mod_end

    def norm_tile(src, hn, junk, stat_col):
        ss = stat(slice(stat_col, stat_col + 1))
        rs = stat(slice(stat_col + 1, stat_col + 2))
        T.op(ACT, lambda e: e.activation(out=junk.ap, in_=src.ap, func=AF.Square, accum_out=ss.ap),
             reads=[src], writes=[junk, ss])
        T.op(ACT, lambda e: e.activation(out=rs.ap, in_=ss.ap, func=AF.Sqrt, scale=1.0 / D, bias=EPS),
             reads=[ss], writes=[rs])
        T.op(DVE, lambda e: e.reciprocal(out=rs.ap, in_=rs.ap), reads=[rs], writes=[rs])
        T.op(DVE, lambda e: e.tensor_scalar(out=hn.ap, in0=src.ap, scalar1=rs.ap, scalar2=None, op0=ALU.mult),
             reads=[src, rs], writes=[hn])
        return rs

    def tile_to_hT(hn, hT, col0, r, n_i, modulate=True):
        banks = [ps_bank(bf=True), ps_bank(bf=True)]
        for k in range(KT):
            o = banks[k // 4](slice((k % 4) * 128, (k % 4 + 1) * 128))
            i = hn(slice(k * 128, (k + 1) * 128))
            T.op(PE, lambda e: e.transpose(out=o.ap, in_=i.ap, identity=ident().ap),
                 reads=[i, ident()], writes=[o], inc=(k % 4 == 3))
        for kk in range(4):
            for half, eng in ((0, ACT), (1, DVE)):
                k = half * 4 + kk
                o = hT(k, slice(col0, col0 + 128))
                i = banks[half](slice(kk * 128, (kk + 1) * 128))
                G = Gsh(r, 2 * n_i, slice(k, k + 1))
                sh = Gsh(r, 2 * n_i + 1, slice(k, k + 1))
                if not modulate:
                    if eng is ACT:
                        T.op(ACT, lambda e: e.copy(out=o.ap, in_=i.ap), reads=[i], writes=[o])
                    else:
                        T.op(DVE, lambda e: e.tensor_copy(out=o.ap, in_=i.ap), reads=[i], writes=[o])
                elif eng is ACT:
                    T.op(ACT, lambda e: e.activation(out=o.ap, in_=i.ap, func=AF.Identity, scale=G.ap, bias=sh.ap),
                         reads=[i, G, sh], writes=[o])
                else:
                    T.op(DVE, lambda e: e.tensor_scalar(out=o.ap, in0=i.ap, scalar1=G.ap, scalar2=sh.ap,
                                                        op0=ALU.mult, op1=ALU.add),
                         reads=[i, G, sh], writes=[o])

    def apply_mod(hT, n_i, with_ctx):
        for k in range(KT):
            for r, (c0, c1) in enumerate(((CTX, CTX + SEQ), (0, CTX))):
                if r == 1 and not with_ctx:
                    continue
                a_ = hT(k, slice(c0, c1))
                G = Gsh(r, 2 * n_i, slice(k, k + 1))
                sh = Gsh(r, 2 * n_i + 1, slice(k, k + 1))
                T.op(DVE, lambda e: e.tensor_scalar(out=a_.ap, in0=a_.ap, scalar1=G.ap, scalar2=sh.ap,
                                                    op0=ALU.mult, op1=ALU.add), reads=[a_, G, sh], writes=[a_])

    K.apply_mod = apply_mod

    def norm_all_to_hT(l, n_i, hT, with_ctx, hn, junk, modulate=True, hook=None):
        tiles = []
        if with_ctx:
            tiles += [(1, cx(t), t * 128) for t in range(2)]
        tiles += [(0, xs(t), 256 + t * 128) for t in range(16)]
        n = len(tiles)
        norm_tile(tiles[0][1], hn[0](), junk(), 0)
        for t in range(n):
            if t + 1 < n:
                norm_tile(tiles[t + 1][1], hn[(t + 1) % 2](), junk(), 2 * ((t + 1) % 8))
            r, src, col0 = tiles[t]
            tile_to_hT(hn[t % 2], hT, col0, r, n_i, modulate)
            if hook is not None:
                hook(t)

    K.norm_all_to_hT = norm_all_to_hT

    def mlp_phase(l):
        m = sb.mark()
        with_ctx = l < DEPTH - 1
        load_gates(5, l)
        hT = sb.alloc([KT, CTX + SEQ], BF16)
        hn = [sb.alloc([D], BF16) for _ in range(2)]
        junk = sb.alloc([D], BF16)
        w1b = [sb.alloc([KT, 512], BF16) for _ in range(2)]
        w2b = [sb.alloc([4, D], BF16) for _ in range(2)]

        def load_mlp_w(fg):
            a1 = w1b[fg % 2]()
            T.dma(POOL, dsem("d_w1_%d" % (fg % 2)), a1.ap,
                  K.w1_d[l, :, fg * 512:(fg + 1) * 512].rearrange("(kt p) n -> p kt n", p=128), writes=[a1])
            a2 = w2b[fg % 2]()
            T.dma(POOL, dsem("d_w2_%d" % (fg % 2)), a2.ap,
                  K.w2_d[l, fg * 512:(fg + 1) * 512, :].rearrange("(ft p) n -> p ft n", p=128), writes=[a2])

        load_mlp_w(0)
        norm_all_to_hT(l, 1, hT, with_ctx, hn, junk)
        h1T = [sb.alloc([4, 512], BF16) for _ in range(2)]
        tmp = [sb.alloc([D], F32) for _ in range(2)]
        blocks = []
        if with_ctx:
            blocks.append((1, 0, [cx(0), cx(1)]))
        for b in range(4):
            blocks.append((0, 256 + b * 512, [xs(t) for t in range(4 * b, 4 * b + 4)]))
        cnt = 0
        ev = 0
        for fg in range(8):
            wb1 = w1b[fg % 2]
            wb2 = w2b[fg % 2]
            if fg + 1 < 8:
                load_mlp_w(fg + 1)
            for r, c0, tiles in blocks:
                nt = len(tiles)
                ntok = nt * 128
                h1 = h1T[cnt % 2]
                cnt += 1
                for fi in range(4):
                    ps = ps_bank()
                    o = ps(slice(0, ntok))
                    for k in range(KT):
                        lh = wb1(k, slice(fi * 128, (fi + 1) * 128))
                        rh = hT(k, slice(c0, c0 + ntok))
                        T.op(PE, lambda e: e.matmul(o.ap, lhsT=lh.ap, rhs=rh.ap, start=(k == 0), stop=(k == KT - 1)),
                             reads=[lh, rh], writes=[o], inc=(k == KT - 1))
                    dst = h1(fi, slice(0, ntok))
                    T.op(ACT, lambda e: e.activation(out=dst.ap, in_=o.ap, func=AF.Relu), reads=[o], writes=[dst])
                    T.op(ACT, lambda e: e.activation(out=dst.ap, in_=dst.ap, func=AF.Square),
                         reads=[dst], writes=[dst])
                if cfg.get("mlp_upto", 3) < 3:
                    continue
                for ti in range(nt):
                    pp = ps_pair()[0]
                    for hh in range(2):
                        o = pp(slice(hh * 512, (hh + 1) * 512))
                        for fi in range(4):
                            lh = h1(fi, slice(ti * 128, (ti + 1) * 128))
                            rh = wb2(fi, slice(hh * 512, (hh + 1) * 512))
                            T.op(PE, lambda e: e.matmul(o.ap, lhsT=lh.ap, rhs=rh.ap, start=(fi == 0), stop=(fi == 3)),
                                 reads=[lh, rh], writes=[o], inc=(fi == 3))
                    gt = gates(r)
                    tp = tmp[ev % 2]()
                    ev += 1
                    o = pp()
                    dst = tiles[ti]
                    T.op(DVE, lambda e: e.tensor_tensor(out=tp.ap, in0=o.ap, in1=gt.ap, op=ALU.mult),
                         reads=[o, gt], writes=[tp])
                    T.op(POOL if ev % 2 == 0 else DVE,
                         lambda e: e.tensor_tensor(out=dst.ap, in0=dst.ap, in1=tp.ap, op=ALU.add),
                         reads=[dst, tp], writes=[dst])
        sb.reset(m)

    for l in range(n_layers):
        stages = cfg.get("stages", ("mod", "mixer", "mlp"))
        if "mod" in stages and not ("mixer" in stages and cfg.get("mixer_fn") and cfg.get("mod_in_mixer", True)):
            compute_mod(l)
        if "mixer" in stages and cfg.get("mixer_fn"):
            cfg["mixer_fn"](K, l)
        if "mlp" in stages:
            mlp_phase(l)

    m = sb.mark()
    fg = sb.alloc([D], F32)
    ob = [sb.alloc([D], F32) for _ in range(2)]
    junk = sb.alloc([D], BF16)
    a = fg()
    T.dma(SP, dsem("d_fg"), a.ap, K.fg_d, writes=[a])
    for t in range(16):
        src = xs(t)
        ss = stat(slice(2 * (t % 8), 2 * (t % 8) + 1))
        rs = stat(slice(2 * (t % 8) + 1, 2 * (t % 8) + 2))
        jk = junk()
        T.op(ACT, lambda e: e.activation(out=jk.ap, in_=src.ap, func=AF.Square, accum_out=ss.ap),
             reads=[src], writes=[jk, ss])
        T.op(ACT, lambda e: e.activation(out=rs.ap, in_=ss.ap, func=AF.Sqrt, scale=1.0 / D, bias=EPS),
             reads=[ss], writes=[rs])
        T.op(DVE, lambda e: e.reciprocal(out=rs.ap, in_=rs.ap), reads=[rs], writes=[rs])
        o = ob[t % 2]()
        T.op(DVE, lambda e: e.scalar_tensor_tensor(out=o.ap, in0=src.ap, scalar=rs.ap, in1=fg().ap,
                                                   op0=ALU.mult, op1=ALU.mult),
             reads=[src, rs, fg()], writes=[o])
        T.dma(SP, dsem("d_out%d" % (t % 2)), K.out_d[t * 128:(t + 1) * 128, :], o.ap, reads=[o])
    for t in range(2):
        SP.eng.wait_ge(dsem("d_out%d" % t).sem, dsem("d_out%d" % t).count)
    if cfg.get("dump") and "d_dbg" in dsems:
        SP.eng.wait_ge(dsems["d_dbg"].sem, dsems["d_dbg"].count)
    sb.reset(m)
    print("build: inst=%d waits=%d sbuf_peak=%d" % (T.n_inst, T.n_wait, sb.peak))
    return nc


WR = 2310
NLC = 11


def mixer_decl(K, din):
    K.lrucol_d = din("lrucol", [128, DEPTH * 4 * NLC])
    K.wab_d = din("wab", [DEPTH * 16, 128, 128])
    if K.cfg.get("ssd_decl"):
        K.cfg["ssd_decl"](K, din)


def mixer_phase(K, l):
    nc, sb, T = K.nc, K.sb, K.T
    PE, ACT, DVE, POOL, SP = K.PE, K.ACT, K.DVE, K.POOL, K.SP
    xs, cx, gates, ident = K.xs, K.cx, K.gates, K.ident
    ps_bank, ps_pair, dsem = K.ps_bank, K.ps_pair, K.dsem
    cfg = K.cfg
    with_ctx_out = l < DEPTH - 1

    m0 = sb.mark()
    NT = CTX + SEQ
    hT = sb.alloc([KT, NT], BF16)
    catT = sb.alloc([4, NT], BF16)
    m1 = sb.mark()
    hn = [sb.alloc([D], BF16) for _ in range(2)]
    junk = sb.alloc([D], BF16)
    if cfg.get("mod_in_mixer", True):
        mst = K.mod_begin(l)

        def hook(t):
            if t % 3 != 2:
                K.mod_block(mst)

        K.norm_all_to_hT(l, 0, hT, True, hn, junk, modulate=False, hook=hook)
        K.mod_end(mst)
        K.apply_mod(hT, 0, True)
    else:
        K.norm_all_to_hT(l, 0, hT, True, hn, junk)
    sb.reset(m1)
    K.load_gates(2, l)

    wt_bufs = [sb.alloc([KT, 128], BF16) for _ in range(2)]
    wt_cnt = [0]

    def load_wcols(col0, ncols=128):
        i = wt_cnt[0] % 2
        wt_cnt[0] += 1
        wb = wt_bufs[i]
        a = wb(slice(0, KT), slice(0, ncols))
        T.dma(POOL, dsem("d_win%d" % i), a.ap,
              K.w_in_d[l, :, col0:col0 + ncols].rearrange("(kt p) n -> p kt n", p=128), writes=[a])
        return wb

    blocks = [(0, CTX, 2)] + [(CTX + 512 * b, 512, 260 + 512 * b) for b in range(4)]

    def outproj_half(which):
        m = sb.mark()
        wo = sb.alloc([4, D], BF16)
        wos = [sb.alloc([4, D], BF16) for _ in range(2 if with_ctx_out else 1)]
        a = wo()
        T.dma(POOL, dsem("d_wo"), a.ap,
              K.w_out_d[l, which * 512:(which + 1) * 512, :].rearrange("(ft p) n -> p ft n", p=128), writes=[a])
        for r in range(len(wos)):
            for fi in range(4):
                src = wo(fi)
                dst = wos[r](fi)
                gt = gates(r)
                T.op(DVE if (fi + r) % 2 == 0 else POOL,
                     lambda e: e.tensor_tensor(out=dst.ap, in0=src.ap, in1=gt.ap, op=ALU.mult),
                     reads=[src, gt], writes=[dst])
        tl = []
        if with_ctx_out:
            tl += [(1, cx(t), t * 128) for t in range(2)]
        tl += [(0, xs(t), CTX + t * 128) for t in range(16)]
        for ev, (r, dst, c0) in enumerate(tl):
            pp = ps_pair()[0]
            wr = wos[r]
            for hh in range(2):
                o = pp(slice(hh * 512, (hh + 1) * 512))
                for fi in range(4):
                    lh = catT(fi, slice(c0, c0 + 128))
                    rh = wr(fi, slice(hh * 512, (hh + 1) * 512))
                    T.op(PE, lambda e: e.matmul(o.ap, lhsT=lh.ap, rhs=rh.ap, start=(fi == 0), stop=(fi == 3)),
                         reads=[lh, rh], writes=[o], inc=(fi == 3))
            o = pp()
            T.op(DVE, lambda e: e.tensor_tensor(out=dst.ap, in0=dst.ap, in1=o.ap, op=ALU.add),
                 reads=[dst, o], writes=[dst])
        sb.reset(m)

    def lru_part():
        m = sb.mark()
        LX = sb.alloc([WR], F32)
        U = sb.alloc([WR], F32)
        Ub = sb.alloc([WR], BF16)
        H0 = sb.alloc([WR], F32)
        H1 = LX
        NTT = CTX + SEQ
        Af = sb.alloc([NTT], F32)
        Sf = sb.alloc([NTT], F32)
        Bf = sb.alloc([NTT], F32)
        Gt = [View(sb.t32, "sb", Af.off + 2048 * i_, [512], F32) for i_ in range(2)]
        Bb = [View(sb.t32, "sb", Af.off + 4096 + 2048 * i_, [512], F32) for i_ in range(2)]
        lcol = sb.alloc([4, NLC], F32)
        lhalf = sb.alloc([4, NLC], F32)
        lsp = sb.alloc([4, NLC], F32)
        lcs = sb.alloc([4, NLC], F32)
        lhcs = sb.alloc([4, NLC], F32)
        wab = sb.alloc([16, 128], BF16)
        a = lcol()
        T.dma(SP, dsem("d_lcol"), a.ap, K.lrucol_d[:, l * 4 * NLC:(l + 1) * 4 * NLC], writes=[a])
        a = wab()
        T.dma(POOL, dsem("d_wab"), a.ap, K.wab_d[l * 16:(l + 1) * 16, :, :].rearrange("m p n -> p m n"), writes=[a])
        T.op(DVE, lambda e: e.tensor_scalar(out=lhalf().ap, in0=lcol().ap, scalar1=0.5, scalar2=None, op0=ALU.mult),
             reads=[lcol()], writes=[lhalf()])
        T.op(ACT, lambda e: e.activation(out=lsp().ap, in_=lcol().ap, func=AF.Exp, scale=-1.0),
             reads=[lcol()], writes=[lsp()])
        T.op(ACT, lambda e: e.activation(out=lsp().ap, in_=lsp().ap, func=AF.Ln, bias=1.0),
             reads=[lsp()], writes=[lsp()])
        T.op(DVE, lambda e: e.tensor_scalar(out=lcs().ap, in0=lsp().ap, scalar1=-8.0, scalar2=None, op0=ALU.mult),
             reads=[lsp()], writes=[lcs()])
        T.op(DVE, lambda e: e.tensor_scalar(out=lhcs().ap, in0=lsp().ap, scalar1=-4.0, scalar2=None, op0=ALU.mult),
             reads=[lsp()], writes=[lhcs()])
        for (c0, c1) in ((0, 2), (258, 260), (2308, 2310)):
            a = LX(slice(c0, c1))
            T.op(DVE, lambda e: e.memset(a.ap, 0.0), writes=[a])

        bcnt = 0
        wb_next = load_wcols(0)
        for i in range(4):
            wb = wb_next
            for (hc0, n, dc) in blocks:
                ps = ps_bank()
                o = ps(slice(0, n))
                for k in range(KT):
                    lh = wb(k)
                    rh = hT(k, slice(hc0, hc0 + n))
                    T.op(PE, lambda e: e.matmul(o.ap, lhsT=lh.ap, rhs=rh.ap, start=(k == 0), stop=(k == KT - 1)),
                         reads=[lh, rh], writes=[o], inc=(k == KT - 1))
                dst = LX(slice(dc, dc + n))
                T.op(ACT, lambda e: e.copy(out=dst.ap, in_=o.ap), reads=[o], writes=[dst])
            wb_lg = load_wcols(NSCAN + 128 * i)
            if i + 1 < 4:
                wb_next = load_wcols(128 * (i + 1))
            n = 2306
            uo = U(slice(2, 2 + n))
            i0 = LX(slice(1, 1 + n))
            w0 = lcol(i, slice(0, 1))
            cb = lcol(i, slice(4, 5))
            T.op(DVE, lambda e: e.tensor_scalar(out=uo.ap, in0=i0.ap, scalar1=w0.ap, scalar2=cb.ap,
                                                op0=ALU.mult, op1=ALU.add),
                 reads=[i0, w0, cb], writes=[uo])
            for k in range(1, 4):
                ik = LX(slice(1 + k, 1 + k + n))
                wk = lcol(i, slice(k, k + 1))
                T.op(DVE, lambda e: e.scalar_tensor_tensor(out=uo.ap, in0=ik.ap, scalar=wk.ap, in1=uo.ap,
                                                           op0=ALU.mult, op1=ALU.add),
                     reads=[ik, wk, uo], writes=[uo])
            if i == 0:
                K.dump(LX(), WR, "LX")
                K.dump(U(slice(2, 2 + n)), n, "U")
            ubo = Ub(slice(2, 2 + n))
            T.op(DVE, lambda e: e.tensor_copy(out=ubo.ap, in_=uo.ap), reads=[uo], writes=[ubo])
            for d in range(2):
                Hd = H0 if d == 0 else H1
                hba = lhalf(i, slice(5 + 3 * d, 6 + 3 * d))
                hbx = lhalf(i, slice(6 + 3 * d, 7 + 3 * d))
                cs = lcs(i, slice(7 + 3 * d, 8 + 3 * d))
                hcs = lhcs(i, slice(7 + 3 * d, 8 + 3 * d))
                for (hc0, n, dc) in blocks:
                    A_ = Af(slice(hc0, hc0 + n))
                    S_ = Sf(slice(hc0, hc0 + n))
                    B_ = Bf(slice(hc0, hc0 + n))
                    ub = Ub(slice(dc, dc + n))
                    uf = U(slice(dc, dc + n))
                    psa = ps_bank()(slice(0, n))
                    wa = wab((d * 2 + 0) * 4 + i)
                    T.op(PE, lambda e: e.matmul(psa.ap, lhsT=wa.ap, rhs=ub.ap, start=True, stop=True),
                         reads=[wa, ub], writes=[psa])
                    psx = ps_bank()(slice(0, n))
                    wx = wab((d * 2 + 1) * 4 + i)
                    T.op(PE, lambda e: e.matmul(psx.ap, lhsT=wx.ap, rhs=ub.ap, start=True, stop=True),
                         reads=[wx, ub], writes=[psx])
                    T.op(ACT, lambda e: e.activation(out=A_.ap, in_=psa.ap, func=AF.Tanh, scale=0.5, bias=hba.ap),
                         reads=[psa, hba], writes=[A_])
                    T.op(ACT, lambda e: e.activation(out=B_.ap, in_=psx.ap, func=AF.Tanh, scale=0.5, bias=hbx.ap),
                         reads=[psx, hbx], writes=[B_])
                    T.op(ACT, lambda e: e.activation(out=A_.ap, in_=A_.ap, func=AF.Exp, scale=hcs.ap, bias=hcs.ap),
                         reads=[A_, hcs], writes=[A_])
                    T.op(DVE, lambda e: e.tensor_tensor(out=S_.ap, in0=A_.ap, in1=A_.ap, op=ALU.mult),
                         reads=[A_], writes=[S_])
                    T.op(DVE, lambda e: e.scalar_tensor_tensor(out=B_.ap, in0=B_.ap, scalar=1.0, in1=uf.ap,
                                                               op0=ALU.add, op1=ALU.mult),
                         reads=[B_, uf], writes=[B_])
                T.op(ACT, lambda e: e.activation(out=Sf().ap, in_=Sf().ap, func=AF.Sqrt, scale=-1.0, bias=1.0),
                     reads=[Sf()], writes=[Sf()])
                order = blocks if d == 0 else [blocks[0]] + blocks[:0:-1]
                prev = None
                for (hc0, n, dc) in order:
                    A_ = Af(slice(hc0, hc0 + n))
                    S_ = Sf(slice(hc0, hc0 + n))
                    B_ = Bf(slice(hc0, hc0 + n))
                    T.op(DVE, lambda e: e.scalar_tensor_tensor(out=B_.ap, in0=B_.ap, scalar=0.5, in1=S_.ap,
                                                               op0=ALU.mult, op1=ALU.mult),
                         reads=[B_, S_], writes=[B_])
                    ho = Hd(slice(dc, dc + n))
                    rd = [A_, B_] + ([prev] if prev is not None else [])
                    init = prev.ap if prev is not None else 0.0
                    if d == 0:
                        T.op(DVE, lambda e: e.tensor_tensor_scan(out=ho.ap, data0=A_.ap, data1=B_.ap, initial=init,
                                                                 op0=ALU.mult, op1=ALU.add),
                             reads=rd, writes=[ho])
                        prev = Hd(slice(dc + n - 1, dc + n))
                    else:
                        T.op(DVE, lambda e: e.tensor_tensor_scan(out=ho.ap[:, ::-1], data0=A_.ap[:, ::-1],
                                                                 data1=B_.ap[:, ::-1], initial=init,
                                                                 op0=ALU.mult, op1=ALU.add),
                             reads=rd, writes=[ho])
                        prev = Hd(slice(dc, dc + 1))
            if i == 0:
                K.dump(H0(slice(260, 2308)), 2048, "H0")
                K.dump(H1(slice(260, 2308)), 2048, "H1")
            wb = wb_lg
            gblocks = blocks if with_ctx_out else blocks[1:]
            for gi, (hc0, n, dc) in enumerate(gblocks):
                ps = ps_bank()
                o = ps(slice(0, n))
                for k in range(KT):
                    lh = wb(k)
                    rh = hT(k, slice(hc0, hc0 + n))
                    T.op(PE, lambda e: e.matmul(o.ap, lhsT=lh.ap, rhs=rh.ap, start=(k == 0), stop=(k == KT - 1)),
                         reads=[lh, rh], writes=[o], inc=(k == KT - 1))
                t1 = Gt[gi % 2](slice(0, n))
                t2 = Bb[gi % 2](slice(0, n))
                T.op(ACT, lambda e: e.activation(out=t1.ap, in_=o.ap, func=AF.Square, scale=0.21145921592600305),
                     reads=[o], writes=[t1])
                T.op(DVE, lambda e: e.scalar_tensor_tensor(out=t1.ap, in0=t1.ap, scalar=1.0, in1=o.ap,
                                                           op0=ALU.add, op1=ALU.mult), reads=[t1, o], writes=[t1])
                T.op(ACT, lambda e: e.activation(out=t1.ap, in_=t1.ap, func=AF.Tanh, scale=0.7978845608028654),
                     reads=[t1], writes=[t1])
                T.op(DVE, lambda e: e.scalar_tensor_tensor(out=t1.ap, in0=t1.ap, scalar=1.0, in1=o.ap,
                                                           op0=ALU.add, op1=ALU.mult), reads=[t1, o], writes=[t1])
                h0 = H0(slice(dc, dc + n))
                h1 = H1(slice(dc, dc + n))
                T.op(POOL, lambda e: e.tensor_tensor(out=t2.ap, in0=h0.ap, in1=h1.ap, op=ALU.add),
                     reads=[h0, h1], writes=[t2])
                if i == 0 and hc0 == CTX:
                    K.dump(t1, n, "gelu2")
                    K.dump(t2, n, "hsum")
                co = catT(i, slice(hc0, hc0 + n))
                T.op(DVE, lambda e: e.scalar_tensor_tensor(out=co.ap, in0=t2.ap, scalar=0.5, in1=t1.ap,
                                                           op0=ALU.mult, op1=ALU.mult), reads=[t2, t1], writes=[co])
        sb.reset(m)

    stages = cfg.get("mix_stages", ("lru", "ssd"))
    if "lru" in stages:
        lru_part()
        outproj_half(0)
    if "ssd" in stages and cfg.get("ssd_fn"):
        cfg["ssd_fn"](K, l, locals())
        K.load_gates(2, l)
        outproj_half(1)
    sb.reset(m0)


NBC = 16 + 16 + 8 + 512


def ssd_decl(K, din):
    K.ssdcol_d = din("ssdcol", [128, DEPTH * 8 * 5])
    K.ssdbc_d = din("ssdbc", [DEPTH, 128, NBC])
    K.masks_d = din("masks", [128, 5 * 128])


def ssd_part(K, l, env):
    nc, sb, T = K.nc, K.sb, K.T
    PE, ACT, DVE, POOL, SP = K.PE, K.ACT, K.DVE, K.POOL, K.SP
    ident = K.ident
    ps_bank, dsem = K.ps_bank, K.dsem
    hT, catT, load_wcols, blocks = env["hT"], env["catT"], env["load_wcols"], env["blocks"]
    with_ctx_out = l < DEPTH - 1
    stat = K.stat
    NCH = 18

    m = sb.mark()
    masks = sb.alloc([5, 128], F32)
    LE, GE, GT, LT, ONES = (masks(j) for j in range(5))
    bc = sb.alloc([NBC], F32)
    scol = sb.alloc([8, 5], F32)
    a = masks()
    T.dma(SP, dsem("d_masks"), a.ap, K.masks_d.rearrange("p (j n) -> p j n", n=128), writes=[a])
    a = bc()
    T.dma(SP, dsem("d_sbc"), a.ap, K.ssdbc_d[l], writes=[a])
    a = scol()
    T.dma(SP, dsem("d_scol"), a.ap, K.ssdcol_d[:, l * 40:(l + 1) * 40].rearrange("p (t c) -> p t c", c=5), writes=[a])
    dtbias = bc(slice(0, 16))
    alog = bc(slice(16, 32))
    dsk = bc(slice(32, 40))
    nexpA = sb.alloc([16], F32)
    T.op(ACT, lambda e: e.activation(out=nexpA().ap, in_=alog.ap, func=AF.Exp), reads=[alog], writes=[nexpA()])
    T.op(DVE, lambda e: e.tensor_scalar(out=nexpA().ap, in0=nexpA().ap, scalar1=-1.0, scalar2=None, op0=ALU.mult),
         reads=[nexpA()], writes=[nexpA()])

    mp = sb.mark()
    tmpPs = [sb.alloc([SEQ], BF16) for _ in range(2)]
    for k in range(KT):
        src = hT(k, slice(CTX, CTX + SEQ))
        tp_ = tmpPs[k % 2]()
        T.op(DVE, lambda e: e.tensor_copy(out=tp_.ap.rearrange("p (w r) -> p w r", r=32),
                                          in_=src.ap.rearrange("p (r w) -> p w r", w=64)),
             reads=[src], writes=[tp_])
        T.op(ACT, lambda e: e.copy(out=src.ap, in_=tp_.ap), reads=[tp_], writes=[src])
    sb.reset(mp)

    def chunk_cols(view, row, c):
        if c < 2:
            return view(row, slice(128 * c, 128 * c + 128))
        j = c - 2
        full = view(row, slice(CTX, CTX + SEQ))
        return full.w(full.ap.rearrange("p (r w) -> p w r", w=64)[:, 4 * j:4 * j + 4, :])

    DT = sb.alloc([NCH, 16], F32)
    AA = sb.alloc([NCH, 16], F32)
    LNDT = sb.alloc([NCH, 16], F32)
    CSX = sb.alloc([NCH, 16], F32)
    EX = sb.alloc([NCH, 16], F32)
    WX = sb.alloc([NCH, 16], F32)
    ETOT = sb.alloc([NCH, 16], F32)
    wb = load_wcols(1536, 16)
    dps = ps_bank()
    for c in range(NCH):
        o = dps(slice(16 * c, 16 * c + 16))
        for k in range(KT):
            lh = hT(k, slice(128 * c, 128 * c + 128))
            rh = wb(k, slice(0, 16))
            T.op(PE, lambda e: e.matmul(o.ap, lhsT=lh.ap, rhs=rh.ap, start=(k == 0), stop=(k == KT - 1)),
                 reads=[lh, rh], writes=[o], inc=(k == KT - 1))
    dall = dps(slice(0, 16 * NCH))
    bcb = dtbias.ap.rearrange("p (o n) -> p o n", o=1).to_broadcast([128, NCH, 16])
    nab = nexpA().ap.rearrange("p (o n) -> p o n", o=1).to_broadcast([128, NCH, 16])
    T.op(DVE, lambda e: e.tensor_tensor(out=DT().ap, in0=dall.ap.rearrange("p (c n) -> p c n", n=16), in1=bcb, op=ALU.add),
         reads=[dall, dtbias], writes=[DT()])
    T.op(ACT, lambda e: e.activation(out=DT().ap, in_=DT().ap, func=AF.Exp), reads=[DT()], writes=[DT()])
    T.op(ACT, lambda e: e.activation(out=DT().ap, in_=DT().ap, func=AF.Ln, bias=1.0), reads=[DT()], writes=[DT()])
    T.op(ACT, lambda e: e.activation(out=LNDT().ap, in_=DT().ap, func=AF.Ln), reads=[DT()], writes=[LNDT()])
    T.op(DVE, lambda e: e.tensor_tensor(out=AA().ap, in0=DT().ap, in1=nab, op=ALU.mult),
         reads=[DT(), nexpA()], writes=[AA()])
    aflat = AA().w(AA().ap.rearrange("p c n -> p (c n)"))
    cps = [ps_bank() for _ in range(3)]
    for pj, msk in enumerate((LE, GE, ONES)):
        o = cps[pj](slice(0, 16 * NCH))
        T.op(PE, lambda e: e.matmul(o.ap, lhsT=msk.ap, rhs=aflat.ap, start=True, stop=True),
             reads=[msk, aflat], writes=[o])
    for d in range(2):
        src = cps[d](slice(0, 16 * NCH))
        dst = CSX(slice(0, NCH), slice(8 * d, 8 * d + 8))
        T.op(DVE, lambda e: e.tensor_copy(out=dst.ap, in_=src.ap.rearrange("p (c n) -> p c n", n=16)[:, :, 8 * d:8 * d + 8]),
             reads=[src], writes=[dst])
    tot = cps[2](slice(0, 16 * NCH))
    tot3 = tot.ap.rearrange("p (c n) -> p c n", n=16)
    T.op(ACT, lambda e: e.activation(out=ETOT().ap, in_=tot3, func=AF.Exp), reads=[tot], writes=[ETOT()])
    T.op(ACT, lambda e: e.activation(out=EX().ap, in_=CSX().ap, func=AF.Exp), reads=[CSX()], writes=[EX()])
    T.op(DVE, lambda e: e.tensor_tensor(out=WX().ap, in0=tot3, in1=CSX().ap, op=ALU.subtract),
         reads=[tot, CSX()], writes=[WX()])
    T.op(ACT, lambda e: e.activation(out=WX().ap, in_=WX().ap, func=AF.Exp), reads=[WX()], writes=[WX()])
    T.op(DVE, lambda e: e.tensor_tensor(out=WX().ap, in0=WX().ap, in1=DT().ap, op=ALU.mult),
         reads=[WX(), DT()], writes=[WX()])

    def bc4(view, c, d, g):
        a_ = view(c, slice(8 * d + 4 * g, 8 * d + 4 * g + 4))
        return a_, a_.ap.rearrange("p (h o) -> p h o", o=1).to_broadcast([128, 4, 64])

    small = sb.mark()
    for g in range(2):
        sb.reset(small)
        FT = sb.alloc([4, WR], BF16)
        ovl = sb.mark()
        CXb = sb.alloc([WR], F32)
        CU = sb.alloc([WR], F32)
        for (c0, c1) in ((0, 2), (258, 260), (2308, 2310)):
            a = CXb(slice(c0, c1))
            T.op(DVE, lambda e: e.memset(a.ap, 0.0), writes=[a])
        tile_cols = [512 + 256 * g, 512 + 256 * g + 128, 1024 + 128 * g, 1280 + 128 * g]
        wb_n = load_wcols(tile_cols[0])
        for ti, col0 in enumerate(tile_cols):
            t8 = (col0 - 512) // 128
            wb = wb_n
            if ti + 1 < 4:
                wb_n = load_wcols(tile_cols[ti + 1])
            for bi, (hc0, n, dc) in enumerate(blocks):
                ps = ps_bank()
                o = ps(slice(0, n))
                for k in range(KT):
                    lh = wb(k)
                    rh = hT(k, slice(hc0, hc0 + n))
                    T.op(PE, lambda e: e.matmul(o.ap, lhsT=lh.ap, rhs=rh.ap, start=(k == 0), stop=(k == KT - 1)),
                         reads=[lh, rh], writes=[o], inc=(k == KT - 1))
                dst = CXb(slice(dc, dc + n))
                T.op(ACT, lambda e: e.copy(out=dst.ap, in_=o.ap), reads=[o], writes=[dst])
            n = 2306
            uo = CU(slice(2, 2 + n))
            i0 = CXb(slice(1, 1 + n))
            w0 = scol(t8, slice(0, 1))
            cb = scol(t8, slice(4, 5))
            T.op(DVE, lambda e: e.tensor_scalar(out=uo.ap, in0=i0.ap, scalar1=w0.ap, scalar2=cb.ap,
                                                op0=ALU.mult, op1=ALU.add), reads=[i0, w0, cb], writes=[uo])
            for k in range(1, 4):
                ik = CXb(slice(1 + k, 1 + k + n))
                wk = scol(t8, slice(k, k + 1))
                T.op(DVE, lambda e: e.scalar_tensor_tensor(out=uo.ap, in0=ik.ap, scalar=wk.ap, in1=uo.ap,
                                                           op0=ALU.mult, op1=ALU.add), reads=[ik, wk, uo], writes=[uo])
            fo = FT(ti, slice(2, 2 + n))
            T.op(ACT, lambda e: e.activation(out=fo.ap, in_=uo.ap, func=AF.Silu), reads=[uo], writes=[fo])
        sb.reset(ovl)
        Y = sb.alloc([NCH, 256], F32)
        S = sb.alloc([256], F32)
        Sbf = [sb.alloc([256], BF16) for _ in range(2)]
        XBc = [sb.alloc([384], BF16) for _ in range(2)]
        Xs = [sb.alloc([256], BF16) for _ in range(2)]
        Xd = [sb.alloc([256], BF16) for _ in range(2)]
        CBm = sb.alloc([128], F32)
        Yt = [sb.alloc([256], F32) for _ in range(2)]
        SZ = sb.alloc([256], F32)
        YZ = sb.alloc([256], F32)
        ON = sb.alloc([256], BF16)
        jk = sb.alloc([256], BF16)
        gbase = K.gates.off
        RA = [View(sb.t32, "sb", gbase + 2048 * i, [4, 128], F32) for i in range(2)]
        Lm = View(sb.t32, "sb", gbase + 4096, [4, 128], F32)
        MT = [View(sb.tbf, "sb", gbase + 6144 + 1024 * i, [4, 128], BF16) for i in range(2)]
        wtb = env["wt_bufs"]
        wz = View(sb.tbf, "sb", wtb[0].off, [KT, 256], BF16)
        assert wtb[1].off == wtb[0].off + 2048
        a = wz()
        T.dma(POOL, dsem("d_wz"), a.ap,
              K.w_in_d[l, :, 2064 + 256 * g:2064 + 256 * g + 256].rearrange("(kt p) n -> p kt n", p=128), writes=[a])
        normg = bc(slice(40 + 256 * g, 40 + 256 * g + 256))

        def pad_cols(c):
            return (2 + 128 * c) if c < 2 else (260 + 128 * (c - 2))

        DSKM = sb.alloc([8, 128], BF16)
        idf = Lm(0)
        dtmp = Lm(1)
        T.op(DVE, lambda e: e.tensor_tensor(out=idf.ap, in0=LE.ap, in1=GE.ap, op=ALU.mult), reads=[LE, GE], writes=[idf])
        for h in range(4):
            dcol = bc(slice(32 + 4 * g + h, 32 + 4 * g + h + 1))
            hi = DSKM(2 * h)
            lo = DSKM(2 * h + 1)
            T.op(ACT, lambda e: e.activation(out=hi.ap, in_=idf.ap, func=AF.Copy, scale=dcol.ap),
                 reads=[idf, dcol], writes=[hi])
            T.op(DVE, lambda e: e.scalar_tensor_tensor(out=dtmp.ap, in0=idf.ap, scalar=dcol.ap, in1=hi.ap,
                                                       op0=ALU.mult, op1=ALU.subtract),
                 reads=[idf, dcol, hi], writes=[dtmp])
            T.op(DVE, lambda e: e.tensor_copy(out=lo.ap, in_=dtmp.ap), reads=[dtmp], writes=[lo])

        def h4(acc_):
            return acc_.ap.rearrange("p (h q) -> p h q", q=64)

        for d in range(2):
            T.op(DVE, lambda e: e.memset(S().ap, 0.0), writes=[S()])
            T.op(DVE, lambda e: e.memset(Sbf[0]().ap, 0.0), writes=[Sbf[0]()])
            order = list(range(NCH)) if d == 0 else [1, 0] + list(range(NCH - 1, 1, -1))
            st = {}

            def stageA(i):
                c = order[i]
                want_y = with_ctx_out or c >= 2
                pc = pad_cols(c)
                xb = XBc[i % 2]
                tp = ps_bank(bf=True)
                for ti in range(3):
                    o = tp(slice(128 * ti, 128 * ti + 128))
                    i_ = FT(ti, slice(pc, pc + 128))
                    T.op(PE, lambda e: e.transpose(out=o.ap, in_=i_.ap, identity=ident().ap),
                         reads=[i_, ident()], writes=[o], inc=(ti == 2))
                tpa = tp(slice(0, 384))
                T.op(ACT, lambda e: e.copy(out=xb().ap, in_=tpa.ap), reads=[tpa], writes=[xb()])
                xtok = xb(slice(0, 256))
                btok = xb(slice(256, 384))
                r = {"c": c, "want_y": want_y, "pc": pc, "xtok": xtok}
                if want_y:
                    ct = FT(3, slice(pc, pc + 128))
                    bt = FT(2, slice(pc, pc + 128))
                    cbp = K.PSB[4 + 2 * (i % 2)](slice(0, 128))
                    T.op(PE, lambda e: e.matmul(cbp.ap, lhsT=bt.ap, rhs=ct.ap, start=True, stop=True),
                         reads=[bt, ct], writes=[cbp])
                    ra = RA[i % 2]()
                    dp = K.PSB[5 + 2 * (i % 2)](slice(0, 512))
                    smask = GT if d == 0 else LT
                    T.op(PE, lambda e: e.matmul(dp.ap, lhsT=smask.ap, rhs=ra.ap.rearrange("p h n -> p (h n)"),
                                                start=True, stop=True), reads=[smask, ra], writes=[dp])
                    xd = Xd[i % 2]()
                    da, db = bc4(DT, c, d, g)
                    T.op(POOL, lambda e: e.tensor_tensor(out=h4(xd), in0=h4(xtok), in1=db, op=ALU.mult),
                         reads=[xtok, da], writes=[xd])
                    r.update(ct=ct, cbp=cbp, dp=dp, xd=xd)
                xs_ = Xs[i % 2]()
                wa, wbq = bc4(WX, c, d, g)
                T.op(POOL, lambda e: e.tensor_tensor(out=h4(xs_), in0=h4(xtok), in1=wbq, op=ALU.mult),
                     reads=[xtok, wa], writes=[xs_])
                stp = K.PSB[4 + 2 * (i % 2)](slice(128, 384))
                T.op(PE, lambda e: e.matmul(stp.ap, lhsT=btok.ap, rhs=xs_.ap, start=True, stop=True),
                     reads=[btok, xs_], writes=[stp])
                r["stp"] = stp
                st[i] = r

            def emit_RA(i):
                c = order[i]
                if not (with_ctx_out or c >= 2):
                    return
                rmask = LE if d == 0 else GE
                for h in range(4):
                    ac1 = AA(c, slice(8 * d + 4 * g + h, 8 * d + 4 * g + h + 1))
                    rah = RA[i % 2](h)
                    T.op(ACT, lambda e: e.activation(out=rah.ap, in_=rmask.ap, func=AF.Copy, scale=ac1.ap),
                         reads=[rmask, ac1], writes=[rah])

            def stageB(i):
                r = st[i]
                if not r["want_y"]:
                    return
                dp, cbp, xd = r["dp"], r["cbp"], r["xd"]
                T.op(ACT, lambda e: e.activation(out=Lm().ap.rearrange("p h n -> p (h n)"), in_=dp.ap, func=AF.Exp),
                     reads=[dp], writes=[Lm()])
                cmask = LE if d == 0 else GE
                T.op(DVE, lambda e: e.tensor_tensor(out=CBm().ap, in0=cbp.ap, in1=cmask.ap, op=ALU.mult),
                     reads=[cbp, cmask], writes=[CBm()])
                mt = MT[i % 2]
                T.op(DVE, lambda e: e.tensor_tensor(
                    out=mt().ap, in0=Lm().ap,
                    in1=CBm().ap.rearrange("p (o n) -> p o n", o=1).to_broadcast([128, 4, 128]), op=ALU.mult),
                    reads=[Lm(), CBm()], writes=[mt()])
                yd = ps_bank()
                xtok = r["xtok"]
                for h in range(4):
                    o = yd(slice(64 * h, 64 * h + 64))
                    lh = mt(h)
                    rh = Acc(xd.ap[:, 64 * h:64 * h + 64], "sb", xd.ranges)
                    T.op(PE, lambda e: e.matmul(o.ap, lhsT=lh.ap, rhs=rh.ap, start=True, stop=(d == 1)),
                         reads=[lh, rh], writes=[o], inc=(h == 3 and d == 1))
                    if d == 0:
                        xr = Acc(xtok.ap[:, 64 * h:64 * h + 64], "sb", xtok.ranges)
                        for part in range(2):
                            dm = DSKM(2 * h + part)
                            T.op(PE, lambda e: e.matmul(o.ap, lhsT=dm.ap, rhs=xr.ap, start=False, stop=(part == 1)),
                                 reads=[dm, xr], writes=[o], inc=(h == 3 and part == 1))
                r["yd"] = yd(slice(0, 256))

            def stageC(i):
                r = st.pop(i)
                c = r["c"]
                sb_in = Sbf[i % 2]()
                if r["want_y"]:
                    yo = ps_bank()(slice(0, 256))
                    ct = r["ct"]
                    T.op(PE, lambda e: e.matmul(yo.ap, lhsT=ct.ap, rhs=sb_in.ap, start=True, stop=True),
                         reads=[ct, sb_in], writes=[yo])
                stp = r["stp"]
                ta, tb = bc4(ETOT, c, d, g)
                T.op(DVE, lambda e: e.tensor_tensor(out=h4(S()), in0=h4(S()), in1=tb, op=ALU.mult),
                     reads=[S(), ta], writes=[S()])
                T.op(DVE, lambda e: e.tensor_tensor(out=S().ap, in0=S().ap, in1=stp.ap, op=ALU.add),
                     reads=[S(), stp], writes=[S()])
                sb_out = Sbf[(i + 1) % 2]()
                T.op(DVE, lambda e: e.tensor_copy(out=sb_out.ap, in_=S().ap), reads=[S()], writes=[sb_out])
                if r["want_y"]:
                    yt = Yt[i % 2]()
                    ea, eb = bc4(EX, c, d, g)
                    T.op(DVE, lambda e: e.tensor_tensor(out=h4(yt), in0=h4(yo), in1=eb, op=ALU.mult),
                         reads=[yo, ea], writes=[yt])
                    yda = r["yd"]
                    yc = Y(c)
                    if d == 0:
                        T.op(DVE, lambda e: e.tensor_tensor(out=yc.ap, in0=yt.ap, in1=yda.ap, op=ALU.add),
                             reads=[yt, yda], writes=[yc])
                    else:
                        T.op(DVE, lambda e: e.tensor_tensor(out=yt.ap, in0=yt.ap, in1=yda.ap, op=ALU.add),
                             reads=[yt, yda], writes=[yt])
                        T.op(POOL, lambda e: e.tensor_tensor(out=yc.ap, in0=yc.ap, in1=yt.ap, op=ALU.add),
                             reads=[yc, yt], writes=[yc])

            n_it = len(order)
            K.ps_lim[0] = 4
            if K.ps_rr[0] >= 4:
                K.ps_rr[0] = 0
            emit_RA(0)
            stageA(0)
            if n_it > 1:
                emit_RA(1)
            for i in range(n_it):
                if i + 1 < n_it:
                    stageA(i + 1)
                stageB(i)
                stageC(i)
                if i + 2 < n_it:
                    emit_RA(i + 2)
            K.ps_lim[0] = 8

        fin = [c for c in range(NCH) if (with_ctx_out or c >= 2)]
        ssq = sb.alloc([NCH], F32)
        SZ2 = [SZ, YZ]
        zst = {}

        def f1_head(fi_):
            c = fin[fi_]
            zp = ps_bank()(slice(0, 256))
            for k in range(KT):
                lh = hT(k, slice(128 * c, 128 * c + 128))
                rh = wz(k)
                T.op(PE, lambda e: e.matmul(zp.ap, lhsT=lh.ap, rhs=rh.ap, start=(k == 0), stop=(k == KT - 1)),
                     reads=[lh, rh], writes=[zp], inc=(k == KT - 1))
            sz = SZ2[fi_ % 2]()
            T.op(ACT, lambda e: e.activation(out=sz.ap, in_=zp.ap, func=AF.Tanh, scale=0.5), reads=[zp], writes=[sz])
            zst[fi_] = (zp, sz)

        def f1_tail(fi_):
            c = fin[fi_]
            zp, sz = zst.pop(fi_)
            yc = Y(c)
            T.op(DVE, lambda e: e.scalar_tensor_tensor(out=sz.ap, in0=sz.ap, scalar=1.0, in1=zp.ap,
                                                       op0=ALU.add, op1=ALU.mult), reads=[sz, zp], writes=[sz])
            T.op(DVE, lambda e: e.scalar_tensor_tensor(out=yc.ap, in0=yc.ap, scalar=0.5, in1=sz.ap,
                                                       op0=ALU.mult, op1=ALU.mult), reads=[yc, sz], writes=[yc])
            ss = ssq(slice(c, c + 1))
            T.op(ACT, lambda e: e.activation(out=jk().ap, in_=yc.ap, func=AF.Square, accum_out=ss.ap),
                 reads=[yc], writes=[jk(), ss])

        f1_head(0)
        for fi_ in range(len(fin)):
            if fi_ + 1 < len(fin):
                f1_head(fi_ + 1)
            f1_tail(fi_)
        c_lo, c_hi = fin[0], fin[-1] + 1
        rsq = ssq(slice(c_lo, c_hi))
        T.op(ACT, lambda e: e.activation(out=rsq.ap, in_=rsq.ap, func=AF.Sqrt, scale=1.0 / 256, bias=EPS),
             reads=[rsq], writes=[rsq])
        T.op(DVE, lambda e: e.reciprocal(out=rsq.ap, in_=rsq.ap), reads=[rsq], writes=[rsq])
        ON2 = [ON, jk]

        def f3_head(fi_):
            c = fin[fi_]
            yc = Y(c)
            rs = ssq(slice(c, c + 1))
            on = ON2[fi_ % 2]
            T.op(DVE, lambda e: e.scalar_tensor_tensor(out=on().ap, in0=yc.ap, scalar=rs.ap, in1=normg.ap,
                                                       op0=ALU.mult, op1=ALU.mult),
                 reads=[yc, rs, normg], writes=[on()])

        f3_head(0)
        for fi_, c in enumerate(fin):
            on = ON2[fi_ % 2]
            if fi_ + 1 < len(fin):
                f3_head(fi_ + 1)
            tps = [ps_bank(bf=True), ps_bank(bf=True)]
            for ti in range(2):
                o = tps[ti](slice(0, 128))
                i_ = on(slice(128 * ti, 128 * ti + 128))
                T.op(PE, lambda e: e.transpose(out=o.ap, in_=i_.ap, identity=ident().ap),
                     reads=[i_, ident()], writes=[o])
            for ti in range(2):
                src = tps[ti](slice(0, 128))
                dst = chunk_cols(catT, 2 * g + ti, c)
                sap = src.ap if c < 2 else src.ap.rearrange("p (w r) -> p w r", r=32)
                if ti == 0:
                    T.op(ACT, lambda e: e.copy(out=dst.ap, in_=sap), reads=[src], writes=[dst])
                else:
                    T.op(DVE, lambda e: e.tensor_copy(out=dst.ap, in_=sap), reads=[src], writes=[dst])
    sb.reset(m)


def make_in_maps(inputs, cores):
    f = np.float32
    ident = np.eye(128, dtype=f)
    ngcol = np.zeros((128, DEPTH * 2 * KT), f)
    for l in range(DEPTH):
        ngcol[:, (l * 2) * KT:(l * 2 + 1) * KT] = np.asarray(inputs["norm1_g"][l], f).reshape(KT, 128).T
        ngcol[:, (l * 2 + 1) * KT:(l * 2 + 2) * KT] = np.asarray(inputs["norm2_g"][l], f).reshape(KT, 128).T
    fg_bc = np.ascontiguousarray(np.broadcast_to(np.asarray(inputs["final_g"], f)[None, :], (128, D)))
    shared = {
        "ada_w": np.ascontiguousarray(inputs["ada_w"], f), "ada_b": np.ascontiguousarray(inputs["ada_b"], f),
        "ngcol": ngcol, "w_in": np.ascontiguousarray(inputs["w_in"], f),
        "w_out": np.ascontiguousarray(inputs["w_out"], f), "mlp_w1": np.ascontiguousarray(inputs["mlp_w1"], f),
        "mlp_w2": np.ascontiguousarray(inputs["mlp_w2"], f), "final_g_bc": fg_bc, "ident": ident,
    }
    lrucol = np.zeros((128, DEPTH * 4 * NLC), f)
    wab = np.zeros((DEPTH * 16, 128, 128), f)
    for l in range(DEPTH):
        for i in range(4):
            base = (l * 4 + i) * NLC
            ch = slice(i * 128, (i + 1) * 128)
            for k in range(4):
                lrucol[:, base + k] = inputs["lru_conv_w"][l, k, ch]
            lrucol[:, base + 4] = inputs["lru_conv_b"][l, ch]
            for d in range(2):
                lrucol[:, base + 5 + 3 * d] = inputs["lru_ba"][l, d, ch]
                lrucol[:, base + 6 + 3 * d] = inputs["lru_bx"][l, d, ch]
                lrucol[:, base + 7 + 3 * d] = inputs["lru_lambda"][l, d, ch]
                for which, nm in enumerate(("lru_wa", "lru_wx")):
                    mtx = wab[l * 16 + (d * 2 + which) * 4 + i]
                    for hl in range(2):
                        mtx[hl * 64:(hl + 1) * 64, hl * 64:(hl + 1) * 64] = inputs[nm][l, d, 2 * i + hl]
    shared["lrucol"] = lrucol
    shared["wab"] = wab
    ssdcol = np.zeros((128, DEPTH * 8 * 5), f)
    ssdbc = np.zeros((DEPTH, 128, NBC), f)
    for l in range(DEPTH):
        for t8 in range(8):
            ch = slice(t8 * 128, (t8 + 1) * 128)
            for k in range(4):
                ssdcol[:, (l * 8 + t8) * 5 + k] = inputs["ssd_conv_w"][l, k, ch]
            ssdcol[:, (l * 8 + t8) * 5 + 4] = inputs["ssd_conv_b"][l, ch]
        ssdbc[l, :, 0:16] = np.asarray(inputs["ssd_dt_bias"][l], f).reshape(1, 16)
        ssdbc[l, :, 16:32] = np.asarray(inputs["ssd_a_log"][l], f).reshape(1, 16)
        ssdbc[l, :, 32:40] = np.asarray(inputs["ssd_d"][l], f).reshape(1, 8)
        ssdbc[l, :, 40:552] = np.asarray(inputs["ssd_norm_g"][l], f).reshape(1, 512)
    ia = np.arange(128)[:, None]
    ib = np.arange(128)[None, :]
    masks = np.concatenate([(ia <= ib), (ia >= ib), (ia > ib), (ia < ib), np.ones((128, 128), bool)], axis=1).astype(f)
    shared["ssdcol"] = ssdcol
    shared["ssdbc"] = ssdbc
    shared["masks"] = masks
    maps = []
    for b in cores:
        ccol = np.zeros((128, 2 * KT), f)
        ccol[:, 0:KT] = np.asarray(inputs["c"][b], f).reshape(KT, 128).T
        ccol[:, KT:2 * KT] = np.asarray(inputs["c_ctx"], f).reshape(KT, 128).T
        mp = dict(shared)
        mp["x"] = np.ascontiguousarray(inputs["x"][b], f)
        mp["ctx"] = np.ascontiguousarray(inputs["ctx"][b], f)
        mp["ccol"] = ccol
        maps.append(mp)
    return maps


DEFAULT_CFG = {"mixer_decl": mixer_decl, "mixer_fn": mixer_phase, "ssd_decl": ssd_decl, "ssd_fn": ssd_part}


def kernel(**inputs):
    nc = build_nc(DEFAULT_CFG)
    cores = list(range(8))
    maps = make_in_maps(inputs, cores)
    res = run_bass_kernel_spmd(nc, maps, core_ids=cores)
    return np.stack([np.asarray(r["out"], np.float32) for r in res.results], axis=0)
```

```python
import numpy as np
import concourse.bass as bass
import concourse.mybir as mybir
from concourse.bass_utils import run_bass_kernel_spmd

F32 = mybir.dt.float32
BF16 = mybir.dt.bfloat16
AF = mybir.ActivationFunctionType
ALU = mybir.AluOpType
AX = mybir.AxisListType

D = 1024
SEQ = 2048
CTX = 256
DEPTH = 2
DFF = 4096
KT = D // 128
NIN = 2576
NSCAN = 1552
EPS = 1e-6
GRAN = 64


class Acc:
    __slots__ = ("ap", "space", "ranges")

    def __init__(self, ap, space, ranges):
        self.ap = ap
        self.space = space
        self.ranges = ranges

    def w(self, ap):
        return Acc(ap, self.space, self.ranges)


class View:
    def __init__(self, base_ap, space, byte_off, shape, dtype, base_off=0):
        self.space = space
        self.off = byte_off
        self.shape = tuple(shape)
        self.dtype = dtype
        self.sz = 2 if dtype == BF16 else 4
        n = int(np.prod(shape))
        e0 = (byte_off - base_off) // self.sz
        ap = base_ap[:, e0:e0 + n]
        if len(shape) == 2:
            ap = ap.rearrange("p (a b) -> p a b", b=shape[1])
        elif len(shape) == 3:
            ap = ap.rearrange("p (a b c) -> p a b c", b=shape[1], c=shape[2])
        self.ap = ap
        self.nbytes = n * self.sz

    def __call__(self, *idx, p=None):
        idx = list(idx) + [slice(None)] * (len(self.shape) - len(idx))
        key = [slice(None) if p is None else p]
        los, his = [], []
        for i, d in zip(idx, self.shape):
            if isinstance(i, int):
                los.append(i)
                his.append(i + 1)
                key.append(i)
            else:
                lo = 0 if i.start is None else i.start
                hi = d if i.stop is None else i.stop
                assert 0 <= lo < hi <= d, (i, d, self.shape)
                los.append(lo)
                his.append(hi)
                key.append(slice(lo, hi))
        ap = self.ap[tuple(key)]
        strides = []
        s = self.sz
        for d in reversed(self.shape):
            strides.append(s)
            s *= d
        strides = strides[::-1]
        nd = len(self.shape)
        outer = 1
        for k in range(nd - 1):
            outer *= his[k] - los[k]
        if outer > 48:
            lo_b = self.off + sum(l * st for l, st in zip(los, strides))
            hi_b = self.off + sum((h - 1) * st for h, st in zip(his, strides)) + self.sz
            ranges = [(lo_b, hi_b)]
        else:
            ranges = []
            import itertools
            for combo in itertools.product(*[range(los[k], his[k]) for k in range(nd - 1)]):
                b = self.off + sum(c * st for c, st in zip(combo, strides[:-1]))
                ranges.append((b + los[-1] * strides[-1], b + his[-1] * strides[-1]))
        return Acc(ap, self.space, ranges)


class Eng:
    def __init__(self, nc, eng, name, is_pe=False):
        self.eng = eng
        self.name = name
        self.is_pe = is_pe
        self.sem = nc.alloc_semaphore("s_" + name)
        self.count = 0
        self.seen = {}


class DSem:
    def __init__(self, nc, name):
        self.sem = nc.alloc_semaphore(name)
        self.count = 0
        self.name = name


class Tracker:
    def __init__(self, nc):
        self.nc = nc
        self.lastw = {"sb": {}, "ps": {}, "dr": {}}
        self.readers = {"sb": {}, "ps": {}, "dr": {}}
        self.n_wait = 0
        self.n_inst = 0

    @staticmethod
    def _grans(ranges, space="sb"):
        gran = 2048 if space == "ps" else GRAN
        for lo, hi in ranges:
            for g in range(lo // gran, (hi - 1) // gran + 1):
                yield g

    def _collect(self, reads, writes, E=None):
        need = {}
        for a in reads:
            lw = self.lastw[a.space]
            for g in self._grans(a.ranges, a.space):
                w = lw.get(g)
                if w is not None and need.get(w[0], 0) < w[1]:
                    need[w[0]] = w[1]
            if a.space == "ps":
                rd = self.readers["ps"]
                for g in self._grans(a.ranges, a.space):
                    r = rd.get(g)
                    if r:
                        for k, v in r.items():
                            if k is not E and need.get(k, 0) < v:
                                need[k] = v
        for a in writes:
            lw = self.lastw[a.space]
            rd = self.readers[a.space]
            for g in self._grans(a.ranges, a.space):
                w = lw.get(g)
                if w is not None and need.get(w[0], 0) < w[1]:
                    need[w[0]] = w[1]
                r = rd.get(g)
                if r:
                    for k, v in r.items():
                        if need.get(k, 0) < v:
                            need[k] = v
        return need

    def _emit_waits(self, E, need):
        for semobj, val in need.items():
            if semobj is E and E.is_pe:
                continue
            if E.seen.get(semobj, 0) >= val:
                continue
            h = semobj.sem
            E.eng.wait_ge(h, val)
            E.seen[semobj] = val
            self.n_wait += 1

    def _record(self, tag, reads, writes):
        for a in writes:
            lw = self.lastw[a.space]
            rd = self.readers[a.space]
            for g in self._grans(a.ranges, a.space):
                lw[g] = tag
                if g in rd:
                    del rd[g]
        for a in reads:
            rd = self.readers[a.space]
            for g in self._grans(a.ranges, a.space):
                r = rd.get(g)
                if r is None:
                    rd[g] = {tag[0]: tag[1]}
                elif r.get(tag[0], 0) < tag[1]:
                    r[tag[0]] = tag[1]

    def op(self, E, fn, reads=(), writes=(), inc=True):
        need = self._collect(reads, writes, E)
        self._emit_waits(E, need)
        inst = fn(E.eng)
        self.n_inst += 1
        if inc:
            E.count += 1
            inst.then_inc(E.sem, 1)
            tag = (E, E.count)
        else:
            tag = (E, E.count + 1)
        self._record(tag, reads, writes)
        return inst

    def dma(self, Q, dsem, out_ap, in_ap, reads=(), writes=(), **kw):
        need = self._collect(reads, writes)
        self._emit_waits(Q, need)
        inst = Q.eng.dma_start(out=out_ap, in_=in_ap, **kw)
        dsem.count += 16
        inst.then_inc(dsem.sem, 16)
        self.n_inst += 1
        tag = (dsem, dsem.count)
        self._record(tag, reads, writes)
        return inst


class Bump:
    def __init__(self, t32, tbf, space, limit):
        self.t32 = t32
        self.tbf = tbf
        self.space = space
        self.ptr = 0
        self.limit = limit
        self.peak = 0

    def alloc(self, shape, dtype, align=64):
        sz = 2 if dtype == BF16 else 4
        n = int(np.prod(shape)) * sz
        self.ptr = (self.ptr + align - 1) // align * align
        v = View(self.t32 if dtype == F32 else self.tbf, self.space, self.ptr, shape, dtype)
        self.ptr += n
        self.peak = max(self.peak, self.ptr)
        assert self.ptr <= self.limit, ("SBUF arena overflow", self.ptr, self.limit)
        return v

    def mark(self):
        return self.ptr

    def reset(self, m):
        self.ptr = m


class Ctx:
    pass


def build_nc(cfg=None):
    cfg = dict(cfg or {})
    n_layers = cfg.get("n_layers", DEPTH)
    K = Ctx()
    K.cfg = cfg

    nc = bass.Bass("TRN2", target_bir_lowering=False)
    K.nc = nc

    def din(name, shape):
        return nc.dram_tensor(name, list(shape), F32, kind="ExternalInput").ap()

    K.x_d = din("x", [SEQ, D])
    K.ctx_d = din("ctx", [CTX, D])
    K.ccol_d = din("ccol", [128, 2 * KT])
    K.ada_w_d = din("ada_w", [DEPTH, D, 6 * D])
    K.ada_b_d = din("ada_b", [DEPTH, 6 * D])
    K.ngcol_d = din("ngcol", [128, DEPTH * 2 * KT])
    K.w_in_d = din("w_in", [DEPTH, D, NIN])
    K.w_out_d = din("w_out", [DEPTH, D, D])
    K.w1_d = din("mlp_w1", [DEPTH, D, DFF])
    K.w2_d = din("mlp_w2", [DEPTH, DFF, D])
    K.fg_d = din("final_g_bc", [128, D])
    K.ident_d = din("ident", [128, 128])
    if cfg.get("mixer_decl"):
        cfg["mixer_decl"](K, din)
    K.out_d = nc.dram_tensor("out", [SEQ, D], F32, kind="ExternalOutput").ap()
    K.dbg_n = [0]
    if cfg.get("dump"):
        K.dbg_d = nc.dram_tensor("dbg", [24, 128, 2560], F32, kind="ExternalOutput").ap()

    def dump(acc, n, name=""):
        if not cfg.get("dump") or K.dbg_n[0] >= 24:
            return
        j = K.dbg_n[0]
        K.dbg_n[0] += 1
        print("dump slot", j, name, n)
        if acc.ap.dtype != F32:
            return
        T.dma(K.SP, K.dsem("d_dbg"), K.dbg_d[j, :, 0:n], acc.ap, reads=[acc])

    K.dump = dump
    K.modrows_d = nc.dram_tensor("modrows", [2, 6 * D], F32, kind="Internal").ap()

    ARENA = 212800 // 64 * 64
    arena = nc.alloc_sbuf_tensor("arena", [128, ARENA // 4], F32)
    sb = Bump(arena, arena.bitcast(BF16), "sb", ARENA)
    K.sb = sb
    psum_t = [nc.alloc_psum_tensor("ps%d" % i, [128, 1024], F32) for i in range(4)]
    PSP = []
    for i, t in enumerate(psum_t):
        PSP.append((View(t, "ps", i * 4096, [1024], F32, i * 4096), View(t.bitcast(BF16), "ps", i * 4096, [2048], BF16, i * 4096)))
    PSB = []
    PSBb = []
    for i, t in enumerate(psum_t):
        for h in range(2):
            PSB.append(View(t, "ps", i * 4096 + h * 2048, [512], F32, i * 4096))
            PSBb.append(View(t.bitcast(BF16), "ps", i * 4096 + h * 2048, [1024], BF16, i * 4096))

    T = Tracker(nc)
    K.T = T
    PE = Eng(nc, nc.tensor, "pe", is_pe=True)
    ACT = Eng(nc, nc.scalar, "act")
    DVE = Eng(nc, nc.vector, "dve")
    POOL = Eng(nc, nc.gpsimd, "pool")
    SP = Eng(nc, nc.sync, "sp")
    K.PE, K.ACT, K.DVE, K.POOL, K.SP = PE, ACT, DVE, POOL, SP

    ps_rr = [0]
    ps_lim = [8]

    def ps_pair():
        b = (ps_rr[0] + 1) // 2 * 2
        if b + 2 > ps_lim[0]:
            b = 0
        ps_rr[0] = (b + 2) % ps_lim[0]
        return PSP[b // 2]

    def ps_bank(bf=False):
        b = ps_rr[0]
        ps_rr[0] = (b + 1) % ps_lim[0]
        return PSBb[b] if bf else PSB[b]

    K.ps_pair, K.ps_bank, K.ps_lim, K.PSB, K.PSP, K.PSBb, K.ps_rr = ps_pair, ps_bank, ps_lim, PSB, PSP, PSBb, ps_rr

    dsems = {}

    def dsem(name):
        if name not in dsems:
            dsems[name] = DSem(nc, name)
        return dsems[name]

    K.dsem = dsem

    xs = sb.alloc([16, D], F32)
    cx = sb.alloc([2, D], F32)
    ident = sb.alloc([128], BF16)
    gates = sb.alloc([2, D], F32)
    ccol = sb.alloc([2 * KT], F32)
    ngcol = sb.alloc([DEPTH * 2 * KT], F32)
    Sst = sb.alloc([KT, 64], BF16)
    modcol = sb.alloc([2, 48], F32)
    Gsh = sb.alloc([2, 4, KT], F32)
    stat = sb.alloc([64], F32)
    ones_f = sb.alloc([128], F32)
    K.xs, K.cx, K.ident, K.gates, K.stat, K.ones_f, K.Gsh, K.modcol = xs, cx, ident, gates, stat, ones_f, Gsh, modcol

    dsem_ld = dsem("d_ld")
    dsem_misc = dsem("d_misc")
    dsem_out = dsem("d_out")

    def seal(ds, accs):
        for a in accs:
            T._record((ds, ds.count), [], [a])

    K.seal = seal

    for t in range(16):
        a = xs(t)
        T.dma(SP, dsem_ld, a.ap, K.x_d[t * 128:(t + 1) * 128, :], writes=[a])
    for t in range(2):
        a = cx(t)
        T.dma(SP, dsem_ld, a.ap, K.ctx_d[t * 128:(t + 1) * 128, :], writes=[a])
    seal(dsem_ld, [xs(t) for t in range(16)] + [cx(t) for t in range(2)])

    a = ccol()
    T.dma(SP, dsem("d_cc"), a.ap, K.ccol_d, writes=[a])
    a = ngcol()
    T.dma(SP, dsem("d_cc"), a.ap, K.ngcol_d, writes=[a])
    seal(dsem("d_cc"), [ccol(), ngcol()])
    a = ident()
    T.dma(POOL, dsem("d_ident"), a.ap, K.ident_d, writes=[a])

    a = ones_f()
    T.op(DVE, lambda e: e.memset(a.ap, 1.0), writes=[a])
    a = Sst()
    T.op(DVE, lambda e: e.memset(a.ap, 0.0), writes=[a])
    silu_tmp = stat(slice(32, 48))
    T.op(ACT, lambda e: e.activation(out=silu_tmp.ap, in_=ccol().ap, func=AF.Silu), reads=[ccol()], writes=[silu_tmp])
    for j in range(2):
        dst = Sst(slice(0, KT), slice(32 * j, 32 * j + 1))
        src = stat(slice(32 + KT * j, 32 + KT * (j + 1)))
        T.op(DVE, lambda e: e.tensor_copy(out=dst.ap, in_=src.ap.rearrange("p (k o) -> p k o", o=1)),
             reads=[src], writes=[dst])

    def dr_acc(r, c0, c1):
        return Acc(K.modrows_d[r:r + 1, c0:c1], "dr", [((r * 6 * D + c0) * 4, (r * 6 * D + c1) * 4)])

    def load_gates(mi, l):
        for r in range(2):
            if r == 1 and l == DEPTH - 1:
                continue
            src = dr_acc(r, mi * D, (mi + 1) * D)
            dst = gates(r)
            T.dma(SP, dsem("d_gate%d" % r), dst.ap, src.ap.partition_broadcast(128), reads=[src], writes=[dst])

    K.load_gates = load_gates

    def mod_begin(l):
        st = {"l": l}
        st["wbuf"] = [sb.alloc([KT, 512], BF16) for _ in range(2)]
        st["adab"] = [sb.alloc([512], F32) for _ in range(2)]
        st["rowbs"] = [sb.alloc([2, 512], F32) for _ in range(2)]
        st["colps"] = PSP[3][0]
        st["next"] = 0
        ps_lim[0] = 6
        if ps_rr[0] >= 6:
            ps_rr[0] = 0
        return st

    def mod_block(st):
        j = st["next"]
        if j >= 12:
            return
        st["next"] = j + 1
        l = st["l"]
        colps = st["colps"]
        wb = st["wbuf"][j % 2]
        rowb = st["rowbs"][j % 2]
        ab = st["adab"][j % 2]
        a = wb()
        T.dma(POOL, dsem("d_ada%d" % (j % 2)), a.ap,
              K.ada_w_d[l, :, j * 512:(j + 1) * 512].rearrange("(kt p) n -> p kt n", p=128), writes=[a])
        for prt in (0, 32):
            a = ab(p=slice(prt, prt + 1))
            T.dma(SP, dsem("d_adab%d_%d" % (j % 2, prt)), a.ap, K.ada_b_d[l:l + 1, j * 512:(j + 1) * 512], writes=[a])
        ps = ps_bank()
        o = ps(slice(0, 512), p=slice(0, 33))
        for k in range(KT):
            lh = Sst(k, slice(0, 33))
            rh = wb(k)
            T.op(PE, lambda e: e.matmul(o.ap, lhsT=lh.ap, rhs=rh.ap, start=(k == 0), stop=(k == KT - 1)),
                 reads=[lh, rh], writes=[o], inc=(k == KT - 1))
        for r, prt in enumerate((0, 32)):
            dst = rowb(r, p=slice(prt, prt + 1))
            i0_ = ps(slice(0, 512), p=slice(prt, prt + 1))
            i1_ = ab(p=slice(prt, prt + 1))
            T.op(DVE, lambda e: e.tensor_tensor(out=dst.ap, in0=i0_.ap, in1=i1_.ap, op=ALU.add),
                 reads=[i0_, i1_], writes=[dst])
            dd = dr_acc(r, j * 512, (j + 1) * 512)
            T.dma(SP, dsem("d_mrow%d_%d" % (r, j % 2)), dd.ap, dst.ap, reads=[dst], writes=[dd])
            for q in range(4):
                ft = j * 4 + q
                oc = colps(slice(64 * r + ft, 64 * r + ft + 1))
                lh = rowb(r, slice(q * 128, (q + 1) * 128), p=slice(prt, prt + 1))
                rh = ones_f(slice(0, 1), p=slice(prt, prt + 1))
                T.op(PE, lambda e: e.matmul(oc.ap, lhsT=lh.ap, rhs=rh.ap, start=True, stop=True),
                     reads=[lh, rh], writes=[oc])

    def mod_end(st):
        while st["next"] < 12:
            mod_block(st)
        ps_lim[0] = 8
        l = st["l"]
        colps = st["colps"]
        for r in range(2):
            for n_i, (mi_sh, mi_sc) in enumerate(((0, 1), (3, 4))):
                g = ngcol(slice((l * 2 + n_i) * KT, (l * 2 + n_i + 1) * KT))
                sc = colps(slice(64 * r + mi_sc * KT, 64 * r + (mi_sc + 1) * KT))
                sh = colps(slice(64 * r + mi_sh * KT, 64 * r + (mi_sh + 1) * KT))
                Gd = Gsh(r, 2 * n_i)
                shd = Gsh(r, 2 * n_i + 1)
                T.op(DVE, lambda e: e.scalar_tensor_tensor(out=Gd.ap, in0=sc.ap, scalar=1.0, in1=g.ap,
                                                           op0=ALU.add, op1=ALU.mult),
                     reads=[sc, g], writes=[Gd])
                T.op(DVE, lambda e: e.tensor_copy(out=shd.ap, in_=sh.ap), reads=[sh], writes=[shd])

    def compute_mod(l):
        m = sb.mark()
        st = mod_begin(l)
        mod_end(st)
        sb.reset(m)

    K.mod_begin, K.mod_block, K.mod_end = mod_begin, mod_block, mod_end

    def norm_tile(src, hn, junk, stat_col):
        ss = stat(slice(stat_col, stat_col + 1))
        rs = stat(slice(stat_col + 1, stat_col + 2))
        T.op(ACT, lambda e: e.activation(out=junk.ap, in_=src.ap, func=AF.Square, accum_out=ss.ap),
             reads=[src], writes=[junk, ss])
        T.op(ACT, lambda e: e.activation(out=rs.ap, in_=ss.ap, func=AF.Sqrt, scale=1.0 / D, bias=EPS),
             reads=[ss], writes=[rs])
        T.op(DVE, lambda e: e.reciprocal(out=rs.ap, in_=rs.ap), reads=[rs], writes=[rs])
        T.op(DVE, lambda e: e.tensor_scalar(out=hn.ap, in0=src.ap, scalar1=rs.ap, scalar2=None, op0=ALU.mult),
             reads=[src, rs], writes=[hn])
        return rs

    def tile_to_hT(hn, hT, col0, r, n_i, modulate=True):
        banks = [ps_bank(bf=True), ps_bank(bf=True)]
        for k in range(KT):
            o = banks[k // 4](slice((k % 4) * 128, (k % 4 + 1) * 128))
            i = hn(slice(k * 128, (k + 1) * 128))
            T.op(PE, lambda e: e.transpose(out=o.ap, in_=i.ap, identity=ident().ap),
                 reads=[i, ident()], writes=[o], inc=(k % 4 == 3))
        for kk in range(4):
            for half, eng in ((0, ACT), (1, DVE)):
                k = half * 4 + kk
                o = hT(k, slice(col0, col0 + 128))
                i = banks[half](slice(kk * 128, (kk + 1) * 128))
                G = Gsh(r, 2 * n_i, slice(k, k + 1))
                sh = Gsh(r, 2 * n_i + 1, slice(k, k + 1))
                if not modulate:
                    if eng is ACT:
                        T.op(ACT, lambda e: e.copy(out=o.ap, in_=i.ap), reads=[i], writes=[o])
                    else:
                        T.op(DVE, lambda e: e.tensor_copy(out=o.ap, in_=i.ap), reads=[i], writes=[o])
                elif eng is ACT:
                    T.op(ACT, lambda e: e.activation(out=o.ap, in_=i.ap, func=AF.Identity, scale=G.ap, bias=sh.ap),
                         reads=[i, G, sh], writes=[o])
                else:
                    T.op(DVE, lambda e: e.tensor_scalar(out=o.ap, in0=i.ap, scalar1=G.ap, scalar2=sh.ap,
                                                        op0=ALU.mult, op1=ALU.add),
                         reads=[i, G, sh], writes=[o])

    def apply_mod(hT, n_i, with_ctx):
        for k in range(KT):
            for r, (c0, c1) in enumerate(((CTX, CTX + SEQ), (0, CTX))):
                if r == 1 and not with_ctx:
                    continue
                a_ = hT(k, slice(c0, c1))
                G = Gsh(r, 2 * n_i, slice(k, k + 1))
                sh = Gsh(r, 2 * n_i + 1, slice(k, k + 1))
                T.op(DVE, lambda e: e.tensor_scalar(out=a_.ap, in0=a_.ap, scalar1=G.ap, scalar2=sh.ap,
                                                    op0=ALU.mult, op1=ALU.add), reads=[a_, G, sh], writes=[a_])

    K.apply_mod = apply_mod

    def norm_all_to_hT(l, n_i, hT, with_ctx, hn, junk, modulate=True, hook=None):
        tiles = []
        if with_ctx:
            tiles += [(1, cx(t), t * 128) for t in range(2)]
        tiles += [(0, xs(t), 256 + t * 128) for t in range(16)]
        n = len(tiles)
        norm_tile(tiles[0][1], hn[0](), junk(), 0)
        for t in range(n):
            if t + 1 < n:
                norm_tile(tiles[t + 1][1], hn[(t + 1) % 2](), junk(), 2 * ((t + 1) % 8))
            r, src, col0 = tiles[t]
            tile_to_hT(hn[t % 2], hT, col0, r, n_i, modulate)
            if hook is not None:
                hook(t)

    K.norm_all_to_hT = norm_all_to_hT

    def mlp_phase(l):
        m = sb.mark()
        with_ctx = l < DEPTH - 1
        load_gates(5, l)
        hT = sb.alloc([KT, CTX + SEQ], BF16)
        hn = [sb.alloc([D], BF16) for _ in range(2)]
        junk = sb.alloc([D], BF16)
        w1b = [sb.alloc([KT, 512], BF16) for _ in range(2)]
        w2b = [sb.alloc([4, D], BF16) for _ in range(2)]

        def load_mlp_w(fg):
            a1 = w1b[fg % 2]()
            T.dma(POOL, dsem("d_w1_%d" % (fg % 2)), a1.ap,
                  K.w1_d[l, :, fg * 512:(fg + 1) * 512].rearrange("(kt p) n -> p kt n", p=128), writes=[a1])
            a2 = w2b[fg % 2]()
            T.dma(POOL, dsem("d_w2_%d" % (fg % 2)), a2.ap,
                  K.w2_d[l, fg * 512:(fg + 1) * 512, :].rearrange("(ft p) n -> p ft n", p=128), writes=[a2])

        load_mlp_w(0)
        norm_all_to_hT(l, 1, hT, with_ctx, hn, junk)
        h1T = [sb.alloc([4, 512], BF16) for _ in range(2)]
        tmp = [sb.alloc([D], F32) for _ in range(2)]
        blocks = []
        if with_ctx:
            blocks.append((1, 0, [cx(0), cx(1)]))
        for b in range(4):
            blocks.append((0, 256 + b * 512, [xs(t) for t in range(4 * b, 4 * b + 4)]))
        cnt = 0
        ev = 0
        for fg in range(8):
            wb1 = w1b[fg % 2]
            wb2 = w2b[fg % 2]
            if fg + 1 < 8:
                load_mlp_w(fg + 1)
            for r, c0, tiles in blocks:
                nt = len(tiles)
                ntok = nt * 128
                h1 = h1T[cnt % 2]
                cnt += 1
                for fi in range(4):
                    ps = ps_bank()
                    o = ps(slice(0, ntok))
                    for k in range(KT):
                        lh = wb1(k, slice(fi * 128, (fi + 1) * 128))
                        rh = hT(k, slice(c0, c0 + ntok))
                        T.op(PE, lambda e: e.matmul(o.ap, lhsT=lh.ap, rhs=rh.ap, start=(k == 0), stop=(k == KT - 1)),
                             reads=[lh, rh], writes=[o], inc=(k == KT - 1))
                    dst = h1(fi, slice(0, ntok))
                    T.op(ACT, lambda e: e.activation(out=dst.ap, in_=o.ap, func=AF.Relu), reads=[o], writes=[dst])
                    T.op(ACT, lambda e: e.activation(out=dst.ap, in_=dst.ap, func=AF.Square),
                         reads=[dst], writes=[dst])
                if cfg.get("mlp_upto", 3) < 3:
                    continue
                for ti in range(nt):
                    pp = ps_pair()[0]
                    for hh in range(2):
                        o = pp(slice(hh * 512, (hh + 1) * 512))
                        for fi in range(4):
                            lh = h1(fi, slice(ti * 128, (ti + 1) * 128))
                            rh = wb2(fi, slice(hh * 512, (hh + 1) * 512))
                            T.op(PE, lambda e: e.matmul(o.ap, lhsT=lh.ap, rhs=rh.ap, start=(fi == 0), stop=(fi == 3)),
                                 reads=[lh, rh], writes=[o], inc=(fi == 3))
                    gt = gates(r)
                    tp = tmp[ev % 2]()
                    ev += 1
                    o = pp()
                    dst = tiles[ti]
                    T.op(DVE, lambda e: e.tensor_tensor(out=tp.ap, in0=o.ap, in1=gt.ap, op=ALU.mult),
                         reads=[o, gt], writes=[tp])
                    T.op(POOL if ev % 2 == 0 else DVE,
                         lambda e: e.tensor_tensor(out=dst.ap, in0=dst.ap, in1=tp.ap, op=ALU.add),
                         reads=[dst, tp], writes=[dst])
        sb.reset(m)

    for l in range(n_layers):
        stages = cfg.get("stages", ("mod", "mixer", "mlp"))
        if "mod" in stages and not ("mixer" in stages and cfg.get("mixer_fn") and cfg.get("mod_in_mixer", True)):
            compute_mod(l)
        if "mixer" in stages and cfg.get("mixer_fn"):
            cfg["mixer_fn"](K, l)
        if "mlp" in stages:
            mlp_phase(l)

    m = sb.mark()
    fg = sb.alloc([D], F32)
    ob = [sb.alloc([D], F32) for _ in range(2)]
    junk = sb.alloc([D], BF16)
    a = fg()
    T.dma(SP, dsem("d_fg"), a.ap, K.fg_d, writes=[a])
    for t in range(16):
        src = xs(t)
        ss = stat(slice(2 * (t % 8), 2 * (t % 8) + 1))
        rs = stat(slice(2 * (t % 8) + 1, 2 * (t % 8) + 2))
        jk = junk()
        T.op(ACT, lambda e: e.activation(out=jk.ap, in_=src.ap, func=AF.Square, accum_out=ss.ap),
             reads=[src], writes=[jk, ss])
        T.op(ACT, lambda e: e.activation(out=rs.ap, in_=ss.ap, func=AF.Sqrt, scale=1.0 / D, bias=EPS),
             reads=[ss], writes=[rs])
        T.op(DVE, lambda e: e.reciprocal(out=rs.ap, in_=rs.ap), reads=[rs], writes=[rs])
        o = ob[t % 2]()
        T.op(DVE, lambda e: e.scalar_tensor_tensor(out=o.ap, in0=src.ap, scalar=rs.ap, in1=fg().ap,
                                                   op0=ALU.mult, op1=ALU.mult),
             reads=[src, rs, fg()], writes=[o])
        T.dma(SP, dsem("d_out%d" % (t % 2)), K.out_d[t * 128:(t + 1) * 128, :], o.ap, reads=[o])
    for t in range(2):
        SP.eng.wait_ge(dsem("d_out%d" % t).sem, dsem("d_out%d" % t).count)
    if cfg.get("dump") and "d_dbg" in dsems:
        SP.eng.wait_ge(dsems["d_dbg"].sem, dsems["d_dbg"].count)
    sb.reset(m)
    print("build: inst=%d waits=%d sbuf_peak=%d" % (T.n_inst, T.n_wait, sb.peak))
    return nc


WR = 2310
NLC = 11


def mixer_decl(K, din):
    K.lrucol_d = din("lrucol", [128, DEPTH * 4 * NLC])
    K.wab_d = din("wab", [DEPTH * 16, 128, 128])
    if K.cfg.get("ssd_decl"):
        K.cfg["ssd_decl"](K, din)


def mixer_phase(K, l):
    nc, sb, T = K.nc, K.sb, K.T
    PE, ACT, DVE, POOL, SP = K.PE, K.ACT, K.DVE, K.POOL, K.SP
    xs, cx, gates, ident = K.xs, K.cx, K.gates, K.ident
    ps_bank, ps_pair, dsem = K.ps_bank, K.ps_pair, K.dsem
    cfg = K.cfg
    with_ctx_out = l < DEPTH - 1

    m0 = sb.mark()
    NT = CTX + SEQ
    hT = sb.alloc([KT, NT], BF16)
    catT = sb.alloc([4, NT], BF16)
    m1 = sb.mark()
    hn = [sb.alloc([D], BF16) for _ in range(2)]
    junk = sb.alloc([D], BF16)
    if cfg.get("mod_in_mixer", True):
        mst = K.mod_begin(l)

        def hook(t):
            if t % 3 != 2:
                K.mod_block(mst)

        K.norm_all_to_hT(l, 0, hT, True, hn, junk, modulate=False, hook=hook)
        K.mod_end(mst)
        K.apply_mod(hT, 0, True)
    else:
        K.norm_all_to_hT(l, 0, hT, True, hn, junk)
    sb.reset(m1)
    K.load_gates(2, l)

    wt_bufs = [sb.alloc([KT, 128], BF16) for _ in range(2)]
    wt_cnt = [0]

    def load_wcols(col0, ncols=128):
        i = wt_cnt[0] % 2
        wt_cnt[0] += 1
        wb = wt_bufs[i]
        a = wb(slice(0, KT), slice(0, ncols))
        T.dma(POOL, dsem("d_win%d" % i), a.ap,
              K.w_in_d[l, :, col0:col0 + ncols].rearrange("(kt p) n -> p kt n", p=128), writes=[a])
        return wb

    blocks = [(0, CTX, 2)] + [(CTX + 512 * b, 512, 260 + 512 * b) for b in range(4)]

    wo_pre = {}

    def outproj_half(which):
        m = sb.mark()
        pre = wo_pre.pop(which, None)
        wos = [sb.alloc([4, D], BF16) for _ in range(2 if with_ctx_out else 1)]
        if pre is not None:
            wo = pre
        else:
            wo = sb.alloc([4, D], BF16)
            a = wo()
            T.dma(POOL, dsem("d_wo"), a.ap,
                  K.w_out_d[l, which * 512:(which + 1) * 512, :].rearrange("(ft p) n -> p ft n", p=128), writes=[a])
        for r in range(len(wos)):
            for fi in range(4):
                src = wo(fi)
                dst = wos[r](fi)
                gt = gates(r)
                T.op(DVE if (fi + r) % 2 == 0 else POOL,
                     lambda e: e.tensor_tensor(out=dst.ap, in0=src.ap, in1=gt.ap, op=ALU.mult),
                     reads=[src, gt], writes=[dst])
        tl = []
        if with_ctx_out:
            tl += [(1, cx(t), t * 128) for t in range(2)]
        tl += [(0, xs(t), CTX + t * 128) for t in range(16)]
        for ev, (r, dst, c0) in enumerate(tl):
            pp = ps_pair()[0]
            wr = wos[r]
            for hh in range(2):
                o = pp(slice(hh * 512, (hh + 1) * 512))
                for fi in range(4):
                    lh = catT(fi, slice(c0, c0 + 128))
                    rh = wr(fi, slice(hh * 512, (hh + 1) * 512))
                    T.op(PE, lambda e: e.matmul(o.ap, lhsT=lh.ap, rhs=rh.ap, start=(fi == 0), stop=(fi == 3)),
                         reads=[lh, rh], writes=[o], inc=(fi == 3))
            o = pp()
            T.op(DVE, lambda e: e.tensor_tensor(out=dst.ap, in0=dst.ap, in1=o.ap, op=ALU.add),
                 reads=[dst, o], writes=[dst])
        sb.reset(m)

    def lru_part():
        m = sb.mark()
        LX = sb.alloc([WR], F32)
        U = sb.alloc([WR], F32)
        Ub = sb.alloc([WR], BF16)
        H0 = sb.alloc([WR], F32)
        H1 = LX
        NTT = CTX + SEQ
        Af = sb.alloc([NTT], F32)
        Sf = sb.alloc([NTT], F32)
        Bf = sb.alloc([NTT], F32)
        Gt = [View(sb.t32, "sb", Af.off + 2048 * i_, [512], F32) for i_ in range(2)]
        Bb = [View(sb.t32, "sb", Af.off + 4096 + 2048 * i_, [512], F32) for i_ in range(2)]
        lcol = sb.alloc([4, NLC], F32)
        lhalf = sb.alloc([4, NLC], F32)
        lsp = sb.alloc([4, NLC], F32)
        lcs = sb.alloc([4, NLC], F32)
        lhcs = sb.alloc([4, NLC], F32)
        wab = sb.alloc([16, 128], BF16)
        a = lcol()
        T.dma(SP, dsem("d_lcol"), a.ap, K.lrucol_d[:, l * 4 * NLC:(l + 1) * 4 * NLC], writes=[a])
        a = wab()
        T.dma(POOL, dsem("d_wab"), a.ap, K.wab_d[l * 16:(l + 1) * 16, :, :].rearrange("m p n -> p m n"), writes=[a])
        T.op(DVE, lambda e: e.tensor_scalar(out=lhalf().ap, in0=lcol().ap, scalar1=0.5, scalar2=None, op0=ALU.mult),
             reads=[lcol()], writes=[lhalf()])
        T.op(ACT, lambda e: e.activation(out=lsp().ap, in_=lcol().ap, func=AF.Exp, scale=-1.0),
             reads=[lcol()], writes=[lsp()])
        T.op(ACT, lambda e: e.activation(out=lsp().ap, in_=lsp().ap, func=AF.Ln, bias=1.0),
             reads=[lsp()], writes=[lsp()])
        T.op(DVE, lambda e: e.tensor_scalar(out=lcs().ap, in0=lsp().ap, scalar1=-8.0, scalar2=None, op0=ALU.mult),
             reads=[lsp()], writes=[lcs()])
        T.op(DVE, lambda e: e.tensor_scalar(out=lhcs().ap, in0=lsp().ap, scalar1=-4.0, scalar2=None, op0=ALU.mult),
             reads=[lsp()], writes=[lhcs()])
        for (c0, c1) in ((0, 2), (258, 260), (2308, 2310)):
            a = LX(slice(c0, c1))
            T.op(DVE, lambda e: e.memset(a.ap, 0.0), writes=[a])

        bcnt = 0
        wb_next = load_wcols(0)
        for i in range(4):
            wb = wb_next
            for (hc0, n, dc) in blocks:
                ps = ps_bank()
                o = ps(slice(0, n))
                for k in range(KT):
                    lh = wb(k)
                    rh = hT(k, slice(hc0, hc0 + n))
                    T.op(PE, lambda e: e.matmul(o.ap, lhsT=lh.ap, rhs=rh.ap, start=(k == 0), stop=(k == KT - 1)),
                         reads=[lh, rh], writes=[o], inc=(k == KT - 1))
                dst = LX(slice(dc, dc + n))
                T.op(ACT, lambda e: e.copy(out=dst.ap, in_=o.ap), reads=[o], writes=[dst])
            wb_lg = load_wcols(NSCAN + 128 * i)
            if i + 1 < 4:
                wb_next = load_wcols(128 * (i + 1))
            n = 2306
            uo = U(slice(2, 2 + n))
            i0 = LX(slice(1, 1 + n))
            w0 = lcol(i, slice(0, 1))
            cb = lcol(i, slice(4, 5))
            T.op(DVE, lambda e: e.tensor_scalar(out=uo.ap, in0=i0.ap, scalar1=w0.ap, scalar2=cb.ap,
                                                op0=ALU.mult, op1=ALU.add),
                 reads=[i0, w0, cb], writes=[uo])
            for k in range(1, 4):
                ik = LX(slice(1 + k, 1 + k + n))
                wk = lcol(i, slice(k, k + 1))
                T.op(DVE, lambda e: e.scalar_tensor_tensor(out=uo.ap, in0=ik.ap, scalar=wk.ap, in1=uo.ap,
                                                           op0=ALU.mult, op1=ALU.add),
                     reads=[ik, wk, uo], writes=[uo])
            if i == 0:
                K.dump(LX(), WR, "LX")
                K.dump(U(slice(2, 2 + n)), n, "U")
            ubo = Ub(slice(2, 2 + n))
            T.op(DVE, lambda e: e.tensor_copy(out=ubo.ap, in_=uo.ap), reads=[uo], writes=[ubo])
            for d in range(2):
                Hd = H0 if d == 0 else H1
                hba = lhalf(i, slice(5 + 3 * d, 6 + 3 * d))
                hbx = lhalf(i, slice(6 + 3 * d, 7 + 3 * d))
                cs = lcs(i, slice(7 + 3 * d, 8 + 3 * d))
                hcs = lhcs(i, slice(7 + 3 * d, 8 + 3 * d))
                for (hc0, n, dc) in blocks:
                    A_ = Af(slice(hc0, hc0 + n))
                    S_ = Sf(slice(hc0, hc0 + n))
                    B_ = Bf(slice(hc0, hc0 + n))
                    ub = Ub(slice(dc, dc + n))
                    uf = U(slice(dc, dc + n))
                    psa = ps_bank()(slice(0, n))
                    wa = wab((d * 2 + 0) * 4 + i)
                    T.op(PE, lambda e: e.matmul(psa.ap, lhsT=wa.ap, rhs=ub.ap, start=True, stop=True),
                         reads=[wa, ub], writes=[psa])
                    psx = ps_bank()(slice(0, n))
                    wx = wab((d * 2 + 1) * 4 + i)
                    T.op(PE, lambda e: e.matmul(psx.ap, lhsT=wx.ap, rhs=ub.ap, start=True, stop=True),
                         reads=[wx, ub], writes=[psx])
                    T.op(ACT, lambda e: e.activation(out=A_.ap, in_=psa.ap, func=AF.Tanh, scale=0.5, bias=hba.ap),
                         reads=[psa, hba], writes=[A_])
                    T.op(ACT, lambda e: e.activation(out=B_.ap, in_=psx.ap, func=AF.Tanh, scale=0.5, bias=hbx.ap),
                         reads=[psx, hbx], writes=[B_])
                    T.op(ACT, lambda e: e.activation(out=A_.ap, in_=A_.ap, func=AF.Exp, scale=hcs.ap, bias=hcs.ap),
                         reads=[A_, hcs], writes=[A_])
                    T.op(DVE, lambda e: e.tensor_tensor(out=S_.ap, in0=A_.ap, in1=A_.ap, op=ALU.mult),
                         reads=[A_], writes=[S_])
                    T.op(DVE, lambda e: e.scalar_tensor_tensor(out=B_.ap, in0=B_.ap, scalar=1.0, in1=uf.ap,
                                                               op0=ALU.add, op1=ALU.mult),
                         reads=[B_, uf], writes=[B_])
                T.op(ACT, lambda e: e.activation(out=Sf().ap, in_=Sf().ap, func=AF.Sqrt, scale=-1.0, bias=1.0),
                     reads=[Sf()], writes=[Sf()])
                order = blocks if d == 0 else [blocks[0]] + blocks[:0:-1]
                prev = None
                for (hc0, n, dc) in order:
                    A_ = Af(slice(hc0, hc0 + n))
                    S_ = Sf(slice(hc0, hc0 + n))
                    B_ = Bf(slice(hc0, hc0 + n))
                    T.op(DVE, lambda e: e.scalar_tensor_tensor(out=B_.ap, in0=B_.ap, scalar=0.5, in1=S_.ap,
                                                               op0=ALU.mult, op1=ALU.mult),
                         reads=[B_, S_], writes=[B_])
                    ho = Hd(slice(dc, dc + n))
                    rd = [A_, B_] + ([prev] if prev is not None else [])
                    init = prev.ap if prev is not None else 0.0
                    if d == 0:
                        T.op(DVE, lambda e: e.tensor_tensor_scan(out=ho.ap, data0=A_.ap, data1=B_.ap, initial=init,
                                                                 op0=ALU.mult, op1=ALU.add),
                             reads=rd, writes=[ho])
                        prev = Hd(slice(dc + n - 1, dc + n))
                    else:
                        T.op(DVE, lambda e: e.tensor_tensor_scan(out=ho.ap[:, ::-1], data0=A_.ap[:, ::-1],
                                                                 data1=B_.ap[:, ::-1], initial=init,
                                                                 op0=ALU.mult, op1=ALU.add),
                             reads=rd, writes=[ho])
                        prev = Hd(slice(dc, dc + 1))
            if i == 0:
                K.dump(H0(slice(260, 2308)), 2048, "H0")
                K.dump(H1(slice(260, 2308)), 2048, "H1")
            wb = wb_lg
            gblocks = blocks if with_ctx_out else blocks[1:]
            for gi, (hc0, n, dc) in enumerate(gblocks):
                ps = ps_bank()
                o = ps(slice(0, n))
                for k in range(KT):
                    lh = wb(k)
                    rh = hT(k, slice(hc0, hc0 + n))
                    T.op(PE, lambda e: e.matmul(o.ap, lhsT=lh.ap, rhs=rh.ap, start=(k == 0), stop=(k == KT - 1)),
                         reads=[lh, rh], writes=[o], inc=(k == KT - 1))
                t1 = Gt[gi % 2](slice(0, n))
                t2 = Bb[gi % 2](slice(0, n))
                T.op(ACT, lambda e: e.activation(out=t1.ap, in_=o.ap, func=AF.Square, scale=0.21145921592600305),
                     reads=[o], writes=[t1])
                T.op(DVE, lambda e: e.scalar_tensor_tensor(out=t1.ap, in0=t1.ap, scalar=1.0, in1=o.ap,
                                                           op0=ALU.add, op1=ALU.mult), reads=[t1, o], writes=[t1])
                T.op(ACT, lambda e: e.activation(out=t1.ap, in_=t1.ap, func=AF.Tanh, scale=0.7978845608028654),
                     reads=[t1], writes=[t1])
                T.op(DVE, lambda e: e.scalar_tensor_tensor(out=t1.ap, in0=t1.ap, scalar=1.0, in1=o.ap,
                                                           op0=ALU.add, op1=ALU.mult), reads=[t1, o], writes=[t1])
                h0 = H0(slice(dc, dc + n))
                h1 = H1(slice(dc, dc + n))
                T.op(POOL, lambda e: e.tensor_tensor(out=t2.ap, in0=h0.ap, in1=h1.ap, op=ALU.add),
                     reads=[h0, h1], writes=[t2])
                if i == 0 and hc0 == CTX:
                    K.dump(t1, n, "gelu2")
                    K.dump(t2, n, "hsum")
                co = catT(i, slice(hc0, hc0 + n))
                T.op(DVE, lambda e: e.scalar_tensor_tensor(out=co.ap, in0=t2.ap, scalar=0.5, in1=t1.ap,
                                                           op0=ALU.mult, op1=ALU.mult), reads=[t2, t1], writes=[co])
        wp = View(sb.tbf, "sb", Sf.off, [4, D], BF16)
        a = wp()
        T.dma(POOL, dsem("d_wo"), a.ap, K.w_out_d[l, 0:512, :].rearrange("(ft p) n -> p ft n", p=128), writes=[a])
        wo_pre[0] = wp
        sb.reset(m)

    stages = cfg.get("mix_stages", ("lru", "ssd"))
    if "lru" in stages:
        lru_part()
        outproj_half(0)
    if "ssd" in stages and cfg.get("ssd_fn"):
        cfg["ssd_fn"](K, l, locals())
        K.load_gates(2, l)
        outproj_half(1)
    sb.reset(m0)


NBC = 16 + 16 + 8 + 512


def ssd_decl(K, din):
    K.ssdcol_d = din("ssdcol", [128, DEPTH * 8 * 5])
    K.ssdbc_d = din("ssdbc", [DEPTH, 128, NBC])
    K.masks_d = din("masks", [128, 5 * 128])


def ssd_part(K, l, env):
    nc, sb, T = K.nc, K.sb, K.T
    PE, ACT, DVE, POOL, SP = K.PE, K.ACT, K.DVE, K.POOL, K.SP
    ident = K.ident
    ps_bank, dsem = K.ps_bank, K.dsem
    hT, catT, load_wcols, blocks = env["hT"], env["catT"], env["load_wcols"], env["blocks"]
    with_ctx_out = l < DEPTH - 1
    stat = K.stat
    NCH = 18

    m = sb.mark()
    masks = sb.alloc([5, 128], F32)
    LE, GE, GT, LT, ONES = (masks(j) for j in range(5))
    bc = sb.alloc([NBC], F32)
    scol = sb.alloc([8, 5], F32)
    a = masks()
    T.dma(SP, dsem("d_masks"), a.ap, K.masks_d.rearrange("p (j n) -> p j n", n=128), writes=[a])
    a = bc()
    T.dma(SP, dsem("d_sbc"), a.ap, K.ssdbc_d[l], writes=[a])
    a = scol()
    T.dma(SP, dsem("d_scol"), a.ap, K.ssdcol_d[:, l * 40:(l + 1) * 40].rearrange("p (t c) -> p t c", c=5), writes=[a])
    dtbias = bc(slice(0, 16))
    alog = bc(slice(16, 32))
    dsk = bc(slice(32, 40))
    nexpA = sb.alloc([16], F32)
    T.op(ACT, lambda e: e.activation(out=nexpA().ap, in_=alog.ap, func=AF.Exp), reads=[alog], writes=[nexpA()])
    T.op(DVE, lambda e: e.tensor_scalar(out=nexpA().ap, in0=nexpA().ap, scalar1=-1.0, scalar2=None, op0=ALU.mult),
         reads=[nexpA()], writes=[nexpA()])

    mp = sb.mark()
    tmpPs = [sb.alloc([SEQ], BF16) for _ in range(2)]
    for k in range(KT):
        src = hT(k, slice(CTX, CTX + SEQ))
        tp_ = tmpPs[k % 2]()
        T.op(DVE, lambda e: e.tensor_copy(out=tp_.ap.rearrange("p (w r) -> p w r", r=32),
                                          in_=src.ap.rearrange("p (r w) -> p w r", w=64)),
             reads=[src], writes=[tp_])
        T.op(ACT, lambda e: e.copy(out=src.ap, in_=tp_.ap), reads=[tp_], writes=[src])
    sb.reset(mp)

    def chunk_cols(view, row, c):
        if c < 2:
            return view(row, slice(128 * c, 128 * c + 128))
        j = c - 2
        full = view(row, slice(CTX, CTX + SEQ))
        return full.w(full.ap.rearrange("p (r w) -> p w r", w=64)[:, 4 * j:4 * j + 4, :])

    DT = sb.alloc([NCH, 16], F32)
    AA = sb.alloc([NCH, 16], F32)
    LNDT = sb.alloc([NCH, 16], F32)
    CSX = sb.alloc([NCH, 16], F32)
    EX = sb.alloc([NCH, 16], F32)
    WX = sb.alloc([NCH, 16], F32)
    ETOT = sb.alloc([NCH, 16], F32)
    wb = load_wcols(1536, 16)
    dps = ps_bank()
    for c in range(NCH):
        o = dps(slice(16 * c, 16 * c + 16))
        for k in range(KT):
            lh = hT(k, slice(128 * c, 128 * c + 128))
            rh = wb(k, slice(0, 16))
            T.op(PE, lambda e: e.matmul(o.ap, lhsT=lh.ap, rhs=rh.ap, start=(k == 0), stop=(k == KT - 1)),
                 reads=[lh, rh], writes=[o], inc=(k == KT - 1))
    dall = dps(slice(0, 16 * NCH))
    bcb = dtbias.ap.rearrange("p (o n) -> p o n", o=1).to_broadcast([128, NCH, 16])
    nab = nexpA().ap.rearrange("p (o n) -> p o n", o=1).to_broadcast([128, NCH, 16])
    T.op(DVE, lambda e: e.tensor_tensor(out=DT().ap, in0=dall.ap.rearrange("p (c n) -> p c n", n=16), in1=bcb, op=ALU.add),
         reads=[dall, dtbias], writes=[DT()])
    T.op(ACT, lambda e: e.activation(out=DT().ap, in_=DT().ap, func=AF.Exp), reads=[DT()], writes=[DT()])
    T.op(ACT, lambda e: e.activation(out=DT().ap, in_=DT().ap, func=AF.Ln, bias=1.0), reads=[DT()], writes=[DT()])
    T.op(ACT, lambda e: e.activation(out=LNDT().ap, in_=DT().ap, func=AF.Ln), reads=[DT()], writes=[LNDT()])
    T.op(DVE, lambda e: e.tensor_tensor(out=AA().ap, in0=DT().ap, in1=nab, op=ALU.mult),
         reads=[DT(), nexpA()], writes=[AA()])
    aflat = AA().w(AA().ap.rearrange("p c n -> p (c n)"))
    cps = [ps_bank() for _ in range(3)]
    for pj, msk in enumerate((LE, GE, ONES)):
        o = cps[pj](slice(0, 16 * NCH))
        T.op(PE, lambda e: e.matmul(o.ap, lhsT=msk.ap, rhs=aflat.ap, start=True, stop=True),
             reads=[msk, aflat], writes=[o])
    for d in range(2):
        src = cps[d](slice(0, 16 * NCH))
        dst = CSX(slice(0, NCH), slice(8 * d, 8 * d + 8))
        T.op(DVE, lambda e: e.tensor_copy(out=dst.ap, in_=src.ap.rearrange("p (c n) -> p c n", n=16)[:, :, 8 * d:8 * d + 8]),
             reads=[src], writes=[dst])
    tot = cps[2](slice(0, 16 * NCH))
    tot3 = tot.ap.rearrange("p (c n) -> p c n", n=16)
    T.op(ACT, lambda e: e.activation(out=ETOT().ap, in_=tot3, func=AF.Exp), reads=[tot], writes=[ETOT()])
    T.op(ACT, lambda e: e.activation(out=EX().ap, in_=CSX().ap, func=AF.Exp), reads=[CSX()], writes=[EX()])
    T.op(DVE, lambda e: e.tensor_tensor(out=WX().ap, in0=tot3, in1=CSX().ap, op=ALU.subtract),
         reads=[tot, CSX()], writes=[WX()])
    T.op(ACT, lambda e: e.activation(out=WX().ap, in_=WX().ap, func=AF.Exp), reads=[WX()], writes=[WX()])
    T.op(DVE, lambda e: e.tensor_tensor(out=WX().ap, in0=WX().ap, in1=DT().ap, op=ALU.mult),
         reads=[WX(), DT()], writes=[WX()])

    def bc4(view, c, d, g):
        a_ = view(c, slice(8 * d + 4 * g, 8 * d + 4 * g + 4))
        return a_, a_.ap.rearrange("p (h o) -> p h o", o=1).to_broadcast([128, 4, 64])

    small = sb.mark()
    for g in range(2):
        sb.reset(small)
        FT = sb.alloc([4, WR], BF16)
        ovl = sb.mark()
        CXb = sb.alloc([WR], F32)
        CU = sb.alloc([WR], F32)
        for (c0, c1) in ((0, 2), (258, 260), (2308, 2310)):
            a = CXb(slice(c0, c1))
            T.op(DVE, lambda e: e.memset(a.ap, 0.0), writes=[a])
        tile_cols = [512 + 256 * g, 512 + 256 * g + 128, 1024 + 128 * g, 1280 + 128 * g]
        wb_n = load_wcols(tile_cols[0])
        for ti, col0 in enumerate(tile_cols):
            t8 = (col0 - 512) // 128
            wb = wb_n
            if ti + 1 < 4:
                wb_n = load_wcols(tile_cols[ti + 1])
            for bi, (hc0, n, dc) in enumerate(blocks):
                ps = ps_bank()
                o = ps(slice(0, n))
                for k in range(KT):
                    lh = wb(k)
                    rh = hT(k, slice(hc0, hc0 + n))
                    T.op(PE, lambda e: e.matmul(o.ap, lhsT=lh.ap, rhs=rh.ap, start=(k == 0), stop=(k == KT - 1)),
                         reads=[lh, rh], writes=[o], inc=(k == KT - 1))
                dst = CXb(slice(dc, dc + n))
                T.op(ACT, lambda e: e.copy(out=dst.ap, in_=o.ap), reads=[o], writes=[dst])
            n = 2306
            uo = CU(slice(2, 2 + n))
            i0 = CXb(slice(1, 1 + n))
            w0 = scol(t8, slice(0, 1))
            cb = scol(t8, slice(4, 5))
            T.op(DVE, lambda e: e.tensor_scalar(out=uo.ap, in0=i0.ap, scalar1=w0.ap, scalar2=cb.ap,
                                                op0=ALU.mult, op1=ALU.add), reads=[i0, w0, cb], writes=[uo])
            for k in range(1, 4):
                ik = CXb(slice(1 + k, 1 + k + n))
                wk = scol(t8, slice(k, k + 1))
                T.op(DVE, lambda e: e.scalar_tensor_tensor(out=uo.ap, in0=ik.ap, scalar=wk.ap, in1=uo.ap,
                                                           op0=ALU.mult, op1=ALU.add), reads=[ik, wk, uo], writes=[uo])
            fo = FT(ti, slice(2, 2 + n))
            T.op(ACT, lambda e: e.activation(out=fo.ap, in_=uo.ap, func=AF.Silu), reads=[uo], writes=[fo])
        sb.reset(ovl)
        Y = sb.alloc([NCH, 256], F32)
        S = sb.alloc([256], F32)
        Sbf = [sb.alloc([256], BF16) for _ in range(2)]
        XBc = [sb.alloc([384], BF16) for _ in range(2)]
        Xs = [sb.alloc([256], BF16) for _ in range(2)]
        Xd = [sb.alloc([256], BF16) for _ in range(2)]
        CBm = sb.alloc([128], F32)
        Yt = [sb.alloc([256], F32) for _ in range(2)]
        SZ = sb.alloc([256], F32)
        YZ = sb.alloc([256], F32)
        ON = sb.alloc([256], BF16)
        jk = sb.alloc([256], BF16)
        gbase = K.gates.off
        RA = [View(sb.t32, "sb", gbase + 2048 * i, [4, 128], F32) for i in range(2)]
        Lm = View(sb.t32, "sb", gbase + 4096, [4, 128], F32)
        MT = [View(sb.tbf, "sb", gbase + 6144 + 1024 * i, [4, 128], BF16) for i in range(2)]
        wtb = env["wt_bufs"]
        wz = View(sb.tbf, "sb", wtb[0].off, [KT, 256], BF16)
        assert wtb[1].off == wtb[0].off + 2048
        a = wz()
        T.dma(POOL, dsem("d_wz"), a.ap,
              K.w_in_d[l, :, 2064 + 256 * g:2064 + 256 * g + 256].rearrange("(kt p) n -> p kt n", p=128), writes=[a])
        normg = bc(slice(40 + 256 * g, 40 + 256 * g + 256))

        def pad_cols(c):
            return (2 + 128 * c) if c < 2 else (260 + 128 * (c - 2))

        DSKM = sb.alloc([8, 128], BF16)
        idf = Lm(0)
        dtmp = Lm(1)
        T.op(DVE, lambda e: e.tensor_tensor(out=idf.ap, in0=LE.ap, in1=GE.ap, op=ALU.mult), reads=[LE, GE], writes=[idf])
        for h in range(4):
            dcol = bc(slice(32 + 4 * g + h, 32 + 4 * g + h + 1))
            hi = DSKM(2 * h)
            lo = DSKM(2 * h + 1)
            T.op(ACT, lambda e: e.activation(out=hi.ap, in_=idf.ap, func=AF.Copy, scale=dcol.ap),
                 reads=[idf, dcol], writes=[hi])
            T.op(DVE, lambda e: e.scalar_tensor_tensor(out=dtmp.ap, in0=idf.ap, scalar=dcol.ap, in1=hi.ap,
                                                       op0=ALU.mult, op1=ALU.subtract),
                 reads=[idf, dcol, hi], writes=[dtmp])
            T.op(DVE, lambda e: e.tensor_copy(out=lo.ap, in_=dtmp.ap), reads=[dtmp], writes=[lo])

        def h4(acc_):
            return acc_.ap.rearrange("p (h q) -> p h q", q=64)

        for d in range(2):
            T.op(DVE, lambda e: e.memset(S().ap, 0.0), writes=[S()])
            T.op(DVE, lambda e: e.memset(Sbf[0]().ap, 0.0), writes=[Sbf[0]()])
            order = list(range(NCH)) if d == 0 else [1, 0] + list(range(NCH - 1, 1, -1))
            st = {}

            def stageA(i):
                c = order[i]
                want_y = with_ctx_out or c >= 2
                pc = pad_cols(c)
                xb = XBc[i % 2]
                tp = ps_bank(bf=True)
                for ti in range(3):
                    o = tp(slice(128 * ti, 128 * ti + 128))
                    i_ = FT(ti, slice(pc, pc + 128))
                    T.op(PE, lambda e: e.transpose(out=o.ap, in_=i_.ap, identity=ident().ap),
                         reads=[i_, ident()], writes=[o], inc=(ti == 2))
                tpa = tp(slice(0, 384))
                T.op(ACT, lambda e: e.copy(out=xb().ap, in_=tpa.ap), reads=[tpa], writes=[xb()])
                xtok = xb(slice(0, 256))
                btok = xb(slice(256, 384))
                r = {"c": c, "want_y": want_y, "pc": pc, "xtok": xtok}
                if want_y:
                    ct = FT(3, slice(pc, pc + 128))
                    bt = FT(2, slice(pc, pc + 128))
                    cbp = K.PSB[4 + 2 * (i % 2)](slice(0, 128))
                    T.op(PE, lambda e: e.matmul(cbp.ap, lhsT=bt.ap, rhs=ct.ap, start=True, stop=True),
                         reads=[bt, ct], writes=[cbp])
                    ra = RA[i % 2]()
                    dp = K.PSB[5 + 2 * (i % 2)](slice(0, 512))
                    smask = GT if d == 0 else LT
                    T.op(PE, lambda e: e.matmul(dp.ap, lhsT=smask.ap, rhs=ra.ap.rearrange("p h n -> p (h n)"),
                                                start=True, stop=True), reads=[smask, ra], writes=[dp])
                    xd = Xd[i % 2]()
                    da, db = bc4(DT, c, d, g)
                    T.op(POOL, lambda e: e.tensor_tensor(out=h4(xd), in0=h4(xtok), in1=db, op=ALU.mult),
                         reads=[xtok, da], writes=[xd])
                    r.update(ct=ct, cbp=cbp, dp=dp, xd=xd)
                xs_ = Xs[i % 2]()
                wa, wbq = bc4(WX, c, d, g)
                T.op(POOL, lambda e: e.tensor_tensor(out=h4(xs_), in0=h4(xtok), in1=wbq, op=ALU.mult),
                     reads=[xtok, wa], writes=[xs_])
                stp = K.PSB[4 + 2 * (i % 2)](slice(128, 384))
                T.op(PE, lambda e: e.matmul(stp.ap, lhsT=btok.ap, rhs=xs_.ap, start=True, stop=True),
                     reads=[btok, xs_], writes=[stp])
                r["stp"] = stp
                st[i] = r

            def emit_RA(i):
                c = order[i]
                if not (with_ctx_out or c >= 2):
                    return
                rmask = LE if d == 0 else GE
                for h in range(4):
                    ac1 = AA(c, slice(8 * d + 4 * g + h, 8 * d + 4 * g + h + 1))
                    rah = RA[i % 2](h)
                    T.op(ACT, lambda e: e.activation(out=rah.ap, in_=rmask.ap, func=AF.Copy, scale=ac1.ap),
                         reads=[rmask, ac1], writes=[rah])

            def stageB(i):
                r = st[i]
                if not r["want_y"]:
                    return
                dp, cbp, xd = r["dp"], r["cbp"], r["xd"]
                T.op(ACT, lambda e: e.activation(out=Lm().ap.rearrange("p h n -> p (h n)"), in_=dp.ap, func=AF.Exp),
                     reads=[dp], writes=[Lm()])
                cmask = LE if d == 0 else GE
                T.op(DVE, lambda e: e.tensor_tensor(out=CBm().ap, in0=cbp.ap, in1=cmask.ap, op=ALU.mult),
                     reads=[cbp, cmask], writes=[CBm()])
                mt = MT[i % 2]
                T.op(DVE, lambda e: e.tensor_tensor(
                    out=mt().ap, in0=Lm().ap,
                    in1=CBm().ap.rearrange("p (o n) -> p o n", o=1).to_broadcast([128, 4, 128]), op=ALU.mult),
                    reads=[Lm(), CBm()], writes=[mt()])
                yd = ps_bank()
                xtok = r["xtok"]
                for h in range(4):
                    o = yd(slice(64 * h, 64 * h + 64))
                    lh = mt(h)
                    rh = Acc(xd.ap[:, 64 * h:64 * h + 64], "sb", xd.ranges)
                    T.op(PE, lambda e: e.matmul(o.ap, lhsT=lh.ap, rhs=rh.ap, start=True, stop=(d == 1)),
                         reads=[lh, rh], writes=[o], inc=(h == 3 and d == 1))
                    if d == 0:
                        xr = Acc(xtok.ap[:, 64 * h:64 * h + 64], "sb", xtok.ranges)
                        for part in range(2):
                            dm = DSKM(2 * h + part)
                            T.op(PE, lambda e: e.matmul(o.ap, lhsT=dm.ap, rhs=xr.ap, start=False, stop=(part == 1)),
                                 reads=[dm, xr], writes=[o], inc=(h == 3 and part == 1))
                r["yd"] = yd(slice(0, 256))

            def stageC(i):
                r = st.pop(i)
                c = r["c"]
                sb_in = Sbf[i % 2]()
                if r["want_y"]:
                    yo = ps_bank()(slice(0, 256))
                    ct = r["ct"]
                    T.op(PE, lambda e: e.matmul(yo.ap, lhsT=ct.ap, rhs=sb_in.ap, start=True, stop=True),
                         reads=[ct, sb_in], writes=[yo])
                stp = r["stp"]
                ta, tb = bc4(ETOT, c, d, g)
                T.op(DVE, lambda e: e.tensor_tensor(out=h4(S()), in0=h4(S()), in1=tb, op=ALU.mult),
                     reads=[S(), ta], writes=[S()])
                T.op(DVE, lambda e: e.tensor_tensor(out=S().ap, in0=S().ap, in1=stp.ap, op=ALU.add),
                     reads=[S(), stp], writes=[S()])
                sb_out = Sbf[(i + 1) % 2]()
                T.op(DVE, lambda e: e.tensor_copy(out=sb_out.ap, in_=S().ap), reads=[S()], writes=[sb_out])
                if r["want_y"]:
                    yt = Yt[i % 2]()
                    ea, eb = bc4(EX, c, d, g)
                    T.op(DVE, lambda e: e.tensor_tensor(out=h4(yt), in0=h4(yo), in1=eb, op=ALU.mult),
                         reads=[yo, ea], writes=[yt])
                    yda = r["yd"]
                    yc = Y(c)
                    if d == 0:
                        T.op(DVE, lambda e: e.tensor_tensor(out=yc.ap, in0=yt.ap, in1=yda.ap, op=ALU.add),
                             reads=[yt, yda], writes=[yc])
                    else:
                        T.op(DVE, lambda e: e.tensor_tensor(out=yt.ap, in0=yt.ap, in1=yda.ap, op=ALU.add),
                             reads=[yt, yda], writes=[yt])
                        T.op(POOL, lambda e: e.tensor_tensor(out=yc.ap, in0=yc.ap, in1=yt.ap, op=ALU.add),
                             reads=[yc, yt], writes=[yc])

            n_it = len(order)
            K.ps_lim[0] = 4
            if K.ps_rr[0] >= 4:
                K.ps_rr[0] = 0
            emit_RA(0)
            stageA(0)
            if n_it > 1:
                emit_RA(1)
            for i in range(n_it):
                if i + 1 < n_it:
                    stageA(i + 1)
                stageB(i)
                stageC(i)
                if i + 2 < n_it:
                    emit_RA(i + 2)
            K.ps_lim[0] = 8

        fin = [c for c in range(NCH) if (with_ctx_out or c >= 2)]
        ssq = sb.alloc([NCH], F32)
        SZ2 = [SZ, YZ]
        zst = {}

        def f1_head(fi_):
            c = fin[fi_]
            zp = ps_bank()(slice(0, 256))
            for k in range(KT):
                lh = hT(k, slice(128 * c, 128 * c + 128))
                rh = wz(k)
                T.op(PE, lambda e: e.matmul(zp.ap, lhsT=lh.ap, rhs=rh.ap, start=(k == 0), stop=(k == KT - 1)),
                     reads=[lh, rh], writes=[zp], inc=(k == KT - 1))
            sz = SZ2[fi_ % 2]()
            T.op(ACT, lambda e: e.activation(out=sz.ap, in_=zp.ap, func=AF.Tanh, scale=0.5), reads=[zp], writes=[sz])
            zst[fi_] = (zp, sz)

        def f1_tail(fi_):
            c = fin[fi_]
            zp, sz = zst.pop(fi_)
            yc = Y(c)
            T.op(DVE, lambda e: e.scalar_tensor_tensor(out=sz.ap, in0=sz.ap, scalar=1.0, in1=zp.ap,
                                                       op0=ALU.add, op1=ALU.mult), reads=[sz, zp], writes=[sz])
            T.op(DVE, lambda e: e.scalar_tensor_tensor(out=yc.ap, in0=yc.ap, scalar=0.5, in1=sz.ap,
                                                       op0=ALU.mult, op1=ALU.mult), reads=[yc, sz], writes=[yc])
            ss = ssq(slice(c, c + 1))
            T.op(ACT, lambda e: e.activation(out=jk().ap, in_=yc.ap, func=AF.Square, accum_out=ss.ap),
                 reads=[yc], writes=[jk(), ss])

        f1_head(0)
        for fi_ in range(len(fin)):
            if fi_ + 1 < len(fin):
                f1_head(fi_ + 1)
            f1_tail(fi_)
        c_lo, c_hi = fin[0], fin[-1] + 1
        rsq = ssq(slice(c_lo, c_hi))
        T.op(ACT, lambda e: e.activation(out=rsq.ap, in_=rsq.ap, func=AF.Sqrt, scale=1.0 / 256, bias=EPS),
             reads=[rsq], writes=[rsq])
        T.op(DVE, lambda e: e.reciprocal(out=rsq.ap, in_=rsq.ap), reads=[rsq], writes=[rsq])
        ON2 = [ON, jk]

        def f3_head(fi_):
            c = fin[fi_]
            yc = Y(c)
            rs = ssq(slice(c, c + 1))
            on = ON2[fi_ % 2]
            T.op(DVE, lambda e: e.scalar_tensor_tensor(out=on().ap, in0=yc.ap, scalar=rs.ap, in1=normg.ap,
                                                       op0=ALU.mult, op1=ALU.mult),
                 reads=[yc, rs, normg], writes=[on()])

        f3_head(0)
        for fi_, c in enumerate(fin):
            on = ON2[fi_ % 2]
            if fi_ + 1 < len(fin):
                f3_head(fi_ + 1)
            tps = [ps_bank(bf=True), ps_bank(bf=True)]
            for ti in range(2):
                o = tps[ti](slice(0, 128))
                i_ = on(slice(128 * ti, 128 * ti + 128))
                T.op(PE, lambda e: e.transpose(out=o.ap, in_=i_.ap, identity=ident().ap),
                     reads=[i_, ident()], writes=[o])
            for ti in range(2):
                src = tps[ti](slice(0, 128))
                dst = chunk_cols(catT, 2 * g + ti, c)
                sap = src.ap if c < 2 else src.ap.rearrange("p (w r) -> p w r", r=32)
                if ti == 0:
                    T.op(ACT, lambda e: e.copy(out=dst.ap, in_=sap), reads=[src], writes=[dst])
                else:
                    T.op(DVE, lambda e: e.tensor_copy(out=dst.ap, in_=sap), reads=[src], writes=[dst])
    sb.reset(m)


def make_in_maps(inputs, cores):
    f = np.float32
    ident = np.eye(128, dtype=f)
    ngcol = np.zeros((128, DEPTH * 2 * KT), f)
    for l in range(DEPTH):
        ngcol[:, (l * 2) * KT:(l * 2 + 1) * KT] = np.asarray(inputs["norm1_g"][l], f).reshape(KT, 128).T
        ngcol[:, (l * 2 + 1) * KT:(l * 2 + 2) * KT] = np.asarray(inputs["norm2_g"][l], f).reshape(KT, 128).T
    fg_bc = np.ascontiguousarray(np.broadcast_to(np.asarray(inputs["final_g"], f)[None, :], (128, D)))
    shared = {
        "ada_w": np.ascontiguousarray(inputs["ada_w"], f), "ada_b": np.ascontiguousarray(inputs["ada_b"], f),
        "ngcol": ngcol, "w_in": np.ascontiguousarray(inputs["w_in"], f),
        "w_out": np.ascontiguousarray(inputs["w_out"], f), "mlp_w1": np.ascontiguousarray(inputs["mlp_w1"], f),
        "mlp_w2": np.ascontiguousarray(inputs["mlp_w2"], f), "final_g_bc": fg_bc, "ident": ident,
    }
    lrucol = np.zeros((128, DEPTH * 4 * NLC), f)
    wab = np.zeros((DEPTH * 16, 128, 128), f)
    for l in range(DEPTH):
        for i in range(4):
            base = (l * 4 + i) * NLC
            ch = slice(i * 128, (i + 1) * 128)
            for k in range(4):
                lrucol[:, base + k] = inputs["lru_conv_w"][l, k, ch]
            lrucol[:, base + 4] = inputs["lru_conv_b"][l, ch]
            for d in range(2):
                lrucol[:, base + 5 + 3 * d] = inputs["lru_ba"][l, d, ch]
                lrucol[:, base + 6 + 3 * d] = inputs["lru_bx"][l, d, ch]
                lrucol[:, base + 7 + 3 * d] = inputs["lru_lambda"][l, d, ch]
                for which, nm in enumerate(("lru_wa", "lru_wx")):
                    mtx = wab[l * 16 + (d * 2 + which) * 4 + i]
                    for hl in range(2):
                        mtx[hl * 64:(hl + 1) * 64, hl * 64:(hl + 1) * 64] = inputs[nm][l, d, 2 * i + hl]
    shared["lrucol"] = lrucol
    shared["wab"] = wab
    ssdcol = np.zeros((128, DEPTH * 8 * 5), f)
    ssdbc = np.zeros((DEPTH, 128, NBC), f)
    for l in range(DEPTH):
        for t8 in range(8):
            ch = slice(t8 * 128, (t8 + 1) * 128)
            for k in range(4):
                ssdcol[:, (l * 8 + t8) * 5 + k] = inputs["ssd_conv_w"][l, k, ch]
            ssdcol[:, (l * 8 + t8) * 5 + 4] = inputs["ssd_conv_b"][l, ch]
        ssdbc[l, :, 0:16] = np.asarray(inputs["ssd_dt_bias"][l], f).reshape(1, 16)
        ssdbc[l, :, 16:32] = np.asarray(inputs["ssd_a_log"][l], f).reshape(1, 16)
        ssdbc[l, :, 32:40] = np.asarray(inputs["ssd_d"][l], f).reshape(1, 8)
        ssdbc[l, :, 40:552] = np.asarray(inputs["ssd_norm_g"][l], f).reshape(1, 512)
    ia = np.arange(128)[:, None]
    ib = np.arange(128)[None, :]
    masks = np.concatenate([(ia <= ib), (ia >= ib), (ia > ib), (ia < ib), np.ones((128, 128), bool)], axis=1).astype(f)
    shared["ssdcol"] = ssdcol
    shared["ssdbc"] = ssdbc
    shared["masks"] = masks
    maps = []
    for b in cores:
        ccol = np.zeros((128, 2 * KT), f)
        ccol[:, 0:KT] = np.asarray(inputs["c"][b], f).reshape(KT, 128).T
        ccol[:, KT:2 * KT] = np.asarray(inputs["c_ctx"], f).reshape(KT, 128).T
        mp = dict(shared)
        mp["x"] = np.ascontiguousarray(inputs["x"][b], f)
        mp["ctx"] = np.ascontiguousarray(inputs["ctx"][b], f)
        mp["ccol"] = ccol
        maps.append(mp)
    return maps


DEFAULT_CFG = {"mixer_decl": mixer_decl, "mixer_fn": mixer_phase, "ssd_decl": ssd_decl, "ssd_fn": ssd_part}


def kernel(**inputs):
    nc = build_nc(DEFAULT_CFG)
    cores = list(range(8))
    maps = make_in_maps(inputs, cores)
    res = run_bass_kernel_spmd(nc, maps, core_ids=cores)
    return np.stack([np.asarray(r["out"], np.float32) for r in res.results], axis=0)
```
